# Optimizing a Trainium2 kernel written in Bass

```python
import math
import jax, jax.numpy as jnp
from jax import lax
import numpy as np

D_MODEL = 2048
BATCH = 1
SEQ = 16384
DEPTH = 4

HEAD_DIM = 128
N_HEADS_A = D_MODEL // HEAD_DIM
N_HEADS_B = D_MODEL // HEAD_DIM
N_KV_GROUPS = 4
HEADS_PER_GROUP = N_HEADS_B // N_KV_GROUPS
D_FF = 4 * D_MODEL
N_A_LAYERS = DEPTH // 2
N_B_LAYERS = DEPTH - N_A_LAYERS
Q_BLOCK = 128
CMP_BLOCK = 32
CMP_STRIDE = 16
CMP_HIDDEN = 256
SEL_BLOCK = 64
N_SELECTED = 16
SEL_RATIO = SEL_BLOCK // CMP_STRIDE
WINDOW = 512
N_BUCKETS = 32
REL_MAX_DIST = 2048
ALPHA = (2.0 * DEPTH) ** 0.25
BETA = (8.0 * DEPTH) ** -0.25
LN_EPS = 1e-5
FORCED_SCORE = 1e4
NEG_BIG = -1e30

kernel_name = "yoco_fox_nsa_deepnorm_trunk"


def layer_norm(x, g, b):
    xf = x.astype(jnp.float32)
    mu = jnp.mean(xf, axis=-1, keepdims=True)
    var = jnp.mean(jnp.square(xf - mu), axis=-1, keepdims=True)
    y = (xf - mu) * lax.rsqrt(var + LN_EPS)
    return (y * g + b).astype(x.dtype)


def masked_softmax(logits, mask):
    logits = jnp.where(mask, logits, NEG_BIG)
    m = jnp.max(logits, axis=-1, keepdims=True)
    e = jnp.where(mask, jnp.exp(logits - m), 0.0)
    return e / jnp.maximum(jnp.sum(e, axis=-1, keepdims=True), 1e-30)


def rel_bucket(dist):
    n = jnp.maximum(dist, 0)
    exact = N_BUCKETS // 2
    nf = jnp.maximum(n, 1).astype(jnp.float32)
    large = exact + (jnp.log(nf / exact) / math.log(REL_MAX_DIST / exact) * (N_BUCKETS - exact)).astype(jnp.int32)
    large = jnp.minimum(large, N_BUCKETS - 1)
    return jnp.where(n < exact, n, large)


def sq_relu_mlp(x, w1, w2):
    return jnp.square(jax.nn.relu(x @ w1)) @ w2


def fox_attention(x, w_in, b_f, w_o):
    B, S, _ = x.shape
    H, dh = N_HEADS_A, HEAD_DIM
    proj = x @ w_in
    q = proj[..., :H * dh].reshape(B, S, H, dh).transpose(0, 2, 1, 3) * (dh ** -0.5)
    k = proj[..., H * dh:2 * H * dh].reshape(B, S, H, dh).transpose(0, 2, 1, 3)
    v = proj[..., 2 * H * dh:3 * H * dh].reshape(B, S, H, dh).transpose(0, 2, 1, 3)
    log_f = jax.nn.log_sigmoid((proj[..., 3 * H * dh:] + b_f).astype(jnp.float32))
    cum = jnp.cumsum(log_f, axis=1).transpose(0, 2, 1)
    key_pos = jnp.arange(S)

    def block(i):
        t0 = i * Q_BLOCK
        t = t0 + jnp.arange(Q_BLOCK)
        qb = lax.dynamic_slice_in_dim(q, t0, Q_BLOCK, axis=2)
        cb = lax.dynamic_slice_in_dim(cum, t0, Q_BLOCK, axis=2)
        logits = jnp.einsum('bhqd,bhkd->bhqk', qb, k).astype(jnp.float32) + (cb[..., :, None] - cum[..., None, :])
        p = masked_softmax(logits, key_pos[None, :] <= t[:, None])
        o = jnp.einsum('bhqk,bhkd->bqhd', p.astype(v.dtype), v)
        return o.reshape(B, Q_BLOCK, H * dh)

    out = lax.map(block, jnp.arange(S // Q_BLOCK))
    out = out.transpose(1, 0, 2, 3).reshape(B, S, H * dh)
    return out @ w_o


def compress_blocks(kv_raw, pos, w1, w2):
    B, G, S, dh = kv_raw.shape
    chunks = kv_raw.reshape(B, G, S // CMP_STRIDE, CMP_STRIDE, dh)
    blocks = jnp.concatenate([chunks[:, :, :-1], chunks[:, :, 1:]], axis=3) + pos
    flat = blocks.reshape(B, G, blocks.shape[2], CMP_BLOCK * dh)
    return jax.nn.gelu(flat @ w1) @ w2


def shared_kv(h, kv_w, cmp_pos_k, cmp_pos_v, cmp_k_w1, cmp_k_w2, cmp_v_w1, cmp_v_w2):
    B, S, _ = h.shape
    kv = (h @ kv_w).reshape(B, S, 6, N_KV_GROUPS, HEAD_DIM).transpose(2, 0, 3, 1, 4)
    k_cmp = compress_blocks(kv[0], cmp_pos_k, cmp_k_w1, cmp_k_w2)
    v_cmp = compress_blocks(kv[1], cmp_pos_v, cmp_v_w1, cmp_v_w2)
    return k_cmp, v_cmp, kv[2], kv[3], kv[4], kv[5]


def nsa_attention(x, w_in, w_o, rel_bias, k_cmp, v_cmp, k_sel, v_sel, k_win, v_win):
    B, S, _ = x.shape
    H, G, R, dh = N_HEADS_B, N_KV_GROUPS, HEADS_PER_GROUP, HEAD_DIM
    proj = x @ w_in
    q = proj[..., :H * dh].reshape(B, S, G, R, dh).transpose(0, 2, 3, 1, 4) * (dh ** -0.5)
    gates = jax.nn.sigmoid(proj[..., H * dh:].astype(jnp.float32)).reshape(B, S, H, 3).astype(x.dtype)
    n_cmp = k_cmp.shape[2]
    n_sel_blocks = S // SEL_BLOCK
    n_top = min(N_SELECTED, n_sel_blocks)
    table = rel_bias.T.reshape(G, R, N_BUCKETS)
    cmp_end = jnp.arange(n_cmp) * CMP_STRIDE + CMP_BLOCK - 1
    k_win_p = jnp.pad(k_win, ((0, 0), (0, 0), (WINDOW, 0), (0, 0)))
    v_win_p = jnp.pad(v_win, ((0, 0), (0, 0), (WINDOW, 0), (0, 0)))
    b_idx = jnp.arange(B)[:, None, None, None]
    g_idx = jnp.arange(G)[:, None, None, None][None]
    g_idx = g_idx.reshape(1, G, 1, 1)
    gi = jnp.arange(G).reshape(1, G, 1, 1, 1)
    ri = jnp.arange(R).reshape(1, 1, R, 1, 1)
    j_blk = jnp.arange(n_sel_blocks)

    def block(i):
        t0 = i * Q_BLOCK
        t = t0 + jnp.arange(Q_BLOCK)
        qb = lax.dynamic_slice_in_dim(q, t0, Q_BLOCK, axis=3)
        d_c = t[:, None] - cmp_end[None, :]
        lg = jnp.einsum('bgrqd,bgnd->bgrqn', qb, k_cmp).astype(jnp.float32) + table[:, :, rel_bucket(d_c)]
        p_cmp = masked_softmax(lg, d_c >= 0)
        o_cmp = jnp.einsum('bgrqn,bgnd->bgrqd', p_cmp.astype(v_cmp.dtype), v_cmp)
        imp = jnp.pad(jnp.sum(p_cmp, axis=2), ((0, 0), (0, 0), (0, 0), (1, 1)))
        imp_sel = imp[..., :-1].reshape(B, G, Q_BLOCK, n_sel_blocks, SEL_RATIO).sum(-1) + imp[..., SEL_RATIO::SEL_RATIO]
        blk_t = t // SEL_BLOCK
        forced = (j_blk[None, :] == 0) | (j_blk[None, :] == blk_t[:, None]) | (j_blk[None, :] == blk_t[:, None] - 1)
        valid = j_blk[None, :] <= blk_t[:, None]
        score = jnp.where(forced, FORCED_SCORE, jnp.where(valid, imp_sel, -1.0))
        _, top = lax.top_k(score, n_top)
        tok = (top[..., None] * SEL_BLOCK + jnp.arange(SEL_BLOCK)).reshape(B, G, Q_BLOCK, n_top * SEL_BLOCK)
        ks = k_sel[b_idx, g_idx, tok]
        vs = v_sel[b_idx, g_idx, tok]
        d_s = t[:, None] - tok
        lg = jnp.einsum('bgrqd,bgqld->bgrql', qb, ks).astype(jnp.float32) + table[gi, ri, rel_bucket(d_s)[:, :, None]]
        p = masked_softmax(lg, (d_s >= 0)[:, :, None])
        o_sel = jnp.einsum('bgrql,bgqld->bgrqd', p.astype(vs.dtype), vs)
        kw = lax.dynamic_slice_in_dim(k_win_p, t0, WINDOW + Q_BLOCK, axis=2)
        vw = lax.dynamic_slice_in_dim(v_win_p, t0, WINDOW + Q_BLOCK, axis=2)
        s = t0 - WINDOW + jnp.arange(WINDOW + Q_BLOCK)
        d_w = t[:, None] - s[None, :]
        mask_w = (s[None, :] >= 0) & (d_w >= 0) & (d_w < WINDOW)
        lg = jnp.einsum('bgrqd,bgkd->bgrqk', qb, kw).astype(jnp.float32) + table[:, :, rel_bucket(d_w)]
        p = masked_softmax(lg, mask_w)
        o_win = jnp.einsum('bgrqk,bgkd->bgrqd', p.astype(vw.dtype), vw)
        to_bqhd = lambda o: o.transpose(0, 3, 1, 2, 4).reshape(B, Q_BLOCK, H, dh)
        g = lax.dynamic_slice_in_dim(gates, t0, Q_BLOCK, axis=1)
        out = g[..., 0:1] * to_bqhd(o_cmp) + g[..., 1:2] * to_bqhd(o_sel) + g[..., 2:3] * to_bqhd(o_win)
        return out.reshape(B, Q_BLOCK, H * dh)

    out = lax.map(block, jnp.arange(S // Q_BLOCK))
    out = out.transpose(1, 0, 2, 3).reshape(B, S, H * dh)
    return out @ w_o


def setup_inputs(seed: int = 0) -> dict:
    key = jax.random.key(seed)
    ks = jax.random.split(key, 24)
    D, dh, G = D_MODEL, HEAD_DIM, N_KV_GROUPS
    nrm = lambda k, shape, scale: jax.random.normal(k, shape, jnp.float32) * scale
    x = nrm(ks[0], (BATCH, SEQ, D), 1.0)
    fox_w_in = jnp.concatenate([
        nrm(ks[1], (N_A_LAYERS, D, 2 * N_HEADS_A * dh), D ** -0.5),
        nrm(ks[2], (N_A_LAYERS, D, N_HEADS_A * dh), BETA * D ** -0.5),
        nrm(ks[3], (N_A_LAYERS, D, N_HEADS_A), D ** -0.5),
    ], axis=-1)
    fox_b_f = 2.0 + nrm(ks[4], (N_A_LAYERS, N_HEADS_A), 1.0)
    fox_w_o = nrm(ks[5], (N_A_LAYERS, D, D), BETA * D ** -0.5)
    nsa_w_in = nrm(ks[6], (N_B_LAYERS, D, N_HEADS_B * dh + 3 * N_HEADS_B), D ** -0.5)
    nsa_w_o = nrm(ks[7], (N_B_LAYERS, D, D), BETA * D ** -0.5)
    slot_scale = jnp.array([1.0, BETA, 1.0, BETA, 1.0, BETA], jnp.float32)[None, :, None]
    kv_w = (nrm(ks[8], (D, 6, G * dh), D ** -0.5) * slot_scale).reshape(D, 6 * G * dh)
    cmp_pos_k = nrm(ks[9], (CMP_BLOCK, dh), 0.1)
    cmp_pos_v = nrm(ks[10], (CMP_BLOCK, dh), 0.1)
    cmp_k_w1 = nrm(ks[11], (CMP_BLOCK * dh, CMP_HIDDEN), (CMP_BLOCK * dh) ** -0.5)
    cmp_k_w2 = nrm(ks[12], (CMP_HIDDEN, dh), CMP_HIDDEN ** -0.5)
    cmp_v_w1 = nrm(ks[13], (CMP_BLOCK * dh, CMP_HIDDEN), (CMP_BLOCK * dh) ** -0.5)
    cmp_v_w2 = nrm(ks[14], (CMP_HIDDEN, dh), CMP_HIDDEN ** -0.5)
    rel_bias = nrm(ks[15], (N_BUCKETS, N_HEADS_B), 0.5)
    mlp_w1 = nrm(ks[16], (DEPTH, D, D_FF), D ** -0.5)
    mlp_w2 = nrm(ks[17], (DEPTH, D_FF, D), BETA * D_FF ** -0.5)
    ln1_g = 1.0 + nrm(ks[18], (DEPTH, D), 0.02)
    ln1_b = nrm(ks[19], (DEPTH, D), 0.02)
    ln2_g = 1.0 + nrm(ks[20], (DEPTH, D), 0.02)
    ln2_b = nrm(ks[21], (DEPTH, D), 0.02)
    return {"x": x, "fox_w_in": fox_w_in, "fox_b_f": fox_b_f, "fox_w_o": fox_w_o,
            "nsa_w_in": nsa_w_in, "nsa_w_o": nsa_w_o, "kv_w": kv_w,
            "cmp_pos_k": cmp_pos_k, "cmp_pos_v": cmp_pos_v,
            "cmp_k_w1": cmp_k_w1, "cmp_k_w2": cmp_k_w2, "cmp_v_w1": cmp_v_w1, "cmp_v_w2": cmp_v_w2,
            "rel_bias": rel_bias, "mlp_w1": mlp_w1, "mlp_w2": mlp_w2,
            "ln1_g": ln1_g, "ln1_b": ln1_b, "ln2_g": ln2_g, "ln2_b": ln2_b}


def reference(x, fox_w_in, fox_b_f, fox_w_o, nsa_w_in, nsa_w_o, kv_w,
              cmp_pos_k, cmp_pos_v, cmp_k_w1, cmp_k_w2, cmp_v_w1, cmp_v_w2,
              rel_bias, mlp_w1, mlp_w2, ln1_g, ln1_b, ln2_g, ln2_b):
    h = x
    kv = None
    for layer in range(DEPTH):
        if layer < N_A_LAYERS:
            mix = fox_attention(h, fox_w_in[layer], fox_b_f[layer], fox_w_o[layer])
        else:
            b = layer - N_A_LAYERS
            mix = nsa_attention(h, nsa_w_in[b], nsa_w_o[b], rel_bias, *kv)
        h = layer_norm(ALPHA * h + mix, ln1_g[layer], ln1_b[layer])
        h = layer_norm(ALPHA * h + sq_relu_mlp(h, mlp_w1[layer], mlp_w2[layer]), ln2_g[layer], ln2_b[layer])
        if layer == N_A_LAYERS - 1:
            kv = shared_kv(h, kv_w, cmp_pos_k, cmp_pos_v, cmp_k_w1, cmp_k_w2, cmp_v_w1, cmp_v_w2)
    return h
```

```python
import math
import numpy as np
import ml_dtypes
from contextlib import ExitStack
import concourse.bass as bass
import concourse.mybir as mybir
from concourse.bass_utils import run_bass_kernel_spmd

F32 = mybir.dt.float32
BF16 = mybir.dt.bfloat16
AF = mybir.ActivationFunctionType
ALU = mybir.AluOpType
NPBF = ml_dtypes.bfloat16


class Chan:
    def __init__(self, sem):
        self.sem = sem
        self.count = 0


class Sched:
    ENGS = ("pe", "act", "dve", "pool", "sp")

    def __init__(self, nc, es):
        self.nc, self.es = nc, es
        self.q = {e: [] for e in self.ENGS}
        self.nsem = 0
        self.main = {}

    def chan(self, name="c"):
        sem = self.es.enter_context(self.nc.semaphore(f"{name}{self.nsem}"))
        self.nsem += 1
        return Chan(sem)

    def op(self, eng, fn, deps=(), chan=None, inc=1):
        waits = [(d[0], d[1]) for d in deps if d is not None]
        mc = self.main.get(eng)
        if mc is not None and mc.count > 0:
            waits.append((mc, mc.count))
        t = None
        if chan is not None:
            chan.count += inc
            t = (chan, chan.count)
        self.q[eng].append((waits, fn, (chan, inc) if chan is not None else None))
        return t

    def emit(self):
        nc = self.nc
        q = self.q
        with nc.Block() as block:
            def replay(name):
                def f(e):
                    seen = {}
                    for waits, fn, inc in q[name]:
                        for ch, val in waits:
                            if seen.get(id(ch), 0) >= val:
                                continue
                            seen[id(ch)] = val
                            e.wait_ge(ch.sem, val)
                        ins = fn(e)
                        if inc is not None:
                            ins.then_inc(inc[0].sem, inc[1])
                return f
            block.tensor(replay("pe"))
            block.scalar(replay("act"))
            block.vector(replay("dve"))
            block.gpsimd(replay("pool"))
            block.sync(replay("sp"))


class Ring:
    def __init__(self, n):
        self.n = n
        self.i = 0
        self.free = [[] for _ in range(n)]

    def next(self):
        s = self.i % self.n
        self.i += 1
        deps = self.free[s]
        self.free[s] = []
        return s, deps

    def release(self, s, ticket):
        self.free[s].append(ticket)


ALPHA = 8.0 ** 0.25
LN_EPS = 1e-5
D = 2048
KC = 16
TP = 1024


def build_post(T, DFF, emit_hT=True):
    assert T % TP == 0 and DFF % 512 == 0
    NP, NTT, NG = T // TP, TP // 128, DFF // 512
    nc = bass.Bass("TRN2", target_bir_lowering=False)
    aT_d = nc.dram_tensor("aT", [D, T], BF16, kind="ExternalInput").ap()
    h_d = nc.dram_tensor("h", [T, D], F32, kind="ExternalInput").ap()
    wo_d = nc.dram_tensor("wo", [D, D], F32, kind="ExternalInput").ap()
    w1_d = nc.dram_tensor("w1", [D, DFF], F32, kind="ExternalInput").ap()
    w2_d = nc.dram_tensor("w2", [DFF, D], F32, kind="ExternalInput").ap()
    lnp_d = nc.dram_tensor("lnp", [4, D], F32, kind="ExternalInput").ap()
    id_d = nc.dram_tensor("ident", [128, 128], BF16, kind="ExternalInput").ap()
    hout_d = nc.dram_tensor("hout", [T, D], F32, kind="ExternalOutput").ap()
    if emit_hT:
        hTout_d = nc.dram_tensor("hTout", [D, T], BF16, kind="ExternalOutput").ap()

    with ExitStack() as es:
        sb = lambda name, shape, dt: es.enter_context(nc.sbuf_tensor(name, shape, dt))
        yacc = sb("yacc", [128, NTT, D], F32)
        h1T = sb("h1T", [128, KC, TP], BF16)
        wA = sb("wA", [128, 2, KC, 512], BF16)
        wBf = sb("wBf", [128, 2 * 4 * D], BF16)
        uT = sb("uT", [128, 2, 4, TP], BF16)
        lnp = sb("lnpsb", [128, 2, D], F32)
        tmp = sb("tmp", [128, 2, 512], F32)
        xb = sb("xb", [128, D], BF16)
        st = sb("st", [128, 4, 6], F32)
        mv = sb("mv", [128, 2], F32)
        rs = sb("rs", [128, 1], F32)
        nmr = sb("nmr", [128, 1], F32)
        ident = sb("identsb", [128, 128], BF16)
        ps = es.enter_context(nc.psum_tensor("ps", [128, 6, 512], F32))
        psT = es.enter_context(nc.psum_tensor("psT", [128, 2, 1024], BF16))
        wB = wBf[:, :].rearrange("p (s c n) -> p s c n", s=2, c=4)
        aTv = wBf[:, :].rearrange("p (k t) -> p k t", k=KC)

        S = Sched(nc, es)
        c_pe, c_act, c_dve, c_pool = S.chan("pe"), S.chan("act"), S.chan("dve"), S.chan("pool")
        S.main = {"act": c_act, "dve": c_dve, "pool": c_pool}
        c_h, c_aT, c_ln, c_st, c_id = S.chan("ldh"), S.chan("ldaT"), S.chan("ldln"), S.chan("st"), S.chan("ldid")
        c_wA = [S.chan("wA0"), S.chan("wA1")]
        c_wB = [S.chan("wB0"), S.chan("wB1")]
        bank = Ring(6)
        tbank = Ring(2)
        wAr, wBr, uTr, tmpr = Ring(2), Ring(2), Ring(2), Ring(2)

        t_id = S.op("sp", lambda e: e.dma_start(out=ident[:, :], in_=id_d[:, :]), chan=c_id, inc=16)

        def layer_norm(tt, deps):
            t = None
            for c in range(4):
                t = S.op("dve", lambda e, c=c: e.bn_stats(out=st[:, c, :], in_=yacc[:, tt, c * 512:(c + 1) * 512]),
                         deps=deps if c == 0 else [t], chan=c_dve)
            t = S.op("dve", lambda e: e.bn_aggr(out=mv[:, :], in_=st[:, :, :]), deps=[t], chan=c_dve)
            t = S.op("act", lambda e: e.activation(out=rs[:, :], in_=mv[:, 1:2], func=AF.Sqrt, bias=LN_EPS, scale=1.0),
                     deps=[t], chan=c_act)
            t = S.op("dve", lambda e: e.reciprocal(out=rs[:, :], in_=rs[:, :]), deps=[t], chan=c_dve)
            t = S.op("dve", lambda e: e.scalar_tensor_tensor(out=nmr[:, :], in0=mv[:, 0:1], scalar=-1.0, in1=rs[:, :],
                                                             op0=ALU.mult, op1=ALU.mult), deps=[t], chan=c_dve)
            t = S.op("act", lambda e: e.activation(out=yacc[:, tt, :], in_=yacc[:, tt, :], func=AF.Identity,
                                                   bias=nmr[:, 0:1], scale=rs[:, 0:1]), deps=[t], chan=c_act)
            t = S.op("dve", lambda e: e.tensor_tensor(out=yacc[:, tt, :], in0=yacc[:, tt, :], in1=lnp[:, 0, :], op=ALU.mult),
                     deps=[t], chan=c_dve)
            t = S.op("pool", lambda e: e.tensor_tensor(out=yacc[:, tt, :], in0=yacc[:, tt, :], in1=lnp[:, 1, :], op=ALU.add),
                     deps=[t], chan=c_pool)
            return t

        def to_T(tt, dep_x, extra_deps):
            t_xb = S.op("act", lambda e: e.activation(out=xb[:, :], in_=yacc[:, tt, :], func=AF.Copy),
                        deps=[dep_x] + list(xb_free), chan=c_act)
            last = None
            tps = []
            for q in range(4):
                s, fdeps = tbank.next()
                tp = None
                for j in range(4):
                    kc = q * 4 + j
                    tp = S.op("pe", lambda e, kc=kc, s=s, j=j: e.transpose(
                        out=psT[:, s, j * 128:(j + 1) * 128], in_=xb[:, kc * 128:(kc + 1) * 128], identity=ident[:, :]),
                        deps=([t_xb, t_id] + fdeps + list(extra_deps)) if j == 0 else [], chan=c_pe if j == 3 else None)
                tps.append(tp)
                last = S.op("dve", lambda e, q=q, s=s: e.tensor_copy(
                    out=h1T[:, q * 4:(q + 1) * 4, tt * 128:(tt + 1) * 128],
                    in_=psT[:, s, 0:512].rearrange("p (k t) -> p k t", k=4)), deps=[tp], chan=c_dve)
                tbank.release(s, last)
            xb_free[:] = [tps[-1]]
            return t_xb, last

        xb_free = []
        prev_pass_pe = None
        prev_pass_st = []
        prev_ln_done = None
        for p in range(NP):
            tok0 = p * TP
            t_h = None
            for tt in range(NTT):
                t_h = S.op("sp", lambda e, tt=tt, tok0=tok0: e.dma_start(out=yacc[:, tt, :], in_=h_d[tok0 + tt * 128: tok0 + (tt + 1) * 128, :]),
                           deps=prev_pass_st if tt == 0 else [], chan=c_h, inc=16)
            t_aT = None
            for q in range(4):
                t_aT = S.op("sp", lambda e, q=q, tok0=tok0: e.dma_start(
                    out=aTv[:, q * 4:(q + 1) * 4, :],
                    in_=aT_d[q * 512:(q + 1) * 512, tok0:tok0 + TP].rearrange("(k p) t -> p k t", p=128)),
                    deps=[prev_pass_pe] if q == 0 else [], chan=c_aT, inc=16)
            t_ln = None
            for i in range(2):
                t_ln = S.op("sp", lambda e, i=i: e.dma_start(out=lnp[:, i, :], in_=lnp_d[i:i + 1, :].broadcast_to([128, D])),
                            deps=[prev_ln_done] if i == 0 else [], chan=c_ln, inc=16)
            last_res = [None] * NTT
            for nb in range(4):
                s, fdeps = wAr.next()
                t_w = S.op("pool", lambda e, s=s, nb=nb: e.dma_start(
                    out=wA[:, s, :, :], in_=wo_d[:, nb * 512:(nb + 1) * 512].rearrange("(k p) n -> p k n", p=128)),
                    deps=fdeps, chan=c_wA[s], inc=16)
                for tt in range(NTT):
                    b, bdeps = bank.next()
                    tpe = None
                    for kc in range(KC):
                        tpe = S.op("pe", lambda e, kc=kc, tt=tt, s=s, b=b: e.matmul(
                            ps[:, b, :], lhsT=aTv[:, kc, tt * 128:(tt + 1) * 128], rhs=wA[:, s, kc, :],
                            start=(kc == 0), stop=(kc == KC - 1)),
                            deps=([t_w, t_aT] + bdeps) if kc == 0 else [], chan=c_pe if kc == KC - 1 else None)
                    te = S.op("dve", lambda e, tt=tt, nb=nb, b=b: e.scalar_tensor_tensor(
                        out=yacc[:, tt, nb * 512:(nb + 1) * 512], in0=yacc[:, tt, nb * 512:(nb + 1) * 512], scalar=ALPHA,
                        in1=ps[:, b, :], op0=ALU.mult, op1=ALU.add), deps=[tpe, t_h], chan=c_dve)
                    bank.release(b, te)
                    last_res[tt] = te
                wAr.release(s, tpe)
            last_wo_pe = tpe
            t_sc = None
            hT_ready = None
            for tt in range(NTT):
                t = layer_norm(tt, [last_res[tt], t_ln])
                t_xb, hT_ready = to_T(tt, t, prev_pass_st if tt == 0 else [])
                t_sc = S.op("act", lambda e, tt=tt: e.activation(out=yacc[:, tt, :], in_=yacc[:, tt, :], func=AF.Copy, scale=ALPHA),
                            deps=[t_xb], chan=c_act)
                ln1_last = t
            t_ln2 = None
            for i in range(2):
                t_ln2 = S.op("sp", lambda e, i=i: e.dma_start(out=lnp[:, i, :], in_=lnp_d[2 + i:3 + i, :].broadcast_to([128, D])),
                             deps=[ln1_last] if i == 0 else [], chan=c_ln, inc=16)
            last_acc = [[t_sc] * 4 for _ in range(NTT)]
            for g in range(NG):
                sa, fdeps = wAr.next()
                t_w1 = S.op("pool", lambda e, sa=sa, g=g: e.dma_start(
                    out=wA[:, sa, :, :], in_=w1_d[:, g * 512:(g + 1) * 512].rearrange("(k p) n -> p k n", p=128)),
                    deps=fdeps, chan=c_wA[sa], inc=16)
                sb_, fdeps = wBr.next()
                fdeps = fdeps + [last_wo_pe]
                t_w2 = None
                for c in range(4):
                    t_w2 = S.op("pool", lambda e, sb_=sb_, g=g, c=c: e.dma_start(
                        out=wB[:, sb_, c, :], in_=w2_d[g * 512 + c * 128: g * 512 + (c + 1) * 128, :], max_dma_last_dim=4096),
                        deps=fdeps if c == 0 else [], chan=c_wB[sb_], inc=16)
                su, udeps = uTr.next()
                t_u = []
                for c in range(4):
                    for blk in range(2):
                        b, bdeps = bank.next()
                        tpe = None
                        for kc in range(KC):
                            tpe = S.op("pe", lambda e, kc=kc, c=c, blk=blk, sa=sa, b=b: e.matmul(
                                ps[:, b, :], lhsT=wA[:, sa, kc, c * 128:(c + 1) * 128], rhs=h1T[:, kc, blk * 512:(blk + 1) * 512],
                                start=(kc == 0), stop=(kc == KC - 1)),
                                deps=([t_w1, hT_ready] + bdeps) if kc == 0 else [], chan=c_pe if kc == KC - 1 else None)
                        ts_, tdeps = tmpr.next()
                        ta = S.op("act", lambda e, ts_=ts_, b=b: e.activation(out=tmp[:, ts_, :], in_=ps[:, b, :], func=AF.Relu),
                                  deps=[tpe] + tdeps, chan=c_act)
                        bank.release(b, ta)
                        td = S.op("dve", lambda e, ts_=ts_, su=su, c=c, blk=blk: e.tensor_tensor(
                            out=uT[:, su, c, blk * 512:(blk + 1) * 512], in0=tmp[:, ts_, :], in1=tmp[:, ts_, :], op=ALU.mult),
                            deps=[ta] + (udeps if (c == 0 and blk == 0) else []), chan=c_dve)
                        tmpr.release(ts_, td)
                        t_u.append(td)
                wAr.release(sa, tpe)
                for tt in range(NTT):
                    for nb in range(4):
                        b, bdeps = bank.next()
                        tpe = None
                        for c in range(4):
                            tpe = S.op("pe", lambda e, c=c, tt=tt, nb=nb, su=su, sb_=sb_, b=b: e.matmul(
                                ps[:, b, :], lhsT=uT[:, su, c, tt * 128:(tt + 1) * 128], rhs=wB[:, sb_, c, nb * 512:(nb + 1) * 512],
                                start=(c == 0), stop=(c == 3)),
                                deps=([t_w2] + t_u + bdeps) if c == 0 else [], chan=c_pe if c == 3 else None)
                        te = S.op("dve", lambda e, tt=tt, nb=nb, b=b: e.tensor_tensor(
                            out=yacc[:, tt, nb * 512:(nb + 1) * 512], in0=ps[:, b, :], in1=yacc[:, tt, nb * 512:(nb + 1) * 512],
                            op=ALU.add), deps=[tpe, last_acc[tt][nb]], chan=c_dve)
                        bank.release(b, te)
                        last_acc[tt][nb] = te
                wBr.release(sb_, tpe)
                uTr.release(su, tpe)
                prev_pass_pe = tpe
            prev_pass_st = []
            hT2 = None
            for tt in range(NTT):
                t = layer_norm(tt, last_acc[tt] + [t_ln2])
                prev_ln_done = t
                t_st = S.op("sp", lambda e, tt=tt, tok0=tok0: e.dma_start(out=hout_d[tok0 + tt * 128: tok0 + (tt + 1) * 128, :], in_=yacc[:, tt, :]),
                            deps=[t], chan=c_st, inc=16)
                prev_pass_st = [t_st]
                if emit_hT:
                    t_xb, hT2 = to_T(tt, t, [prev_pass_pe] if tt == 0 else [])
            if emit_hT:
                for q in range(4):
                    t_st = S.op("sp", lambda e, q=q, tok0=tok0: e.dma_start(
                        out=hTout_d[q * 512:(q + 1) * 512, tok0:tok0 + TP].rearrange("(k p) t -> p k t", p=128),
                        in_=h1T[:, q * 4:(q + 1) * 4, :]), deps=[hT2], chan=c_st, inc=16)
                    prev_pass_st = [t_st]
        S.op("sp", lambda e: e.nop(), deps=prev_pass_st)
        S.emit()
    return nc


D = 2048
KC = 16
DH = 128
NEG = -30000.0


def fox_consts():
    ident = np.eye(128, dtype=np.float32)
    U = np.triu(np.ones((128, 128), np.float32))
    ones = np.ones((128, 128), np.float32)
    s = np.arange(128)[:, None]
    t = np.arange(128)[None, :]
    maskneg = np.where(s <= t, 0.0, NEG).astype(np.float32)
    return np.ascontiguousarray(np.stack([ident, U, ones, maskneg], axis=1))


def build_fox(S, NH=2):
    NT = S // 128
    NB = S // 512
    nc = bass.Bass("TRN2", target_bir_lowering=False)
    hT_d = nc.dram_tensor("hT", [D, S], BF16, kind="ExternalInput").ap()
    w_d = nc.dram_tensor("w", [NH, D, 385], F32, kind="ExternalInput").ap()
    bf_d = nc.dram_tensor("bf", [1, NH], F32, kind="ExternalInput").ap()
    cst_d = nc.dram_tensor("cst", [128, 4, 128], F32, kind="ExternalInput").ap()
    o_d = nc.dram_tensor("o", [S, NH * DH], BF16, kind="ExternalOutput").ap()

    with ExitStack() as es:
        sb = lambda name, shape, dt: es.enter_context(nc.sbuf_tensor(name, shape, dt))
        import os
        if os.environ.get("FOX_PAD"):
            pad_ = sb("padd", [128, int(os.environ["FOX_PAD"])], BF16)
        KT = sb("KT", [128, S], BF16)
        QT = sb("QT", [128, S], BF16)
        V1 = sb("V1", [128, NT, 129], BF16)
        w = sb("wsb", [128, KC, 385], BF16)
        hb = sb("hb", [128, 2, KC, 512], BF16)
        P = sb("P", [128, 3, 512], BF16)
        L = sb("L", [128, 2, 512], F32)
        ncrow = sb("ncrow", [128, 2, 512], F32)
        Dg = sb("Dg", [128, 2, 128], F32)
        cst = sb("cstsb", [128, 4, 128], F32)
        lf = sb("lf", [128, NT], F32)
        e1 = sb("e1", [128, NT], F32)
        ll = sb("ll", [128, NT], F32)
        Tsb = sb("Tsb", [128, NT], F32)
        incl = sb("incl", [128, NT], F32)
        Ex = sb("Ex", [128, NT], F32)
        Cc = sb("Cc", [128, NT], F32)
        onesrow = sb("onesrow", [128, NT], F32)
        biasb = sb("biasb", [128, 2, NT], F32)
        fcol = sb("fcol", [128, 2, 4], F32)
        bfb = sb("bfb", [128, NH], F32)
        negbf = sb("negbf", [128, NH], F32)
        odsb = sb("odsb", [128, 2, 129], F32)
        osum = sb("osum", [128, 2, 129], F32)
        rec = sb("rec", [128, 2, 1], F32)
        obuf = sb("obuf", [128, 2, 4, 128], BF16)
        pss = es.enter_context(nc.psum_tensor("pss", [128, 3, 512], F32))
        psm = es.enter_context(nc.psum_tensor("psm", [128, 512], F32))
        po = es.enter_context(nc.psum_tensor("po", [128, 4, 512], F32))
        ident, U, ones, maskneg = cst[:, 0, :], cst[:, 1, :], cst[:, 2, :], cst[:, 3, :]

        S_ = Sched(nc, es)
        c_pe, c_act, c_dve, c_pool = S_.chan("pe"), S_.chan("act"), S_.chan("dve"), S_.chan("pool")
        S_.main = {"act": c_act, "dve": c_dve, "pool": c_pool}
        c_cst, c_w = S_.chan("ldc"), S_.chan("ldw")
        c_st = [S_.chan("st0"), S_.chan("st1")]
        c_hb = [S_.chan("hb0"), S_.chan("hb1")]
        engs = {"pe": c_pe, "act": c_act, "dve": c_dve, "pool": c_pool}

        def barrier():
            deps = [(c, c.count) for c in (c_pe, c_act, c_dve, c_pool, c_st[0], c_st[1]) if c.count > 0]
            for eng in ("pe", "act", "dve", "pool", "sp"):
                S_.op(eng, lambda e: e.nop(), deps=deps)

        t_c = S_.op("sp", lambda e: e.dma_start(out=cst[:, :, :], in_=cst_d[:, :, :]), chan=c_cst, inc=16)
        t_c = S_.op("sp", lambda e: e.dma_start(out=bfb[:, :], in_=bf_d[0:1, :].broadcast_to([128, NH])), chan=c_cst, inc=16)
        S_.op("dve", lambda e: e.tensor_scalar(out=negbf[:, :], in0=bfb[:, :], scalar1=-1.0, scalar2=None, op0=ALU.mult),
              deps=[t_c], chan=c_dve)
        S_.op("pool", lambda e: e.memset(V1[:, :, 128:129], 1.0), chan=c_pool)
        S_.op("pool", lambda e: e.memset(onesrow[:, :], 1.0), chan=c_pool)

        sring = Ring(3)
        hbr = Ring(2)
        for hd in range(NH):
            if hd > 0:
                barrier()
            t_w = S_.op("pool", lambda e, hd=hd: e.dma_start(out=w[:, :, :], in_=w_d[hd].rearrange("(k p) n -> p k n", p=128)),
                        chan=c_w, inc=16)
            for blk in range(NB):
                hs, hdeps = hbr.next()
                t_h = None
                for q in range(4):
                    t_h = S_.op("sp", lambda e, hs=hs, blk=blk, q=q: e.dma_start(
                        out=hb[:, hs, q * 4:(q + 1) * 4, :],
                        in_=hT_d[q * 512:(q + 1) * 512, blk * 512:(blk + 1) * 512].rearrange("(k p) t -> p k t", p=128)),
                        deps=hdeps if q == 0 else [], chan=c_hb[hs], inc=16)
                s, bdeps = sring.next()
                for kc in range(KC):
                    tpe = S_.op("pe", lambda e, kc=kc, s=s, hs=hs: e.matmul(
                        pss[:, s, :], lhsT=w[:, kc, 0:128], rhs=hb[:, hs, kc, :], start=(kc == 0), stop=(kc == KC - 1)),
                        deps=([t_w, t_h] + bdeps) if kc == 0 else [], chan=c_pe if kc == KC - 1 else None)
                te = S_.op("act", lambda e, s=s, blk=blk: e.activation(out=QT[:, blk * 512:(blk + 1) * 512], in_=pss[:, s, :],
                                                                      func=AF.Copy, scale=DH ** -0.5), deps=[tpe], chan=c_act)
                sring.release(s, te)
                s, bdeps = sring.next()
                for kc in range(KC):
                    tpe = S_.op("pe", lambda e, kc=kc, s=s, hs=hs: e.matmul(
                        pss[:, s, :], lhsT=w[:, kc, 128:256], rhs=hb[:, hs, kc, :], start=(kc == 0), stop=(kc == KC - 1)),
                        deps=bdeps if kc == 0 else [], chan=c_pe if kc == KC - 1 else None)
                te = S_.op("dve", lambda e, s=s, blk=blk: e.tensor_copy(out=KT[:, blk * 512:(blk + 1) * 512], in_=pss[:, s, :]),
                           deps=[tpe], chan=c_dve)
                sring.release(s, te)
                for sub in range(4):
                    tile = blk * 4 + sub
                    s, bdeps = sring.next()
                    for kc in range(KC):
                        tpe = S_.op("pe", lambda e, kc=kc, s=s, hs=hs, sub=sub: e.matmul(
                            pss[:, s, 0:129], lhsT=hb[:, hs, kc, sub * 128:(sub + 1) * 128], rhs=w[:, kc, 256:385],
                            start=(kc == 0), stop=(kc == KC - 1)),
                            deps=bdeps if kc == 0 else [], chan=c_pe if kc == KC - 1 else None)
                    ta = S_.op("act", lambda e, s=s, tile=tile: e.activation(out=V1[:, tile, 0:128], in_=pss[:, s, 0:128], func=AF.Copy),
                               deps=[tpe], chan=c_act)
                    td = S_.op("dve", lambda e, s=s, tile=tile: e.tensor_copy(out=lf[:, tile:tile + 1], in_=pss[:, s, 128:129]),
                               deps=[tpe], chan=c_dve)
                    sring.release(s, ta)
                    sring.release(s, td)
                hbr.release(hs, tpe)
            t_projA, t_projD = (c_act, c_act.count), (c_dve, c_dve.count)
            t = S_.op("act", lambda e, hd=hd: e.activation(out=e1[:, :], in_=lf[:, :], func=AF.Exp, bias=negbf[:, hd:hd + 1], scale=-1.0),
                      deps=[t_projD], chan=c_act)
            t_l = S_.op("act", lambda e: e.activation(out=ll[:, :], in_=e1[:, :], func=AF.Ln, bias=1.0, scale=1.0), deps=[t], chan=c_act)
            s, bdeps = sring.next()
            t_W = S_.op("pe", lambda e: e.matmul(psm[:, 0:NT], lhsT=U, rhs=ll[:, :], start=True, stop=True), deps=[t_l, t_c], chan=c_pe)
            t_T = S_.op("pe", lambda e, s=s: e.matmul(pss[:, s, 0:NT], lhsT=ones, rhs=ll[:, :], start=True, stop=True), deps=bdeps, chan=c_pe)
            t = S_.op("dve", lambda e, s=s: e.tensor_copy(out=Tsb[:, :], in_=pss[:, s, 0:NT]), deps=[t_T], chan=c_dve)
            sring.release(s, t)
            t = S_.op("dve", lambda e: e.tensor_tensor_scan(out=incl[:, :], data0=onesrow[:, :], data1=Tsb[:, :], initial=0.0,
                                                            op0=ALU.mult, op1=ALU.add), deps=[t, (c_pool, c_pool.count)], chan=c_dve)
            t = S_.op("dve", lambda e: e.tensor_tensor(out=Ex[:, :], in0=incl[:, :], in1=Tsb[:, :], op=ALU.subtract), deps=[t], chan=c_dve)
            t_C = S_.op("dve", lambda e: e.tensor_tensor(out=Cc[:, :], in0=psm[:, 0:NT], in1=Ex[:, :], op=ALU.add), deps=[t, t_W], chan=c_dve)
            pring, lring, cring, bring, oring = Ring(3), Ring(2), Ring(2), Ring(2), Ring(2)
            po_free = []
            psm_free = [t_C]
            import os
            for b in range(int(os.environ.get('FOX_NB', NB))):
                nk = 4 * b + 4
                bs, bdeps_ = bring.next()
                t_bias = S_.op("dve", lambda e, bs=bs, b=b, nk=nk: e.tensor_scalar(
                    out=biasb[:, bs, 0:nk], in0=Cc[:, 0:nk], scalar1=Ex[:, 4 * b:4 * b + 1], scalar2=None, op0=ALU.subtract),
                    deps=[t_C] + bdeps_, chan=c_dve)
                t_f = S_.op("act", lambda e, bs=bs, b=b: e.activation(out=fcol[:, bs, :], in_=biasb[:, bs, 4 * b:4 * b + 4],
                                                                      func=AF.Exp, scale=-1.0), deps=[t_bias], chan=c_act)
                cs, cdeps = cring.next()
                t_cr = None
                for qq in range(4):
                    ds_ = qq % 2
                    t_dg = S_.op("dve", lambda e, ds_=ds_, bs=bs, b=b, qq=qq: e.tensor_scalar(
                        out=Dg[:, ds_, :], in0=ident, scalar1=biasb[:, bs, 4 * b + qq:4 * b + qq + 1], scalar2=-1.0,
                        op0=ALU.mult, op1=ALU.mult), deps=[t_bias, t_cr] if t_cr is not None else [t_bias, t_c], chan=c_dve)
                    t_cr = S_.op("pe", lambda e, ds_=ds_, qq=qq: e.matmul(psm[:, qq * 128:(qq + 1) * 128], lhsT=ones, rhs=Dg[:, ds_, :],
                                                                          start=True, stop=True),
                                 deps=[t_dg] + (psm_free if qq == 0 else []), chan=c_pe)
                t_ncr = S_.op("act", lambda e, cs=cs: e.activation(out=ncrow[:, cs, :], in_=psm[:, :], func=AF.Copy),
                              deps=[t_cr] + cdeps, chan=c_act)
                psm_free = [t_ncr]
                tiles = [("off", j) for j in range(4 * b)] + [("diag", kk) for kk in range(4)]
                pend = None
                first_pv = True
                last_pv = None
                diag_exp = []

                def emit_pv(info):
                    nonlocal first_pv, last_pv
                    kind, idx, pslot, t_exp = info
                    tpv = None
                    if kind == "off":
                        j = idx
                        for qq in range(4):
                            st_flag = (j == 0)
                            sp_flag = (j == 4 * b - 1)
                            tpv = S_.op("pe", lambda e, qq=qq, pslot=pslot, j=j, st_flag=st_flag, sp_flag=sp_flag: e.matmul(
                                po[:, qq, 0:129], lhsT=P[:, pslot, qq * 128:(qq + 1) * 128], rhs=V1[:, j, :],
                                start=st_flag, stop=sp_flag, skip_group_check=True),
                                deps=([t_exp] + (po_free if first_pv else [])) if qq == 0 else [], chan=c_pe if qq == 3 else None)
                            first_pv = False
                    else:
                        kk = idx
                        for qq in range(kk, 4):
                            st_flag = (b == 0 and kk == 0)
                            tpv = S_.op("pe", lambda e, qq=qq, pslot=pslot, kk=kk, st_flag=st_flag, b=b: e.matmul(
                                po[:, qq, 256:385], lhsT=P[:, pslot, (qq - kk) * 128:(qq - kk + 1) * 128], rhs=V1[:, 4 * b + kk, :],
                                start=st_flag, stop=(kk == qq), skip_group_check=True),
                                deps=([t_exp] + (po_free if first_pv else [])) if qq == kk else [], chan=c_pe if qq == 3 else None)
                            first_pv = False
                    pring.release(pslot, tpv)
                    last_pv = tpv

                for kind, idx in tiles:
                    s, sdeps = sring.next()
                    pslot, pdeps = pring.next()
                    if kind == "off":
                        j = idx
                        t_s = S_.op("pe", lambda e, s=s, j=j, b=b: e.matmul(
                            pss[:, s, :], lhsT=KT[:, j * 128:(j + 1) * 128], rhs=QT[:, b * 512:(b + 1) * 512], start=True, stop=True),
                            deps=sdeps + [t_projA, t_projD], chan=c_pe)
                        t_exp = S_.op("act", lambda e, s=s, pslot=pslot, bs=bs, j=j: e.activation(
                            out=P[:, pslot, :], in_=pss[:, s, :], func=AF.Exp, bias=biasb[:, bs, j:j + 1], scale=1.0),
                            deps=[t_s, t_bias] + pdeps, chan=c_act)
                        sring.release(s, t_exp)
                    else:
                        kk = idx
                        N = (4 - kk) * 128
                        t_s = S_.op("pe", lambda e, s=s, kk=kk, b=b, N=N: e.matmul(
                            pss[:, s, 0:N], lhsT=KT[:, (4 * b + kk) * 128:(4 * b + kk + 1) * 128],
                            rhs=QT[:, b * 512 + kk * 128:(b + 1) * 512], start=True, stop=True),
                            deps=sdeps + [t_projA, t_projD], chan=c_pe)
                        ls, ldeps = lring.next()
                        t_l1 = S_.op("dve", lambda e, s=s, ls=ls, cs=cs, kk=kk, N=N: e.tensor_tensor(
                            out=L[:, ls, 0:N], in0=pss[:, s, 0:N], in1=ncrow[:, cs, kk * 128:512], op=ALU.add),
                            deps=[t_s, t_ncr] + ldeps, chan=c_dve)
                        sring.release(s, t_l1)
                        import os
                        if os.environ.get("FOX_NOPOOL"):
                            t_l2 = S_.op("dve", lambda e, ls=ls: e.tensor_tensor(out=L[:, ls, 0:128], in0=L[:, ls, 0:128], in1=maskneg, op=ALU.add),
                                         deps=[t_l1, t_c], chan=c_dve)
                        else:
                            t_l2 = S_.op("pool", lambda e, ls=ls: e.tensor_tensor(out=L[:, ls, 0:128], in0=L[:, ls, 0:128], in1=maskneg, op=ALU.add),
                                         deps=[t_l1, t_c], chan=c_pool)
                        t_exp = S_.op("act", lambda e, ls=ls, pslot=pslot, bs=bs, kk=kk, b=b, N=N: e.activation(
                            out=P[:, pslot, 0:N], in_=L[:, ls, 0:N], func=AF.Exp, bias=biasb[:, bs, 4 * b + kk:4 * b + kk + 1], scale=1.0),
                            deps=[t_l2] + pdeps, chan=c_act)
                        lring.release(ls, t_exp)
                        diag_exp.append(t_exp)
                    if pend is not None:
                        emit_pv(pend)
                    pend = (kind, idx, pslot, t_exp)
                emit_pv(pend)
                cring.release(cs, diag_exp[-1])
                bring.release(bs, diag_exp[-1])
                os_, odeps = oring.next()
                t_o = None
                po_free = []
                for qq in range(4):
                    k2 = qq % 2
                    t1 = S_.op("act", lambda e, qq=qq, k2=k2: e.activation(out=odsb[:, k2, :], in_=po[:, qq, 256:385], func=AF.Copy),
                               deps=[last_pv] + ([t_o] if t_o is not None else []), chan=c_act)
                    if b > 0:
                        t2 = S_.op("dve", lambda e, qq=qq, k2=k2, bs=bs: e.scalar_tensor_tensor(
                            out=osum[:, k2, :], in0=po[:, qq, 0:129], scalar=fcol[:, bs, qq:qq + 1], in1=odsb[:, k2, :],
                            op0=ALU.mult, op1=ALU.add), deps=[t1, t_f], chan=c_dve)
                    else:
                        t2 = S_.op("dve", lambda e, k2=k2: e.tensor_copy(out=osum[:, k2, :], in_=odsb[:, k2, :]), deps=[t1], chan=c_dve)
                    t3 = S_.op("dve", lambda e, k2=k2: e.reciprocal(out=rec[:, k2, :], in_=osum[:, k2, 128:129]), deps=[t2], chan=c_dve)
                    t_o = S_.op("act", lambda e, qq=qq, k2=k2, os_=os_: e.activation(
                        out=obuf[:, os_, qq, :], in_=osum[:, k2, 0:128], func=AF.Identity, scale=rec[:, k2, 0:1]),
                        deps=[t3] + (odeps if qq == 0 else []), chan=c_act)
                    po_free += [t1, t2]
                t_st = S_.op("sp", lambda e, os_=os_, b=b, hd=hd: e.dma_start(
                    out=o_d[b * 512:(b + 1) * 512, hd * DH:(hd + 1) * DH].rearrange("(q p) d -> p q d", p=128),
                    in_=obuf[:, os_, :, :]), deps=[t_o], chan=c_st[os_], inc=16)
                oring.release(os_, t_st)
        S_.op("sp", lambda e: e.nop(), deps=[(c, c.count) for c in c_st if c.count > 0])
        S_.emit()
    return nc


D = 2048
KC = 16


def build_skv(S):
    NB = S // 512
    NCMP = S // 16 - 1
    nc = bass.Bass("TRN2", target_bir_lowering=False)
    hT_d = nc.dram_tensor("hT", [D, S], BF16, kind="ExternalInput").ap()
    w_d = nc.dram_tensor("w", [D, 384], F32, kind="ExternalInput").ap()
    w1_d = nc.dram_tensor("w1", [4096, 256], F32, kind="ExternalInput").ap()
    w2_d = nc.dram_tensor("w2", [256, 128], F32, kind="ExternalInput").ap()
    posT_d = nc.dram_tensor("posT", [128, 32], F32, kind="ExternalInput").ap()
    cmpT_d = nc.dram_tensor("cmpT", [128, S // 16], BF16, kind="ExternalOutput").ap()
    raw_d = nc.dram_tensor("raw", [2, 128, S], BF16, kind="ExternalOutput").ap()

    with ExitStack() as es:
        sb = lambda name, shape, dt: es.enter_context(nc.sbuf_tensor(name, shape, dt))
        rawT = sb("rawT", [128, 3, S], BF16)
        w = sb("wsb", [128, KC, 384], BF16)
        w1 = sb("w1sb", [128, 32, 256], BF16)
        w2 = sb("w2sb", [128, 2, 128], BF16)
        posT = sb("posTsb", [128, 32], BF16)
        hb = sb("hb", [128, 2, KC, 512], BF16)
        pbias = sb("pbias", [128, 2], F32)
        xx = sb("xx", [128, 512], F32)
        x2 = sb("x2", [128, 512], F32)
        th = sb("th", [128, 512], F32)
        hidT = sb("hidT", [128, 2, 1024], BF16)
        cmpo = sb("cmpo", [128, S // 16], BF16)
        ps = es.enter_context(nc.psum_tensor("ps", [128, 4, 512], F32))
        psb = es.enter_context(nc.psum_tensor("psb", [128, 2], F32))

        S_ = Sched(nc, es)
        c_pe, c_act, c_dve, c_pool = S_.chan("pe"), S_.chan("act"), S_.chan("dve"), S_.chan("pool")
        S_.main = {"act": c_act, "dve": c_dve, "pool": c_pool}
        c_w, c_st = S_.chan("ldw"), S_.chan("st")
        c_hb = [S_.chan("hb0"), S_.chan("hb1")]
        ring = Ring(4)
        hbr = Ring(2)

        S_.op("pool", lambda e: e.dma_start(out=w[:, :, :], in_=w_d.rearrange("(k p) n -> p k n", p=128)), chan=c_w, inc=16)
        S_.op("pool", lambda e: e.dma_start(out=w1[:, :, :], in_=w1_d.rearrange("(j d) h -> d j h", d=128)), chan=c_w, inc=16)
        S_.op("pool", lambda e: e.dma_start(out=w2[:, :, :], in_=w2_d.rearrange("(c p) d -> p c d", p=128)), chan=c_w, inc=16)
        t_w = S_.op("pool", lambda e: e.dma_start(out=posT[:, :], in_=posT_d[:, :]), chan=c_w, inc=16)
        S_.op("pool", lambda e: e.memset(cmpo[:, :], 0.0), chan=c_pool)
        t_ms = (c_pool, c_pool.count)

        t_raw = None
        for blk in range(NB):
            hs, hdeps = hbr.next()
            t_h = None
            for q in range(4):
                t_h = S_.op("sp", lambda e, hs=hs, blk=blk, q=q: e.dma_start(
                    out=hb[:, hs, q * 4:(q + 1) * 4, :],
                    in_=hT_d[q * 512:(q + 1) * 512, blk * 512:(blk + 1) * 512].rearrange("(k p) t -> p k t", p=128)),
                    deps=hdeps if q == 0 else [], chan=c_hb[hs], inc=16)
            for cb in range(3):
                s, bdeps = ring.next()
                for kc in range(KC):
                    tpe = S_.op("pe", lambda e, kc=kc, s=s, hs=hs, cb=cb: e.matmul(
                        ps[:, s, :], lhsT=w[:, kc, cb * 128:(cb + 1) * 128], rhs=hb[:, hs, kc, :], start=(kc == 0), stop=(kc == KC - 1)),
                        deps=([t_w, t_h] + bdeps) if kc == 0 else [], chan=c_pe if kc == KC - 1 else None)
                if cb % 2 == 0:
                    te = S_.op("act", lambda e, s=s, blk=blk, cb=cb: e.activation(out=rawT[:, cb, blk * 512:(blk + 1) * 512], in_=ps[:, s, :], func=AF.Copy),
                               deps=[tpe], chan=c_act)
                else:
                    te = S_.op("dve", lambda e, s=s, blk=blk, cb=cb: e.tensor_copy(out=rawT[:, cb, blk * 512:(blk + 1) * 512], in_=ps[:, s, :]),
                               deps=[tpe], chan=c_dve)
                ring.release(s, te)
            hbr.release(hs, tpe)
        t_rawA, t_rawD = (c_act, c_act.count), (c_dve, c_dve.count)
        for k in range(2):
            S_.op("sp", lambda e, k=k: e.dma_start(out=raw_d[k], in_=rawT[:, 1 + k, :]), deps=[t_rawA, t_rawD], chan=c_st, inc=16)
        for half in range(2):
            for j in range(32):
                tpb = S_.op("pe", lambda e, j=j, half=half: e.matmul(
                    psb[:, half:half + 1], lhsT=w1[:, j, half * 128:(half + 1) * 128], rhs=posT[:, j:j + 1], start=(j == 0), stop=(j == 31)),
                    deps=[t_w] if j == 0 else [], chan=c_pe if j == 31 else None)
        t_pb = S_.op("dve", lambda e: e.tensor_copy(out=pbias[:, :], in_=psb[:, :]), deps=[tpb], chan=c_dve)
        chunks = [(0, 512), (512, NCMP - 512)] if NCMP > 512 else [(0, NCMP)]
        t_hid = None
        for (n0, nn) in chunks:
            for half in range(2):
                s, bdeps = ring.next()
                for j in range(32):
                    tpe = S_.op("pe", lambda e, j=j, half=half, s=s, n0=n0, nn=nn: e.matmul(
                        ps[:, s, 0:nn], lhsT=w1[:, j, half * 128:(half + 1) * 128],
                        rhs=rawT[:, 0, 16 * n0 + j: 16 * n0 + j + 16 * (nn - 1) + 1: 16], start=(j == 0), stop=(j == 31)),
                        deps=([t_rawA, t_rawD] + bdeps) if j == 0 else [], chan=c_pe if j == 31 else None)
                t = S_.op("act", lambda e, s=s, nn=nn, half=half: e.activation(out=xx[:, 0:nn], in_=ps[:, s, 0:nn], func=AF.Identity,
                                                                              bias=pbias[:, half:half + 1], scale=1.0),
                          deps=[tpe, t_pb] + ([t_hid] if t_hid is not None else []), chan=c_act)
                ring.release(s, t)
                t = S_.op("dve", lambda e, nn=nn: e.tensor_tensor(out=x2[:, 0:nn], in0=xx[:, 0:nn], in1=xx[:, 0:nn], op=ALU.mult), deps=[t], chan=c_dve)
                t = S_.op("dve", lambda e, nn=nn: e.tensor_scalar(out=x2[:, 0:nn], in0=x2[:, 0:nn], scalar1=0.044715, scalar2=1.0,
                                                                  op0=ALU.mult, op1=ALU.add), deps=[t], chan=c_dve)
                t = S_.op("dve", lambda e, nn=nn: e.tensor_tensor(out=x2[:, 0:nn], in0=x2[:, 0:nn], in1=xx[:, 0:nn], op=ALU.mult), deps=[t], chan=c_dve)
                t = S_.op("act", lambda e, nn=nn: e.activation(out=th[:, 0:nn], in_=x2[:, 0:nn], func=AF.Tanh, scale=0.7978845608028654),
                          deps=[t], chan=c_act)
                t = S_.op("dve", lambda e, nn=nn: e.scalar_tensor_tensor(out=th[:, 0:nn], in0=th[:, 0:nn], scalar=1.0, in1=xx[:, 0:nn],
                                                                         op0=ALU.add, op1=ALU.mult), deps=[t], chan=c_dve)
                t_hid = S_.op("dve", lambda e, nn=nn, n0=n0, half=half: e.tensor_scalar(
                    out=hidT[:, half, n0:n0 + nn], in0=th[:, 0:nn], scalar1=0.5, scalar2=None, op0=ALU.mult), deps=[t], chan=c_dve)
        t_last = None
        for (n0, nn) in chunks:
            s, bdeps = ring.next()
            for half in range(2):
                tpe = S_.op("pe", lambda e, half=half, s=s, n0=n0, nn=nn: e.matmul(
                    ps[:, s, 0:nn], lhsT=w2[:, half, :], rhs=hidT[:, half, n0:n0 + nn], start=(half == 0), stop=(half == 1)),
                    deps=([t_hid] + bdeps) if half == 0 else [], chan=c_pe if half == 1 else None)
            t_last = S_.op("act", lambda e, s=s, n0=n0, nn=nn: e.activation(out=cmpo[:, n0:n0 + nn], in_=ps[:, s, 0:nn], func=AF.Copy),
                           deps=[tpe, t_ms], chan=c_act)
            ring.release(s, t_last)
        S_.op("sp", lambda e: e.dma_start(out=cmpT_d[:, :], in_=cmpo[:, :]), deps=[t_last], chan=c_st, inc=16)
        S_.op("sp", lambda e: e.nop(), deps=[(c_st, c_st.count)])
        S_.emit()
    return nc


D = 2048
KC = 16
DH = 128
NEG = -30000.0


def rel_bucket_np(d):
    n = np.maximum(d, 0)
    nf = np.maximum(n, 1).astype(np.float32)
    large = 16 + (np.log(nf / np.float32(16)) / np.float32(math.log(2048 / 16)) * np.float32(16)).astype(np.int32)
    large = np.minimum(large, 31)
    return np.where(n < 16, n, large)


def nsa_consts(rel4):
    p = np.arange(128)[:, None]
    u = np.arange(128)[None, :]
    d = p + 1889 - 16 * u
    Bc = np.where(d[:, None, :] >= 0, rel4[rel_bucket_np(d)].transpose(0, 2, 1), NEG).astype(np.float32)
    y2 = np.arange(2048)[None, :]
    d2 = p + 1920 - y2
    Bs2 = np.where(d2[:, None, :] >= 0, rel4[:, 0:2][rel_bucket_np(d2)].transpose(0, 2, 1), NEG).astype(np.float32)
    y = np.arange(640)[None, :]
    dw = p + 512 - y
    Bw = np.where(((dw >= 0) & (dw < 512))[:, None, :], rel4[:, 0:2][rel_bucket_np(dw)].transpose(0, 2, 1), NEG).astype(np.float32)
    x = np.arange(510)[None, :]
    c = x - 254
    hi = (p >= 64).astype(np.int64)
    Vu = (c <= hi).astype(np.float32).astype(NPBF)
    Fu = np.where((c == hi) | (c == hi - 1), 1e4, -5.0).astype(np.float32).astype(NPBF)
    t31 = np.ascontiguousarray(np.broadcast_to(rel4[31][None, :], (128, 4))).astype(np.float32)
    ident = np.eye(128, dtype=NPBF)
    return dict(Bc=np.ascontiguousarray(Bc), Bs2=np.ascontiguousarray(Bs2), Bw=np.ascontiguousarray(Bw),
                Vu=np.ascontiguousarray(Vu), Fu=np.ascontiguousarray(Fu), t31=t31, ident=ident)


def build_nsa(S, tiles=None):
    NT = S // 128
    NCMP = S // 16 - 1
    NCP = S // 16
    NSB = S // 64
    IW = NCP + 4
    tiles = list(range(NT)) if tiles is None else tiles
    nc = bass.Bass("TRN2", target_bir_lowering=False)
    dI = lambda name, shape, dt: nc.dram_tensor(name, shape, dt, kind="ExternalInput").ap()
    hT_d = dI("hT", [D, S], BF16)
    w_d = dI("w", [D, 524], F32)
    kcmpT_d = dI("kcmpT", [128, NCP], BF16)
    vcmp_d = dI("vcmp", [128, NCP // 128, 128], BF16)
    kselT_d = dI("kselT", [128, S], BF16)
    vsel_d = dI("vsel", [128, NT, 128], BF16)
    kwinT_d = dI("kwinT", [128, S], BF16)
    vwin_d = dI("vwin", [128, NT, 128], BF16)
    Bc_d = dI("Bc", [128, 4, 128], F32)
    Bs2_d = dI("Bs2", [128, 2, 2048], F32)
    Bw_d = dI("Bw", [128, 2, 640], F32)
    Vu_d = dI("Vu", [128, 510], BF16)
    Fu_d = dI("Fu", [128, 510], BF16)
    t31_d = dI("t31", [128, 4], F32)
    id_d = dI("ident", [128, 128], BF16)
    o_d = nc.dram_tensor("o", [S, 256], BF16, kind="ExternalOutput").ap()

    with ExitStack() as es:
        sb = lambda name, shape, dt: es.enter_context(nc.sbuf_tensor(name, shape, dt))
        kselT = sb("kselTsb", [128, S], BF16)
        vsel = sb("vselsb", [128, NT, 128], BF16)
        kwinT = sb("kwinTsb", [128, S], BF16)
        vwin = sb("vwinsb", [128, NT, 128], BF16)
        kcmpT = sb("kcmpTsb", [128, NCP], BF16)
        vcmp = sb("vcmpsb", [128, NCP // 128, 128], BF16)
        w = sb("wsb", [128, KC, 524], BF16)
        Bc = sb("Bcsb", [128, 4, 128], F32)
        Bs2 = sb("Bs2sb", [128, 2, 2048], F32)
        Bw = sb("Bwsb", [128, 2, 640], F32)
        Vu = sb("Vusb", [128, 510], BF16)
        Fu = sb("Fusb", [128, 510], BF16)
        t31 = sb("t31sb", [128, 4], F32)
        ident = sb("identsb", [128, 128], BF16)
        hTt = sb("hTt", [128, 1, KC, 128], BF16)
        qT = sb("qT", [128, 2, 4, 128], BF16)
        gate = sb("gate", [128, 2, 12], F32)
        E = sb("E", [128, 2, 512], BF16)
        Lb = sb("Lb", [128, 2, 512], F32)
        Pb = sb("Pb", [128, 3, 512], BF16)
        PT = sb("PT", [128, 3, 512], BF16)
        Ecmp = sb("Ecmp", [128, 1, NCP], F32)
        imp = sb("imp", [128, IW], F32)
        isel = sb("isel", [128, NSB], F32)
        sc = sb("sc", [128, NSB], F32)
        wk = isel
        selm = sb("selm", [128, 2, NSB], BF16)
        m8a = sb("m8a", [128, 8], F32)
        m8b = sb("m8b", [128, 8], F32)
        thr = sb("thr", [128, 1], F32)
        rsum = sb("rsum", [128, 2, 4], F32)
        rcmp = sb("rcmp", [128, 2, 4], F32)
        rs = sb("rs", [128, 2, 2, 40], F32)
        rtot = sb("rtot", [128, 4], F32)
        outt = sb("outt", [128, 2, 2, 128], F32)
        obuf = sb("obuf", [128, 2, 2, 128], BF16)
        pring = es.enter_context(nc.psum_tensor("pring", [128, 3, 512], F32))
        pT = es.enter_context(nc.psum_tensor("pT", [128, 2, 1024], BF16))
        pO = es.enter_context(nc.psum_tensor("pO", [128, 2, 512], F32))
        pq = es.enter_context(nc.psum_tensor("pq", [128, 512], F32))

        S_ = Sched(nc, es)
        c_pe, c_act, c_dve, c_pool = S_.chan("pe"), S_.chan("act"), S_.chan("dve"), S_.chan("pool")
        S_.main = {"act": c_act, "dve": c_dve, "pool": c_pool}
        c_ld, c_w = S_.chan("ld"), S_.chan("ldw")
        c_st = [S_.chan("st0"), S_.chan("st1")]
        c_h = [S_.chan("h0")]

        ld = lambda out, in_: S_.op("sp", lambda e: e.dma_start(out=out, in_=in_), chan=c_ld, inc=16)
        for q4 in range(4):
            ld(kselT[:, q4 * (S // 4):(q4 + 1) * (S // 4)], kselT_d[:, q4 * (S // 4):(q4 + 1) * (S // 4)])
            ld(kwinT[:, q4 * (S // 4):(q4 + 1) * (S // 4)], kwinT_d[:, q4 * (S // 4):(q4 + 1) * (S // 4)])
        ld(vsel[:, :, :], vsel_d[:, :, :])
        ld(vwin[:, :, :], vwin_d[:, :, :])
        ld(kcmpT[:, :], kcmpT_d[:, :])
        ld(vcmp[:, :, :], vcmp_d[:, :, :])
        ld(Bc[:, :, :], Bc_d[:, :, :]); ld(Bs2[:, :, :], Bs2_d[:, :, :]); ld(Bw[:, :, :], Bw_d[:, :, :])
        ld(Vu[:, :], Vu_d[:, :]); ld(Fu[:, :], Fu_d[:, :]); ld(t31[:, :], t31_d[:, :])
        t_ld = ld(ident[:, :], id_d[:, :])
        t_w = S_.op("pool", lambda e: e.dma_start(out=w[:, :, :], in_=w_d.rearrange("(k p) n -> p k n", p=128)), chan=c_w, inc=16)
        S_.op("pool", lambda e: e.memset(imp[:, :], 0.0), chan=c_pool)
        t_ms = (c_pool, c_pool.count)

        ring, pbr, ptr, ptsr, por = Ring(3), Ring(3), Ring(2), Ring(3), Ring(2)
        er, lbr, hr = Ring(2), Ring(2), Ring(1)
        queue = []

        def emitB(tk):
            cols = tk["cols"]
            nb = (cols + 127) // 128
            tsl, tdeps = ptr.next()
            tp = None
            for a in range(nb):
                ca = min(128, cols - a * 128)
                tp = S_.op("pe", lambda e, a=a, ca=ca, tsl=tsl, ps_=tk["pb"]: e.transpose(
                    out=pT[0:ca, tsl, a * 128:(a + 1) * 128], in_=Pb[:, ps_, a * 128:a * 128 + ca], identity=ident[:, :]),
                    deps=([tk["ready"], t_ld] + tdeps) if a == 0 else [], chan=c_pe if a == nb - 1 else None)
            pbr.release(tk["pb"], tp)
            pts, pdeps = ptsr.next()
            tk["pts"] = pts
            if cols % 128 == 0:
                tcp = S_.op("act", lambda e, tsl=tsl, pts=pts, cols=cols: e.activation(out=PT[:, pts, 0:cols], in_=pT[:, tsl, 0:cols], func=AF.Copy),
                            deps=[tp] + pdeps, chan=c_act)
            else:
                full = (cols // 128) * 128
                rem = cols - full
                if full:
                    S_.op("act", lambda e, tsl=tsl, pts=pts, full=full: e.activation(out=PT[:, pts, 0:full], in_=pT[:, tsl, 0:full], func=AF.Copy),
                          deps=[tp] + pdeps, chan=c_act)
                tcp = S_.op("act", lambda e, tsl=tsl, pts=pts, full=full, rem=rem: e.activation(
                    out=PT[0:rem, pts, full:full + 128], in_=pT[0:rem, tsl, full:full + 128], func=AF.Copy),
                    deps=[tp] + pdeps, chan=c_act)
            ptr.release(tsl, tcp)
            tk["ptready"] = tcp

        def emitC(tk):
            cols = tk["cols"]
            nb = (cols + 127) // 128
            g = tk["grp"]
            if g["os"] is None:
                g["os"], g["odeps"] = por.next()
            os_ = g["os"]
            tpv = None
            for a in range(nb):
                ca = min(128, cols - a * 128)
                first = tk["first"] and a == 0
                last = tk["last"] and a == nb - 1
                vap = tk["v"](a, ca)
                tpv = S_.op("pe", lambda e, a=a, ca=ca, os_=os_, pts=tk["pts"], vap=vap, first=first, last=last: e.matmul(
                    pO[:, os_, 0:128], lhsT=PT[0:ca, pts, a * 128:(a + 1) * 128], rhs=vap, start=first, stop=last),
                    deps=([tk["ptready"]] + (g["odeps"] if first else [])) if a == 0 else [], chan=c_pe if a == nb - 1 else None)
            ptsr.release(tk["pts"], tpv)
            if tk["last"]:
                g["done"](tpv)

        def push(tk):
            queue.append(tk)
            if len(queue) >= 2:
                emitB(queue[-2])
            if len(queue) >= 3:
                emitC(queue[-3])

        def flush():
            if len(queue) >= 1:
                emitB(queue[-1])
            if len(queue) >= 2:
                emitC(queue[-2])
            if len(queue) >= 1:
                emitC(queue[-1])
            queue.clear()

        out_state = {}
        selm_free = [[], []]
        ecmp_free = [[], []]
        outt_free = [[], []]
        obuf_free = [[], []]
        for ti, i in enumerate(tiles):
            t0 = 128 * i
            sl = ti % 2
            hs, hdeps = hr.next()
            t_h = S_.op("sp", lambda e, hs=hs, t0=t0: e.dma_start(
                out=hTt[:, hs, :, :], in_=hT_d[:, t0:t0 + 128].rearrange("(k p) t -> p k t", p=128)), deps=hdeps, chan=c_h[hs], inc=16)
            tq = None
            for h in range(4):
                for kc in range(KC):
                    tq = S_.op("pe", lambda e, h=h, kc=kc, hs=hs: e.matmul(
                        pq[:, h * 128:(h + 1) * 128], lhsT=w[:, kc, h * 128:(h + 1) * 128], rhs=hTt[:, hs, kc, :],
                        start=(kc == 0), stop=(kc == KC - 1)),
                        deps=([t_w, t_h] + out_state.get("pq_free", [])) if (h == 0 and kc == 0) else [],
                        chan=c_pe if (h == 3 and kc == KC - 1) else None)
            t_qT = S_.op("act", lambda e, sl=sl: e.activation(out=qT[:, sl, :, :], in_=pq[:, 0:512].rearrange("p (h t) -> p h t", h=4),
                                                              func=AF.Copy, scale=DH ** -0.5),
                         deps=[tq] + out_state.get("qT_free%d" % sl, []), chan=c_act)
            tg = None
            for kc in range(KC):
                tg = S_.op("pe", lambda e, kc=kc, hs=hs: e.matmul(pq[:, 0:12], lhsT=hTt[:, hs, kc, :], rhs=w[:, kc, 512:524],
                                                                   start=(kc == 0), stop=(kc == KC - 1)),
                           deps=[t_qT] if kc == 0 else [], chan=c_pe if kc == KC - 1 else None)
            hr.release(hs, tg)
            t_g = S_.op("act", lambda e, sl=sl: e.activation(out=gate[:, sl, :], in_=pq[:, 0:12], func=AF.Exp, scale=-1.0),
                        deps=[tg] + out_state.get("gate_free%d" % sl, []), chan=c_act)
            out_state["pq_free"] = [t_g]
            t_g = S_.op("dve", lambda e, sl=sl: e.tensor_scalar(out=gate[:, sl, :], in0=gate[:, sl, :], scalar1=1.0, scalar2=None, op0=ALU.add),
                        deps=[t_g], chan=c_dve)
            t_g = S_.op("dve", lambda e, sl=sl: e.reciprocal(out=gate[:, sl, :], in_=gate[:, sl, :]), deps=[t_g], chan=c_dve)

            qT_users = []
            groups_done = []

            def make_group(h, br, rs_cols_fn, cmp_rec=None):
                g = {"os": None, "odeps": None}

                def done(tpv, h=h, br=br, g=g, sl=sl):
                    os_ = g["os"]
                    if cmp_rec is None:
                        ncol = g["ncols"]
                        t1 = S_.op("dve", lambda e: e.reduce_sum(out=rtot[:, 0:1], in_=rs[:, sl, h, g["c0"]:g["c0"] + ncol],
                                                                 axis=mybir.AxisListType.X), deps=[g["rs_ready"]], chan=c_dve)
                        t1 = S_.op("dve", lambda e: e.tensor_scalar(out=rtot[:, 0:1], in0=rtot[:, 0:1], scalar1=1e-30, scalar2=None, op0=ALU.max),
                                   deps=[t1], chan=c_dve)
                        t1 = S_.op("dve", lambda e: e.reciprocal(out=rtot[:, 1:2], in_=rtot[:, 0:1]), deps=[t1], chan=c_dve)
                        recap = rtot[:, 1:2]
                    else:
                        t1 = cmp_rec[1]
                        recap = cmp_rec[0]
                    t2 = S_.op("dve", lambda e: e.tensor_tensor(out=rtot[:, 2:3], in0=recap, in1=gate[:, sl, h * 3 + br:h * 3 + br + 1], op=ALU.mult),
                               deps=[t1, t_g], chan=c_dve)
                    if br == 0:
                        t3 = S_.op("dve", lambda e: e.tensor_scalar(out=outt[:, sl, h, :], in0=pO[:, os_, 0:128], scalar1=rtot[:, 2:3], scalar2=None,
                                                                    op0=ALU.mult), deps=[t2, tpv] + outt_free[sl], chan=c_dve)
                    else:
                        t3 = S_.op("dve", lambda e: e.scalar_tensor_tensor(out=outt[:, sl, h, :], in0=pO[:, os_, 0:128], scalar=rtot[:, 2:3],
                                                                           in1=outt[:, sl, h, :], op0=ALU.mult, op1=ALU.add),
                                   deps=[t2, tpv], chan=c_dve)
                    por.release(os_, t3)
                    groups_done.append(t3)
                g["done"] = done
                return g

            hi = min(NCMP, 8 * i + 8)
            lo = max(0, 8 * i - 120)
            u0 = lo - (8 * i - 120)
            cchunks = [(0, min(512, hi))] + ([(512, hi - 512)] if hi > 512 else [])
            t_imp = None
            for h in range(4):
                ec = 0
                t_e = None
                for (n0, nn) in cchunks:
                    s, sdeps = ring.next()
                    tpe = S_.op("pe", lambda e, s=s, h=h, n0=n0, nn=nn, sl=sl: e.matmul(
                        pring[:, s, 0:nn], lhsT=qT[:, sl, h, :], rhs=kcmpT[:, n0:n0 + nn], start=True, stop=True),
                        deps=[t_qT, t_ld] + sdeps, chan=c_pe)
                    t_e = S_.op("act", lambda e, s=s, ec=ec, h=h, n0=n0, nn=nn: e.activation(
                        out=Ecmp[:, ec, n0:n0 + nn], in_=pring[:, s, 0:nn], func=AF.Exp, bias=t31[:, h:h + 1], scale=1.0),
                        deps=[tpe] + ecmp_free[ec], chan=c_act)
                    ring.release(s, t_e)
                s, sdeps = ring.next()
                nw = hi - lo
                tpe = S_.op("pe", lambda e, s=s, h=h, lo=lo, nw=nw, sl=sl: e.matmul(
                    pring[:, s, 0:nw], lhsT=qT[:, sl, h, :], rhs=kcmpT[:, lo:lo + nw], start=True, stop=True), deps=sdeps, chan=c_pe)
                qT_users.append(tpe)
                ls, ldeps = lbr.next()
                tl = S_.op("dve", lambda e, s=s, ls=ls, h=h, nw=nw, u0=u0: e.tensor_tensor(
                    out=Lb[:, ls, 0:nw], in0=pring[:, s, 0:nw], in1=Bc[:, h, u0:u0 + nw], op=ALU.add), deps=[tpe] + ldeps, chan=c_dve)
                ring.release(s, tl)
                t_e = S_.op("act", lambda e, ls=ls, ec=ec, lo=lo, nw=nw: e.activation(out=Ecmp[:, ec, lo:lo + nw], in_=Lb[:, ls, 0:nw], func=AF.Exp),
                            deps=[tl, t_e], chan=c_act)
                lbr.release(ls, t_e)
                t1 = S_.op("dve", lambda e, ec=ec, h=h, hi=hi, sl=sl: e.reduce_sum(out=rsum[:, sl, h:h + 1], in_=Ecmp[:, ec, 0:hi],
                                                                                  axis=mybir.AxisListType.X), deps=[t_e], chan=c_dve)
                t1 = S_.op("dve", lambda e, h=h, sl=sl: e.tensor_scalar(out=rsum[:, sl, h:h + 1], in0=rsum[:, sl, h:h + 1], scalar1=1e-30, scalar2=None,
                                                                        op0=ALU.max), deps=[t1], chan=c_dve)
                t_rc = S_.op("dve", lambda e, h=h, sl=sl: e.reciprocal(out=rcmp[:, sl, h:h + 1], in_=rsum[:, sl, h:h + 1]), deps=[t1], chan=c_dve)
                if h == 0:
                    t_imp = S_.op("dve", lambda e, ec=ec, hi=hi, sl=sl: e.tensor_scalar(
                        out=imp[:, 1:1 + hi], in0=Ecmp[:, ec, 0:hi], scalar1=rcmp[:, sl, 0:1], scalar2=None, op0=ALU.mult),
                        deps=[t_rc, t_ms] + out_state.get("imp_free", []), chan=c_dve)
                else:
                    t_imp = S_.op("dve", lambda e, ec=ec, hi=hi, h=h, sl=sl: e.scalar_tensor_tensor(
                        out=imp[:, 1:1 + hi], in0=Ecmp[:, ec, 0:hi], scalar=rcmp[:, sl, h:h + 1], in1=imp[:, 1:1 + hi],
                        op0=ALU.mult, op1=ALU.add), deps=[t_rc, t_imp], chan=c_dve)
                ecmp_free[ec] = [t_imp]
                if h < 2:
                    g = make_group(h, 0, None, cmp_rec=(rcmp[:, sl, h:h + 1], t_rc))
                    for ci, (n0, nn) in enumerate(cchunks):
                        pb, pdeps = pbr.next()
                        tcp = S_.op("pool", lambda e, pb=pb, ec=ec, n0=n0, nn=nn: e.tensor_copy(out=Pb[:, pb, 0:nn], in_=Ecmp[:, ec, n0:n0 + nn]),
                                    deps=[t_e] + pdeps, chan=c_pool)
                        ecmp_free[ec] = ecmp_free[ec] + [tcp]
                        push(dict(pb=pb, cols=nn, ready=tcp, grp=g, first=(ci == 0), last=(ci == len(cchunks) - 1),
                                  v=lambda a, ca, n0=n0: vcmp[0:ca, n0 // 128 + a, :]))
            ss = sl
            t = S_.op("dve", lambda e: e.tensor_reduce(out=isel[:, :], in_=imp[:, 0:4 * NSB].rearrange("p (j f) -> p j f", f=4),
                                                       axis=mybir.AxisListType.X, op=ALU.add), deps=[t_imp], chan=c_dve)
            t = S_.op("dve", lambda e: e.tensor_tensor(out=isel[:, :], in0=isel[:, :], in1=imp[:, 4:4 * NSB + 4:4], op=ALU.add), deps=[t], chan=c_dve)
            out_state["imp_free"] = [t]
            x0 = 254 - 2 * i
            t = S_.op("dve", lambda e, x0=x0: e.scalar_tensor_tensor(out=sc[:, :], in0=isel[:, :], scalar=1.0, in1=Vu[:, x0:x0 + NSB],
                                                                     op0=ALU.add, op1=ALU.mult), deps=[t, t_ld], chan=c_dve)
            t = S_.op("dve", lambda e, x0=x0: e.scalar_tensor_tensor(out=sc[:, :], in0=sc[:, :], scalar=-1.0, in1=Fu[:, x0:x0 + NSB],
                                                                     op0=ALU.add, op1=ALU.max), deps=[t], chan=c_dve)
            t = S_.op("dve", lambda e: e.memset(sc[:, 0:1], 1e4), deps=[t], chan=c_dve)
            t = S_.op("dve", lambda e: e.max(out=m8a[:, :], in_=sc[:, :]), deps=[t], chan=c_dve)
            t = S_.op("dve", lambda e: e.match_replace(out=wk[:, :], in_to_replace=m8a[:, :], in_values=sc[:, :], imm_value=-2.0), deps=[t], chan=c_dve)
            t = S_.op("dve", lambda e: e.max(out=m8b[:, :], in_=wk[:, :]), deps=[t], chan=c_dve)
            t = S_.op("dve", lambda e: e.tensor_scalar(out=thr[:, :], in0=m8b[:, 7:8], scalar1=-0.5, scalar2=None, op0=ALU.max), deps=[t], chan=c_dve)
            t_selm = S_.op("dve", lambda e, ss=ss: e.tensor_scalar(out=selm[:, ss, :], in0=sc[:, :], scalar1=thr[:, 0:1], scalar2=None, op0=ALU.is_ge),
                           deps=[t] + selm_free[ss], chan=c_dve)
            nkw = min(640, t0 + 128)
            s0 = t0 + 128 - nkw
            yoff = 640 - nkw
            wchunks = [(0, min(512, nkw))] + ([(512, nkw - 512)] if nkw > 512 else [])
            for h in range(2):
                g = make_group(h, 2, None)
                g["c0"], g["ncols"] = 32, len(wchunks)
                for ci, (off, cols) in enumerate(wchunks):
                    s, sdeps = ring.next()
                    tpe = S_.op("pe", lambda e, s=s, h=h, off=off, cols=cols, s0=s0, sl=sl: e.matmul(
                        pring[:, s, 0:cols], lhsT=qT[:, sl, h, :], rhs=kwinT[:, s0 + off:s0 + off + cols], start=True, stop=True),
                        deps=sdeps, chan=c_pe)
                    qT_users.append(tpe)
                    ls, ldeps = lbr.next()
                    tl = S_.op("dve", lambda e, s=s, ls=ls, h=h, off=off, cols=cols, yoff=yoff: e.tensor_tensor(
                        out=Lb[:, ls, 0:cols], in0=pring[:, s, 0:cols], in1=Bw[:, h, yoff + off:yoff + off + cols], op=ALU.add),
                        deps=[tpe] + ldeps, chan=c_dve)
                    ring.release(s, tl)
                    pb, pdeps = pbr.next()
                    t_p = S_.op("act", lambda e, ls=ls, pb=pb, cols=cols, h=h, ci=ci, sl=sl: e.activation(
                        out=Pb[:, pb, 0:cols], in_=Lb[:, ls, 0:cols], func=AF.Exp, accum_out=rs[:, sl, h, 32 + ci:33 + ci]),
                        deps=[tl] + pdeps, chan=c_act)
                    lbr.release(ls, t_p)
                    g["rs_ready"] = t_p
                    push(dict(pb=pb, cols=cols, ready=t_p, grp=g, first=(ci == 0), last=(ci == len(wchunks) - 1),
                              v=lambda a, ca, s0=s0, off=off: vwin[:, (s0 + off) // 128 + a, :]))
            nk = t0 + 128
            nch = (nk + 511) // 512
            t_m = None
            for h in range(2):
                g = make_group(h, 1, None)
                g["c0"], g["ncols"] = 0, nch
                for kb in range(nch):
                    cols = min(512, nk - 512 * kb)
                    far = (512 * kb <= t0 - 2048)
                    s, sdeps = ring.next()
                    tpe = S_.op("pe", lambda e, s=s, h=h, kb=kb, cols=cols, sl=sl: e.matmul(
                        pring[:, s, 0:cols], lhsT=qT[:, sl, h, :], rhs=kselT[:, 512 * kb:512 * kb + cols], start=True, stop=True),
                        deps=sdeps, chan=c_pe)
                    qT_users.append(tpe)
                    es_, edeps = er.next()
                    if far:
                        t_e = S_.op("act", lambda e, s=s, es_=es_, h=h, cols=cols: e.activation(
                            out=E[:, es_, 0:cols], in_=pring[:, s, 0:cols], func=AF.Exp, bias=t31[:, h:h + 1], scale=1.0),
                            deps=[tpe] + edeps, chan=c_act)
                        ring.release(s, t_e)
                    else:
                        y2 = 512 * kb - t0 + 1920
                        ls, ldeps = lbr.next()
                        tl = S_.op("dve", lambda e, s=s, ls=ls, h=h, cols=cols, y2=y2: e.tensor_tensor(
                            out=Lb[:, ls, 0:cols], in0=pring[:, s, 0:cols], in1=Bs2[:, h, y2:y2 + cols], op=ALU.add),
                            deps=[tpe] + ldeps, chan=c_dve)
                        ring.release(s, tl)
                        t_e = S_.op("act", lambda e, ls=ls, es_=es_, cols=cols: e.activation(out=E[:, es_, 0:cols], in_=Lb[:, ls, 0:cols], func=AF.Exp),
                                    deps=[tl] + edeps, chan=c_act)
                        lbr.release(ls, t_e)
                    pb, pdeps = pbr.next()
                    nj = cols // 64
                    t_m = S_.op("dve", lambda e, es_=es_, pb=pb, cols=cols, nj=nj, kb=kb, h=h, ss=ss, sl=sl: e.scalar_tensor_tensor(
                        out=Pb[:, pb, 0:cols].rearrange("p (j f) -> p j f", f=64),
                        in0=E[:, es_, 0:cols].rearrange("p (j f) -> p j f", f=64), scalar=1.0,
                        in1=selm[:, ss, 8 * kb:8 * kb + nj].unsqueeze(2).broadcast_to([128, nj, 64]),
                        op0=ALU.mult, op1=ALU.mult, accum_out=rs[:, sl, h, kb:kb + 1]),
                        deps=[t_e, t_selm] + pdeps, chan=c_dve)
                    er.release(es_, t_m)
                    g["rs_ready"] = t_m
                    push(dict(pb=pb, cols=cols, ready=t_m, grp=g, first=(kb == 0), last=(kb == nch - 1),
                              v=lambda a, ca, kb=kb: vsel[:, 4 * kb + a, :]))
            selm_free[ss] = [t_m]
            out_state["qT_free%d" % sl] = [qT_users[-1]]
            out_state["gate_free%d" % sl] = []
            flush()
            t_fin = groups_done[-1]
            t_ob = S_.op("act", lambda e, sl=sl: e.activation(out=obuf[:, sl, :, :], in_=outt[:, sl, :, :], func=AF.Copy),
                         deps=[(c_dve, c_dve.count)] + obuf_free[sl], chan=c_act)
            outt_free[sl] = [t_ob]
            out_state["gate_free%d" % sl] = [(c_dve, c_dve.count)]
            t_st = S_.op("sp", lambda e, sl=sl, t0=t0: e.dma_start(out=o_d[t0:t0 + 128, :].rearrange("p (h d) -> p h d", h=2), in_=obuf[:, sl, :, :]),
                         deps=[t_ob], chan=c_st[sl], inc=16)
            obuf_free[sl] = [t_st]
        S_.op("sp", lambda e: e.nop(), deps=[(c, c.count) for c in c_st if c.count > 0])
        S_.emit()
    return nc


NCORES = 8
FOX_CORES = 2


def build_cast(C):
    nc = bass.Bass("TRN2", target_bir_lowering=False)
    x_d = nc.dram_tensor("xT", [D, C], F32, kind="ExternalInput").ap()
    y_d = nc.dram_tensor("yT", [D, C], BF16, kind="ExternalOutput").ap()
    with ExitStack() as es:
        buf = es.enter_context(nc.sbuf_tensor("buf", [128, 2, C], BF16))
        S_ = Sched(nc, es)
        c_ld = [S_.chan("ld0"), S_.chan("ld1")]
        c_st = [S_.chan("st0"), S_.chan("st1")]
        st = [None, None]
        for kc in range(D // 128):
            s = kc % 2
            t = S_.op("pool", lambda e, kc=kc, s=s: e.dma_start(out=buf[:, s, :], in_=x_d[kc * 128:(kc + 1) * 128, :], max_dma_last_dim=4096),
                      deps=[st[s]], chan=c_ld[s], inc=16)
            st[s] = S_.op("sp", lambda e, kc=kc, s=s: e.dma_start(out=y_d[kc * 128:(kc + 1) * 128, :], in_=buf[:, s, :]),
                          deps=[t], chan=c_st[s], inc=16)
        S_.op("sp", lambda e: e.nop(), deps=st)
        S_.emit()
    return nc


def _launch(nc, in_maps):
    import concourse.bass_utils as bu
    res = bu.run_bass_kernel_spmd(nc, in_maps, core_ids=list(range(len(in_maps))))
    return res.results


def _pmaj(v):
    return np.ascontiguousarray(v.reshape(-1, 128, 128).transpose(1, 0, 2))


_PROGS = {}


def _prog(key, fn):
    if key not in _PROGS:
        _PROGS[key] = fn()
    return _PROGS[key]


def forward(inp, S):
    f32 = lambda a: np.ascontiguousarray(np.asarray(a, dtype=np.float32))
    x = f32(inp["x"]).reshape(S, D)
    fox_w_in, fox_b_f, fox_w_o = f32(inp["fox_w_in"]), f32(inp["fox_b_f"]), f32(inp["fox_w_o"])
    nsa_w_in, nsa_w_o, kv_w = f32(inp["nsa_w_in"]), f32(inp["nsa_w_o"]), f32(inp["kv_w"])
    rel_bias = f32(inp["rel_bias"])
    mlp_w1, mlp_w2 = f32(inp["mlp_w1"]), f32(inp["mlp_w2"])
    lng = [f32(inp[k]) for k in ("ln1_g", "ln1_b", "ln2_g", "ln2_b")]
    NPC = min(NCORES, S // TP)
    T = S // NPC
    NT = S // 128
    NCP = S // 16
    ident_bf = np.eye(128, dtype=NPBF)
    cst_fox = fox_consts()

    CC = S // NCORES
    xT = np.ascontiguousarray(x.T)
    nc_cast = _prog(("cast", CC), lambda: build_cast(CC))
    r = _launch(nc_cast, [{"xT": np.ascontiguousarray(xT[:, c * CC:(c + 1) * CC])} for c in range(NCORES)])
    hT = np.ascontiguousarray(np.concatenate([np.asarray(r[c]["yT"]) for c in range(NCORES)], axis=1))
    del xT
    h = x

    nc_post = _prog(("post", T), lambda: build_post(T, 4 * D, True))

    def post(A, h, wo, layer):
        aT = np.ascontiguousarray(A.T)
        lnp = np.ascontiguousarray(np.stack([lng[0][layer], lng[1][layer], lng[2][layer], lng[3][layer]]))
        maps = [{"aT": np.ascontiguousarray(aT[:, c * T:(c + 1) * T]), "h": np.ascontiguousarray(h[c * T:(c + 1) * T]),
                 "wo": wo, "w1": mlp_w1[layer], "w2": mlp_w2[layer], "lnp": lnp, "ident": ident_bf} for c in range(NPC)]
        r = _launch(nc_post, maps)
        h2 = np.concatenate([np.asarray(r[c]["hout"]) for c in range(NPC)], axis=0)
        hT2 = np.ascontiguousarray(np.concatenate([np.asarray(r[c]["hTout"]) for c in range(NPC)], axis=1))
        return h2, hT2

    nc_fox = _prog(("fox", S), lambda: build_fox(S, 2))
    for l in range(2):
        maps = []
        for c in range(NCORES):
            w = np.empty((2, D, 385), np.float32)
            for k in range(2):
                hd = 2 * c + k
                w[k, :, 0:128] = fox_w_in[l][:, hd * 128:(hd + 1) * 128]
                w[k, :, 128:256] = fox_w_in[l][:, 2048 + hd * 128:2048 + (hd + 1) * 128]
                w[k, :, 256:384] = fox_w_in[l][:, 4096 + hd * 128:4096 + (hd + 1) * 128]
                w[k, :, 384] = fox_w_in[l][:, 6144 + hd]
            maps.append({"hT": hT, "w": w, "bf": np.ascontiguousarray(fox_b_f[l][None, 2 * c:2 * c + 2]), "cst": cst_fox})
        r = []
        for p0 in range(0, NCORES, FOX_CORES):
            r += _launch(nc_fox, maps[p0:p0 + FOX_CORES])
        A = np.concatenate([np.asarray(r[c]["o"]) for c in range(NCORES)], axis=1)
        h, hT = post(A, h, fox_w_o[l], l)

    nc_skv = _prog(("skv", S), lambda: build_skv(S))
    maps = []
    for c in range(NCORES):
        sc, g = c // 4, c % 4
        cols = [(sc * 4 + g) * 128]
        for e in (2 * c, 2 * c + 1):
            cols.append(((2 + e // 4) * 4 + e % 4) * 128)
        w = np.ascontiguousarray(np.concatenate([kv_w[:, c0:c0 + 128] for c0 in cols], axis=1))
        maps.append({"hT": hT, "w": w, "w1": f32(inp["cmp_k_w1"] if sc == 0 else inp["cmp_v_w1"]),
                     "w2": f32(inp["cmp_k_w2"] if sc == 0 else inp["cmp_v_w2"]),
                     "posT": np.ascontiguousarray(f32(inp["cmp_pos_k"] if sc == 0 else inp["cmp_pos_v"]).T)})
    r = _launch(nc_skv, maps)
    kcmpT = [np.asarray(r[g]["cmpT"]) for g in range(4)]
    vcmp = [_pmaj(np.ascontiguousarray(np.asarray(r[4 + g]["cmpT"]).T)) if NCP % 128 == 0 else None for g in range(4)]
    raw = {}
    for c in range(NCORES):
        for k, e in enumerate((2 * c, 2 * c + 1)):
            raw[(2 + e // 4, e % 4)] = np.asarray(r[c]["raw"])[k]
    kselT = [np.ascontiguousarray(raw[(2, g)]) for g in range(4)]
    vsel = [_pmaj(np.ascontiguousarray(raw[(3, g)].T)) for g in range(4)]
    kwinT = [np.ascontiguousarray(raw[(4, g)]) for g in range(4)]
    vwin = [_pmaj(np.ascontiguousarray(raw[(5, g)].T)) for g in range(4)]

    nc_nsa = _prog(("nsa", S), lambda: build_nsa(S))
    for b in range(2):
        layer = 2 + b
        maps = []
        for c in range(NCORES):
            g, half = c // 2, c % 2
            ho = [2 * half, 2 * half + 1, 2 * (1 - half), 2 * (1 - half) + 1]
            heads = [4 * g + r_ for r_ in ho]
            w = np.ascontiguousarray(np.concatenate(
                [nsa_w_in[b][:, hd * 128:(hd + 1) * 128] for hd in heads] +
                [nsa_w_in[b][:, 2048 + 3 * hd:2048 + 3 * hd + 3] for hd in heads], axis=1))
            cs = nsa_consts(np.ascontiguousarray(rel_bias[:, heads]))
            maps.append(dict(hT=hT, w=w, kcmpT=kcmpT[g], vcmp=vcmp[g], kselT=kselT[g], vsel=vsel[g], kwinT=kwinT[g], vwin=vwin[g], **cs))
        r = _launch(nc_nsa, maps)
        A = np.concatenate([np.asarray(r[c]["o"]) for c in range(NCORES)], axis=1)
        h, hT = post(A, h, nsa_w_o[b], layer)
    return h.reshape(1, S, D).astype(np.float32)


def kernel(**inputs):
    return forward(inputs, 16384)
```

```python
import math
import numpy as np
import ml_dtypes
from contextlib import ExitStack
import concourse.bass as bass
import concourse.mybir as mybir
from concourse.bass_utils import run_bass_kernel_spmd

F32 = mybir.dt.float32
BF16 = mybir.dt.bfloat16
AF = mybir.ActivationFunctionType
ALU = mybir.AluOpType
NPBF = ml_dtypes.bfloat16


class Chan:
    def __init__(self, sem):
        self.sem = sem
        self.count = 0


class Sched:
    ENGS = ("pe", "act", "dve", "pool", "sp")

    def __init__(self, nc, es):
        self.nc, self.es = nc, es
        self.q = {e: [] for e in self.ENGS}
        self.nsem = 0
        self.main = {}

    def chan(self, name="c"):
        sem = self.es.enter_context(self.nc.semaphore(f"{name}{self.nsem}"))
        self.nsem += 1
        return Chan(sem)

    def op(self, eng, fn, deps=(), chan=None, inc=1):
        waits = [(d[0], d[1]) for d in deps if d is not None]
        mc = self.main.get(eng)
        if mc is not None and mc.count > 0:
            waits.append((mc, mc.count))
        t = None
        if chan is not None:
            chan.count += inc
            t = (chan, chan.count)
        self.q[eng].append((waits, fn, (chan, inc) if chan is not None else None))
        return t

    def emit(self):
        nc = self.nc
        q = self.q
        with nc.Block() as block:
            def replay(name):
                def f(e):
                    seen = {}
                    for waits, fn, inc in q[name]:
                        for ch, val in waits:
                            if seen.get(id(ch), 0) >= val:
                                continue
                            seen[id(ch)] = val
                            e.wait_ge(ch.sem, val)
                        ins = fn(e)
                        if inc is not None:
                            ins.then_inc(inc[0].sem, inc[1])
                return f
            block.tensor(replay("pe"))
            block.scalar(replay("act"))
            block.vector(replay("dve"))
            block.gpsimd(replay("pool"))
            block.sync(replay("sp"))


class Ring:
    def __init__(self, n):
        self.n = n
        self.i = 0
        self.free = [[] for _ in range(n)]

    def next(self):
        s = self.i % self.n
        self.i += 1
        deps = self.free[s]
        self.free[s] = []
        return s, deps

    def release(self, s, ticket):
        self.free[s].append(ticket)


ALPHA = 8.0 ** 0.25
LN_EPS = 1e-5
D = 2048
KC = 16
TP = 1024


def build_post(T, DFF, emit_hT=True):
    assert T % TP == 0 and DFF % 512 == 0
    NP, NTT, NG = T // TP, TP // 128, DFF // 512
    nc = bass.Bass("TRN2", target_bir_lowering=False)
    aT_d = nc.dram_tensor("aT", [D, T], BF16, kind="ExternalInput").ap()
    h_d = nc.dram_tensor("h", [T, D], F32, kind="ExternalInput").ap()
    wo_d = nc.dram_tensor("wo", [D, D], F32, kind="ExternalInput").ap()
    w1_d = nc.dram_tensor("w1", [D, DFF], F32, kind="ExternalInput").ap()
    w2_d = nc.dram_tensor("w2", [DFF, D], F32, kind="ExternalInput").ap()
    lnp_d = nc.dram_tensor("lnp", [4, D], F32, kind="ExternalInput").ap()
    id_d = nc.dram_tensor("ident", [128, 128], BF16, kind="ExternalInput").ap()
    hout_d = nc.dram_tensor("hout", [T, D], F32, kind="ExternalOutput").ap()
    if emit_hT:
        hTout_d = nc.dram_tensor("hTout", [D, T], BF16, kind="ExternalOutput").ap()

    with ExitStack() as es:
        sb = lambda name, shape, dt: es.enter_context(nc.sbuf_tensor(name, shape, dt))
        yacc = sb("yacc", [128, NTT, D], F32)
        h1T = sb("h1T", [128, KC, TP], BF16)
        wA = sb("wA", [128, 2, KC, 512], BF16)
        wBf = sb("wBf", [128, 2 * 4 * D], BF16)
        uT = sb("uT", [128, 2, 4, TP], BF16)
        lnp = sb("lnpsb", [128, 2, D], F32)
        tmp = sb("tmp", [128, 2, 512], F32)
        xb = sb("xb", [128, D], BF16)
        st = sb("st", [128, 4, 6], F32)
        mv = sb("mv", [128, 2], F32)
        rs = sb("rs", [128, 1], F32)
        nmr = sb("nmr", [128, 1], F32)
        ident = sb("identsb", [128, 128], BF16)
        ps = es.enter_context(nc.psum_tensor("ps", [128, 6, 512], F32))
        psT = es.enter_context(nc.psum_tensor("psT", [128, 2, 1024], BF16))
        wB = wBf[:, :].rearrange("p (s c n) -> p s c n", s=2, c=4)
        aTv = wBf[:, :].rearrange("p (k t) -> p k t", k=KC)

        S = Sched(nc, es)
        c_pe, c_act, c_dve, c_pool = S.chan("pe"), S.chan("act"), S.chan("dve"), S.chan("pool")
        S.main = {"act": c_act, "dve": c_dve, "pool": c_pool}
        c_h, c_aT, c_ln, c_st, c_id = S.chan("ldh"), S.chan("ldaT"), S.chan("ldln"), S.chan("st"), S.chan("ldid")
        c_wA = [S.chan("wA0"), S.chan("wA1")]
        c_wB = [S.chan("wB0"), S.chan("wB1")]
        bank = Ring(6)
        tbank = Ring(2)
        wAr, wBr, uTr, tmpr = Ring(2), Ring(2), Ring(2), Ring(2)

        t_id = S.op("sp", lambda e: e.dma_start(out=ident[:, :], in_=id_d[:, :]), chan=c_id, inc=16)

        def layer_norm(tt, deps):
            t = None
            for c in range(4):
                t = S.op("dve", lambda e, c=c: e.bn_stats(out=st[:, c, :], in_=yacc[:, tt, c * 512:(c + 1) * 512]),
                         deps=deps if c == 0 else [t], chan=c_dve)
            t = S.op("dve", lambda e: e.bn_aggr(out=mv[:, :], in_=st[:, :, :]), deps=[t], chan=c_dve)
            t = S.op("act", lambda e: e.activation(out=rs[:, :], in_=mv[:, 1:2], func=AF.Sqrt, bias=LN_EPS, scale=1.0),
                     deps=[t], chan=c_act)
            t = S.op("dve", lambda e: e.reciprocal(out=rs[:, :], in_=rs[:, :]), deps=[t], chan=c_dve)
            t = S.op("dve", lambda e: e.scalar_tensor_tensor(out=nmr[:, :], in0=mv[:, 0:1], scalar=-1.0, in1=rs[:, :],
                                                             op0=ALU.mult, op1=ALU.mult), deps=[t], chan=c_dve)
            t = S.op("act", lambda e: e.activation(out=yacc[:, tt, :], in_=yacc[:, tt, :], func=AF.Identity,
                                                   bias=nmr[:, 0:1], scale=rs[:, 0:1]), deps=[t], chan=c_act)
            t = S.op("dve", lambda e: e.tensor_tensor(out=yacc[:, tt, :], in0=yacc[:, tt, :], in1=lnp[:, 0, :], op=ALU.mult),
                     deps=[t], chan=c_dve)
            t = S.op("pool", lambda e: e.tensor_tensor(out=yacc[:, tt, :], in0=yacc[:, tt, :], in1=lnp[:, 1, :], op=ALU.add),
                     deps=[t], chan=c_pool)
            return t

        def to_T(tt, dep_x, extra_deps):
            t_xb = S.op("act", lambda e: e.activation(out=xb[:, :], in_=yacc[:, tt, :], func=AF.Copy),
                        deps=[dep_x] + list(xb_free), chan=c_act)
            last = None
            tps = []
            for q in range(4):
                s, fdeps = tbank.next()
                tp = None
                for j in range(4):
                    kc = q * 4 + j
                    tp = S.op("pe", lambda e, kc=kc, s=s, j=j: e.transpose(
                        out=psT[:, s, j * 128:(j + 1) * 128], in_=xb[:, kc * 128:(kc + 1) * 128], identity=ident[:, :]),
                        deps=([t_xb, t_id] + fdeps + list(extra_deps)) if j == 0 else [], chan=c_pe if j == 3 else None)
                tps.append(tp)
                last = S.op("dve", lambda e, q=q, s=s: e.tensor_copy(
                    out=h1T[:, q * 4:(q + 1) * 4, tt * 128:(tt + 1) * 128],
                    in_=psT[:, s, 0:512].rearrange("p (k t) -> p k t", k=4)), deps=[tp], chan=c_dve)
                tbank.release(s, last)
            xb_free[:] = [tps[-1]]
            return t_xb, last

        xb_free = []
        prev_pass_pe = None
        prev_pass_st = []
        prev_ln_done = None
        for p in range(NP):
            tok0 = p * TP
            t_h = None
            for tt in range(NTT):
                t_h = S.op("sp", lambda e, tt=tt, tok0=tok0: e.dma_start(out=yacc[:, tt, :], in_=h_d[tok0 + tt * 128: tok0 + (tt + 1) * 128, :]),
                           deps=prev_pass_st if tt == 0 else [], chan=c_h, inc=16)
            t_aT = None
            for q in range(4):
                t_aT = S.op("sp", lambda e, q=q, tok0=tok0: e.dma_start(
                    out=aTv[:, q * 4:(q + 1) * 4, :],
                    in_=aT_d[q * 512:(q + 1) * 512, tok0:tok0 + TP].rearrange("(k p) t -> p k t", p=128)),
                    deps=[prev_pass_pe] if q == 0 else [], chan=c_aT, inc=16)
            t_ln = None
            for i in range(2):
                t_ln = S.op("sp", lambda e, i=i: e.dma_start(out=lnp[:, i, :], in_=lnp_d[i:i + 1, :].broadcast_to([128, D])),
                            deps=[prev_ln_done] if i == 0 else [], chan=c_ln, inc=16)
            last_res = [None] * NTT
            for nb in range(4):
                s, fdeps = wAr.next()
                t_w = S.op("pool", lambda e, s=s, nb=nb: e.dma_start(
                    out=wA[:, s, :, :], in_=wo_d[:, nb * 512:(nb + 1) * 512].rearrange("(k p) n -> p k n", p=128)),
                    deps=fdeps, chan=c_wA[s], inc=16)
                for tt in range(NTT):
                    b, bdeps = bank.next()
                    tpe = None
                    for kc in range(KC):
                        tpe = S.op("pe", lambda e, kc=kc, tt=tt, s=s, b=b: e.matmul(
                            ps[:, b, :], lhsT=aTv[:, kc, tt * 128:(tt + 1) * 128], rhs=wA[:, s, kc, :],
                            start=(kc == 0), stop=(kc == KC - 1)),
                            deps=([t_w, t_aT] + bdeps) if kc == 0 else [], chan=c_pe if kc == KC - 1 else None)
                    te = S.op("dve", lambda e, tt=tt, nb=nb, b=b: e.scalar_tensor_tensor(
                        out=yacc[:, tt, nb * 512:(nb + 1) * 512], in0=yacc[:, tt, nb * 512:(nb + 1) * 512], scalar=ALPHA,
                        in1=ps[:, b, :], op0=ALU.mult, op1=ALU.add), deps=[tpe, t_h], chan=c_dve)
                    bank.release(b, te)
                    last_res[tt] = te
                wAr.release(s, tpe)
            last_wo_pe = tpe
            t_sc = None
            hT_ready = None
            for tt in range(NTT):
                t = layer_norm(tt, [last_res[tt], t_ln])
                t_xb, hT_ready = to_T(tt, t, prev_pass_st if tt == 0 else [])
                t_sc = S.op("act", lambda e, tt=tt: e.activation(out=yacc[:, tt, :], in_=yacc[:, tt, :], func=AF.Copy, scale=ALPHA),
                            deps=[t_xb], chan=c_act)
                ln1_last = t
            t_ln2 = None
            for i in range(2):
                t_ln2 = S.op("sp", lambda e, i=i: e.dma_start(out=lnp[:, i, :], in_=lnp_d[2 + i:3 + i, :].broadcast_to([128, D])),
                             deps=[ln1_last] if i == 0 else [], chan=c_ln, inc=16)
            last_acc = [[t_sc] * 4 for _ in range(NTT)]
            for g in range(NG):
                sa, fdeps = wAr.next()
                t_w1 = S.op("pool", lambda e, sa=sa, g=g: e.dma_start(
                    out=wA[:, sa, :, :], in_=w1_d[:, g * 512:(g + 1) * 512].rearrange("(k p) n -> p k n", p=128)),
                    deps=fdeps, chan=c_wA[sa], inc=16)
                sb_, fdeps = wBr.next()
                fdeps = fdeps + [last_wo_pe]
                t_w2 = None
                for c in range(4):
                    t_w2 = S.op("pool", lambda e, sb_=sb_, g=g, c=c: e.dma_start(
                        out=wB[:, sb_, c, :], in_=w2_d[g * 512 + c * 128: g * 512 + (c + 1) * 128, :], max_dma_last_dim=4096),
                        deps=fdeps if c == 0 else [], chan=c_wB[sb_], inc=16)
                su, udeps = uTr.next()
                t_u = []
                for c in range(4):
                    for blk in range(2):
                        b, bdeps = bank.next()
                        tpe = None
                        for kc in range(KC):
                            tpe = S.op("pe", lambda e, kc=kc, c=c, blk=blk, sa=sa, b=b: e.matmul(
                                ps[:, b, :], lhsT=wA[:, sa, kc, c * 128:(c + 1) * 128], rhs=h1T[:, kc, blk * 512:(blk + 1) * 512],
                                start=(kc == 0), stop=(kc == KC - 1)),
                                deps=([t_w1, hT_ready] + bdeps) if kc == 0 else [], chan=c_pe if kc == KC - 1 else None)
                        ts_, tdeps = tmpr.next()
                        ta = S.op("act", lambda e, ts_=ts_, b=b: e.activation(out=tmp[:, ts_, :], in_=ps[:, b, :], func=AF.Relu),
                                  deps=[tpe] + tdeps, chan=c_act)
                        bank.release(b, ta)
                        td = S.op("dve", lambda e, ts_=ts_, su=su, c=c, blk=blk: e.tensor_tensor(
                            out=uT[:, su, c, blk * 512:(blk + 1) * 512], in0=tmp[:, ts_, :], in1=tmp[:, ts_, :], op=ALU.mult),
                            deps=[ta] + (udeps if (c == 0 and blk == 0) else []), chan=c_dve)
                        tmpr.release(ts_, td)
                        t_u.append(td)
                wAr.release(sa, tpe)
                for tt in range(NTT):
                    for nb in range(4):
                        b, bdeps = bank.next()
                        tpe = None
                        for c in range(4):
                            tpe = S.op("pe", lambda e, c=c, tt=tt, nb=nb, su=su, sb_=sb_, b=b: e.matmul(
                                ps[:, b, :], lhsT=uT[:, su, c, tt * 128:(tt + 1) * 128], rhs=wB[:, sb_, c, nb * 512:(nb + 1) * 512],
                                start=(c == 0), stop=(c == 3)),
                                deps=([t_w2] + t_u + bdeps) if c == 0 else [], chan=c_pe if c == 3 else None)
                        te = S.op("dve", lambda e, tt=tt, nb=nb, b=b: e.tensor_tensor(
                            out=yacc[:, tt, nb * 512:(nb + 1) * 512], in0=ps[:, b, :], in1=yacc[:, tt, nb * 512:(nb + 1) * 512],
                            op=ALU.add), deps=[tpe, last_acc[tt][nb]], chan=c_dve)
                        bank.release(b, te)
                        last_acc[tt][nb] = te
                wBr.release(sb_, tpe)
                uTr.release(su, tpe)
                prev_pass_pe = tpe
            prev_pass_st = []
            hT2 = None
            for tt in range(NTT):
                t = layer_norm(tt, last_acc[tt] + [t_ln2])
                prev_ln_done = t
                t_st = S.op("sp", lambda e, tt=tt, tok0=tok0: e.dma_start(out=hout_d[tok0 + tt * 128: tok0 + (tt + 1) * 128, :], in_=yacc[:, tt, :]),
                            deps=[t], chan=c_st, inc=16)
                prev_pass_st = [t_st]
                if emit_hT:
                    t_xb, hT2 = to_T(tt, t, [prev_pass_pe] if tt == 0 else [])
            if emit_hT:
                for q in range(4):
                    t_st = S.op("sp", lambda e, q=q, tok0=tok0: e.dma_start(
                        out=hTout_d[q * 512:(q + 1) * 512, tok0:tok0 + TP].rearrange("(k p) t -> p k t", p=128),
                        in_=h1T[:, q * 4:(q + 1) * 4, :]), deps=[hT2], chan=c_st, inc=16)
                    prev_pass_st = [t_st]
        S.op("sp", lambda e: e.nop(), deps=prev_pass_st)
        S.emit()
    return nc


D = 2048
KC = 16
DH = 128
NEG = -30000.0


def fox_consts():
    ident = np.eye(128, dtype=np.float32)
    U = np.triu(np.ones((128, 128), np.float32))
    ones = np.ones((128, 128), np.float32)
    s = np.arange(128)[:, None]
    t = np.arange(128)[None, :]
    maskneg = np.where(s <= t, 0.0, NEG).astype(np.float32)
    return np.ascontiguousarray(np.stack([ident, U, ones, maskneg], axis=1))


def build_fox(S, NH=2):
    NT = S // 128
    NB = S // 512
    nc = bass.Bass("TRN2", target_bir_lowering=False)
    hT_d = nc.dram_tensor("hT", [D, S], BF16, kind="ExternalInput").ap()
    w_d = nc.dram_tensor("w", [NH, D, 385], F32, kind="ExternalInput").ap()
    bf_d = nc.dram_tensor("bf", [1, NH], F32, kind="ExternalInput").ap()
    cst_d = nc.dram_tensor("cst", [128, 4, 128], F32, kind="ExternalInput").ap()
    o_d = nc.dram_tensor("o", [S, NH * DH], BF16, kind="ExternalOutput").ap()

    with ExitStack() as es:
        sb = lambda name, shape, dt: es.enter_context(nc.sbuf_tensor(name, shape, dt))
        import os
        if os.environ.get("FOX_PAD"):
            pad_ = sb("padd", [128, int(os.environ["FOX_PAD"])], BF16)
        KT = sb("KT", [128, S], BF16)
        QT = sb("QT", [128, S], BF16)
        V1 = sb("V1", [128, NT, 129], BF16)
        w = sb("wsb", [128, KC, 385], BF16)
        hb = sb("hb", [128, 2, KC, 512], BF16)
        P = sb("P", [128, 3, 512], BF16)
        L = sb("L", [128, 2, 512], F32)
        ncrow = sb("ncrow", [128, 2, 512], F32)
        import os
        CRBF = bool(os.environ.get("FOX_BF16CR"))
        Dg = sb("Dg", [128, 2, 128], BF16 if CRBF else F32)
        onesbf = sb("onesbf", [128, 128], BF16)
        cst = sb("cstsb", [128, 4, 128], F32)
        lf = sb("lf", [128, NT], F32)
        e1 = sb("e1", [128, NT], F32)
        ll = sb("ll", [128, NT], F32)
        Tsb = sb("Tsb", [128, NT], F32)
        incl = sb("incl", [128, NT], F32)
        Ex = sb("Ex", [128, NT], F32)
        Cc = sb("Cc", [128, NT], F32)
        onesrow = sb("onesrow", [128, NT], F32)
        biasb = sb("biasb", [128, 2, NT], F32)
        fcol = sb("fcol", [128, 2, 4], F32)
        bfb = sb("bfb", [128, NH], F32)
        negbf = sb("negbf", [128, NH], F32)
        odsb = sb("odsb", [128, 2, 129], F32)
        osum = sb("osum", [128, 2, 129], F32)
        rec = sb("rec", [128, 2, 1], F32)
        obuf = sb("obuf", [128, 2, 4, 128], BF16)
        pss = es.enter_context(nc.psum_tensor("pss", [128, 3, 512], F32))
        psm = es.enter_context(nc.psum_tensor("psm", [128, 512], F32))
        po = es.enter_context(nc.psum_tensor("po", [128, 4, 512], F32))
        ident, U, ones, maskneg = cst[:, 0, :], cst[:, 1, :], cst[:, 2, :], cst[:, 3, :]

        S_ = Sched(nc, es)
        c_pe, c_act, c_dve, c_pool = S_.chan("pe"), S_.chan("act"), S_.chan("dve"), S_.chan("pool")
        S_.main = {"act": c_act, "dve": c_dve, "pool": c_pool}
        c_cst, c_w = S_.chan("ldc"), S_.chan("ldw")
        c_st = [S_.chan("st0"), S_.chan("st1")]
        c_hb = [S_.chan("hb0"), S_.chan("hb1")]
        engs = {"pe": c_pe, "act": c_act, "dve": c_dve, "pool": c_pool}

        def barrier():
            deps = [(c, c.count) for c in (c_pe, c_act, c_dve, c_pool, c_st[0], c_st[1]) if c.count > 0]
            for eng in ("pe", "act", "dve", "pool", "sp"):
                S_.op(eng, lambda e: e.nop(), deps=deps)

        t_c = S_.op("sp", lambda e: e.dma_start(out=cst[:, :, :], in_=cst_d[:, :, :]), chan=c_cst, inc=16)
        t_c = S_.op("sp", lambda e: e.dma_start(out=bfb[:, :], in_=bf_d[0:1, :].broadcast_to([128, NH])), chan=c_cst, inc=16)
        S_.op("dve", lambda e: e.tensor_scalar(out=negbf[:, :], in0=bfb[:, :], scalar1=-1.0, scalar2=None, op0=ALU.mult),
              deps=[t_c], chan=c_dve)
        S_.op("pool", lambda e: e.memset(V1[:, :, 128:129], 1.0), chan=c_pool)
        S_.op("pool", lambda e: e.memset(onesrow[:, :], 1.0), chan=c_pool)
        S_.op("pool", lambda e: e.memset(onesbf[:, :], 1.0), chan=c_pool)
        t_ob1 = (c_pool, c_pool.count)

        sring = Ring(3)
        hbr = Ring(2)
        for hd in range(NH):
            if hd > 0:
                barrier()
            t_w = S_.op("pool", lambda e, hd=hd: e.dma_start(out=w[:, :, :], in_=w_d[hd].rearrange("(k p) n -> p k n", p=128)),
                        chan=c_w, inc=16)
            for blk in range(NB):
                hs, hdeps = hbr.next()
                t_h = None
                for q in range(4):
                    t_h = S_.op("sp", lambda e, hs=hs, blk=blk, q=q: e.dma_start(
                        out=hb[:, hs, q * 4:(q + 1) * 4, :],
                        in_=hT_d[q * 512:(q + 1) * 512, blk * 512:(blk + 1) * 512].rearrange("(k p) t -> p k t", p=128)),
                        deps=hdeps if q == 0 else [], chan=c_hb[hs], inc=16)
                s, bdeps = sring.next()
                for kc in range(KC):
                    tpe = S_.op("pe", lambda e, kc=kc, s=s, hs=hs: e.matmul(
                        pss[:, s, :], lhsT=w[:, kc, 0:128], rhs=hb[:, hs, kc, :], start=(kc == 0), stop=(kc == KC - 1)),
                        deps=([t_w, t_h] + bdeps) if kc == 0 else [], chan=c_pe if kc == KC - 1 else None)
                te = S_.op("act", lambda e, s=s, blk=blk: e.activation(out=QT[:, blk * 512:(blk + 1) * 512], in_=pss[:, s, :],
                                                                      func=AF.Copy, scale=DH ** -0.5), deps=[tpe], chan=c_act)
                sring.release(s, te)
                s, bdeps = sring.next()
                for kc in range(KC):
                    tpe = S_.op("pe", lambda e, kc=kc, s=s, hs=hs: e.matmul(
                        pss[:, s, :], lhsT=w[:, kc, 128:256], rhs=hb[:, hs, kc, :], start=(kc == 0), stop=(kc == KC - 1)),
                        deps=bdeps if kc == 0 else [], chan=c_pe if kc == KC - 1 else None)
                te = S_.op("dve", lambda e, s=s, blk=blk: e.tensor_copy(out=KT[:, blk * 512:(blk + 1) * 512], in_=pss[:, s, :]),
                           deps=[tpe], chan=c_dve)
                sring.release(s, te)
                for sub in range(4):
                    tile = blk * 4 + sub
                    s, bdeps = sring.next()
                    for kc in range(KC):
                        tpe = S_.op("pe", lambda e, kc=kc, s=s, hs=hs, sub=sub: e.matmul(
                            pss[:, s, 0:129], lhsT=hb[:, hs, kc, sub * 128:(sub + 1) * 128], rhs=w[:, kc, 256:385],
                            start=(kc == 0), stop=(kc == KC - 1)),
                            deps=bdeps if kc == 0 else [], chan=c_pe if kc == KC - 1 else None)
                    ta = S_.op("act", lambda e, s=s, tile=tile: e.activation(out=V1[:, tile, 0:128], in_=pss[:, s, 0:128], func=AF.Copy),
                               deps=[tpe], chan=c_act)
                    td = S_.op("act", lambda e, s=s, tile=tile: e.activation(out=lf[:, tile:tile + 1], in_=pss[:, s, 128:129], func=AF.Copy),
                               deps=[tpe], chan=c_act)
                    sring.release(s, ta)
                    sring.release(s, td)
                hbr.release(hs, tpe)
            t_projA, t_projD = (c_act, c_act.count), (c_dve, c_dve.count)
            t = S_.op("act", lambda e, hd=hd: e.activation(out=e1[:, :], in_=lf[:, :], func=AF.Exp, bias=negbf[:, hd:hd + 1], scale=-1.0),
                      deps=[t_projD, t_projA], chan=c_act)
            t_l = S_.op("act", lambda e: e.activation(out=ll[:, :], in_=e1[:, :], func=AF.Ln, bias=1.0, scale=1.0), deps=[t], chan=c_act)
            s, bdeps = sring.next()
            t_W = S_.op("pe", lambda e: e.matmul(psm[:, 0:NT], lhsT=U, rhs=ll[:, :], start=True, stop=True), deps=[t_l, t_c], chan=c_pe)
            t_T = S_.op("pe", lambda e, s=s: e.matmul(pss[:, s, 0:NT], lhsT=ones, rhs=ll[:, :], start=True, stop=True), deps=bdeps, chan=c_pe)
            t = S_.op("dve", lambda e, s=s: e.tensor_copy(out=Tsb[:, :], in_=pss[:, s, 0:NT]), deps=[t_T], chan=c_dve)
            sring.release(s, t)
            t = S_.op("dve", lambda e: e.tensor_tensor_scan(out=incl[:, :], data0=onesrow[:, :], data1=Tsb[:, :], initial=0.0,
                                                            op0=ALU.mult, op1=ALU.add), deps=[t, (c_pool, c_pool.count)], chan=c_dve)
            t = S_.op("dve", lambda e: e.tensor_tensor(out=Ex[:, :], in0=incl[:, :], in1=Tsb[:, :], op=ALU.subtract), deps=[t], chan=c_dve)
            t_C = S_.op("dve", lambda e: e.tensor_tensor(out=Cc[:, :], in0=psm[:, 0:NT], in1=Ex[:, :], op=ALU.add), deps=[t, t_W], chan=c_dve)
            pring, lring, cring, bring, oring = Ring(3), Ring(2), Ring(2), Ring(2), Ring(2)
            po_free = []
            psm_free = [t_C]
            import os
            for b in range(int(os.environ.get('FOX_NB', NB))):
                nk = 4 * b + 4
                bs, bdeps_ = bring.next()
                t_bias = S_.op("dve", lambda e, bs=bs, b=b, nk=nk: e.tensor_scalar(
                    out=biasb[:, bs, 0:nk], in0=Cc[:, 0:nk], scalar1=Ex[:, 4 * b:4 * b + 1], scalar2=None, op0=ALU.subtract),
                    deps=[t_C] + bdeps_, chan=c_dve)
                t_f = S_.op("act", lambda e, bs=bs, b=b: e.activation(out=fcol[:, bs, :], in_=biasb[:, bs, 4 * b:4 * b + 4],
                                                                      func=AF.Exp, scale=-1.0), deps=[t_bias], chan=c_act)
                cs, cdeps = cring.next()
                t_cr = None
                for qq in range(4):
                    ds_ = qq % 2
                    t_dg = S_.op("dve", lambda e, ds_=ds_, bs=bs, b=b, qq=qq: e.tensor_scalar(
                        out=Dg[:, ds_, :], in0=ident, scalar1=biasb[:, bs, 4 * b + qq:4 * b + qq + 1], scalar2=-1.0,
                        op0=ALU.mult, op1=ALU.mult), deps=[t_bias, t_cr] if t_cr is not None else [t_bias, t_c], chan=c_dve)
                    t_cr = S_.op("pe", lambda e, ds_=ds_, qq=qq: e.matmul(psm[:, qq * 128:(qq + 1) * 128], lhsT=(onesbf[:, :] if CRBF else ones), rhs=Dg[:, ds_, :],
                                                                          start=True, stop=True),
                                 deps=[t_dg, t_ob1] + (psm_free if qq == 0 else []), chan=c_pe)
                t_ncr = S_.op("act", lambda e, cs=cs: e.activation(out=ncrow[:, cs, :], in_=psm[:, :], func=AF.Copy),
                              deps=[t_cr] + cdeps, chan=c_act)
                psm_free = [t_ncr]
                tiles = [("off", j) for j in range(4 * b)] + [("diag", kk) for kk in range(4)]
                pend = None
                first_pv = True
                last_pv = None
                diag_exp = []

                def emit_pv(info):
                    nonlocal first_pv, last_pv
                    kind, idx, pslot, t_exp = info
                    tpv = None
                    if kind == "off":
                        j = idx
                        for qq in range(4):
                            st_flag = (j == 0)
                            sp_flag = (j == 4 * b - 1)
                            tpv = S_.op("pe", lambda e, qq=qq, pslot=pslot, j=j, st_flag=st_flag, sp_flag=sp_flag: e.matmul(
                                po[:, qq, 0:129], lhsT=P[:, pslot, qq * 128:(qq + 1) * 128], rhs=V1[:, j, :],
                                start=st_flag, stop=sp_flag, skip_group_check=True),
                                deps=([t_exp] + (po_free if first_pv else [])) if qq == 0 else [], chan=c_pe if qq == 3 else None)
                            first_pv = False
                    else:
                        kk = idx
                        for qq in range(kk, 4):
                            st_flag = (b == 0 and kk == 0)
                            tpv = S_.op("pe", lambda e, qq=qq, pslot=pslot, kk=kk, st_flag=st_flag, b=b: e.matmul(
                                po[:, qq, 256:385], lhsT=P[:, pslot, (qq - kk) * 128:(qq - kk + 1) * 128], rhs=V1[:, 4 * b + kk, :],
                                start=st_flag, stop=(kk == qq), skip_group_check=True),
                                deps=([t_exp] + (po_free if first_pv else [])) if qq == kk else [], chan=c_pe if qq == 3 else None)
                            first_pv = False
                    pring.release(pslot, tpv)
                    last_pv = tpv

                for kind, idx in tiles:
                    s, sdeps = sring.next()
                    pslot, pdeps = pring.next()
                    if kind == "off":
                        j = idx
                        t_s = S_.op("pe", lambda e, s=s, j=j, b=b: e.matmul(
                            pss[:, s, :], lhsT=KT[:, j * 128:(j + 1) * 128], rhs=QT[:, b * 512:(b + 1) * 512], start=True, stop=True),
                            deps=sdeps + [t_projA, t_projD], chan=c_pe)
                        t_exp = S_.op("act", lambda e, s=s, pslot=pslot, bs=bs, j=j: e.activation(
                            out=P[:, pslot, :], in_=pss[:, s, :], func=AF.Exp, bias=biasb[:, bs, j:j + 1], scale=1.0),
                            deps=[t_s, t_bias] + pdeps, chan=c_act)
                        sring.release(s, t_exp)
                    else:
                        kk = idx
                        N = (4 - kk) * 128
                        t_s = S_.op("pe", lambda e, s=s, kk=kk, b=b, N=N: e.matmul(
                            pss[:, s, 0:N], lhsT=KT[:, (4 * b + kk) * 128:(4 * b + kk + 1) * 128],
                            rhs=QT[:, b * 512 + kk * 128:(b + 1) * 512], start=True, stop=True),
                            deps=sdeps + [t_projA, t_projD], chan=c_pe)
                        ls, ldeps = lring.next()
                        t_l1 = S_.op("dve", lambda e, s=s, ls=ls, cs=cs, kk=kk, N=N: e.tensor_tensor(
                            out=L[:, ls, 0:N], in0=pss[:, s, 0:N], in1=ncrow[:, cs, kk * 128:512], op=ALU.add),
                            deps=[t_s, t_ncr] + ldeps, chan=c_dve)
                        sring.release(s, t_l1)
                        import os
                        if os.environ.get("FOX_NOPOOL"):
                            t_l2 = S_.op("dve", lambda e, ls=ls: e.tensor_tensor(out=L[:, ls, 0:128], in0=L[:, ls, 0:128], in1=maskneg, op=ALU.add),
                                         deps=[t_l1, t_c], chan=c_dve)
                        else:
                            t_l2 = S_.op("pool", lambda e, ls=ls: e.tensor_tensor(out=L[:, ls, 0:128], in0=L[:, ls, 0:128], in1=maskneg, op=ALU.add),
                                         deps=[t_l1, t_c], chan=c_pool)
                        t_exp = S_.op("act", lambda e, ls=ls, pslot=pslot, bs=bs, kk=kk, b=b, N=N: e.activation(
                            out=P[:, pslot, 0:N], in_=L[:, ls, 0:N], func=AF.Exp, bias=biasb[:, bs, 4 * b + kk:4 * b + kk + 1], scale=1.0),
                            deps=[t_l2] + pdeps, chan=c_act)
                        lring.release(ls, t_exp)
                        diag_exp.append(t_exp)
                    if pend is not None:
                        emit_pv(pend)
                    pend = (kind, idx, pslot, t_exp)
                emit_pv(pend)
                cring.release(cs, diag_exp[-1])
                bring.release(bs, diag_exp[-1])
                os_, odeps = oring.next()
                t_o = None
                po_free = []
                for qq in range(4):
                    k2 = qq % 2
                    t1 = S_.op("dve", lambda e, qq=qq, k2=k2: e.tensor_copy(out=odsb[:, k2, :], in_=po[:, qq, 256:385]),
                               deps=[last_pv] + ([t_o] if t_o is not None else []), chan=c_dve)
                    if b > 0:
                        t2 = S_.op("dve", lambda e, qq=qq, k2=k2, bs=bs: e.scalar_tensor_tensor(
                            out=osum[:, k2, :], in0=po[:, qq, 0:129], scalar=fcol[:, bs, qq:qq + 1], in1=odsb[:, k2, :],
                            op0=ALU.mult, op1=ALU.add), deps=[t1, t_f], chan=c_dve)
                    else:
                        t2 = S_.op("dve", lambda e, k2=k2: e.tensor_copy(out=osum[:, k2, :], in_=odsb[:, k2, :]), deps=[t1], chan=c_dve)
                    t3 = S_.op("dve", lambda e, k2=k2: e.reciprocal(out=rec[:, k2, :], in_=osum[:, k2, 128:129]), deps=[t2], chan=c_dve)
                    t_o = S_.op("act", lambda e, qq=qq, k2=k2, os_=os_: e.activation(
                        out=obuf[:, os_, qq, :], in_=osum[:, k2, 0:128], func=AF.Identity, scale=rec[:, k2, 0:1]),
                        deps=[t3] + (odeps if qq == 0 else []), chan=c_act)
                    po_free += [t1, t2]
                t_st = S_.op("sp", lambda e, os_=os_, b=b, hd=hd: e.dma_start(
                    out=o_d[b * 512:(b + 1) * 512, hd * DH:(hd + 1) * DH].rearrange("(q p) d -> p q d", p=128),
                    in_=obuf[:, os_, :, :]), deps=[t_o], chan=c_st[os_], inc=16)
                oring.release(os_, t_st)
        S_.op("sp", lambda e: e.nop(), deps=[(c, c.count) for c in c_st if c.count > 0])
        S_.emit()
    return nc


D = 2048
KC = 16


def build_skv(S):
    NB = S // 512
    NCMP = S // 16 - 1
    nc = bass.Bass("TRN2", target_bir_lowering=False)
    hT_d = nc.dram_tensor("hT", [D, S], BF16, kind="ExternalInput").ap()
    w_d = nc.dram_tensor("w", [D, 384], F32, kind="ExternalInput").ap()
    w1_d = nc.dram_tensor("w1", [4096, 256], F32, kind="ExternalInput").ap()
    w2_d = nc.dram_tensor("w2", [256, 128], F32, kind="ExternalInput").ap()
    posT_d = nc.dram_tensor("posT", [128, 32], F32, kind="ExternalInput").ap()
    cmpT_d = nc.dram_tensor("cmpT", [128, S // 16], BF16, kind="ExternalOutput").ap()
    raw_d = nc.dram_tensor("raw", [2, 128, S], BF16, kind="ExternalOutput").ap()

    with ExitStack() as es:
        sb = lambda name, shape, dt: es.enter_context(nc.sbuf_tensor(name, shape, dt))
        rawT = sb("rawT", [128, 3, S], BF16)
        w = sb("wsb", [128, KC, 384], BF16)
        w1 = sb("w1sb", [128, 32, 256], BF16)
        w2 = sb("w2sb", [128, 2, 128], BF16)
        posT = sb("posTsb", [128, 32], BF16)
        hb = sb("hb", [128, 2, KC, 512], BF16)
        pbias = sb("pbias", [128, 2], F32)
        xx = sb("xx", [128, 512], F32)
        x2 = sb("x2", [128, 512], F32)
        th = sb("th", [128, 512], F32)
        hidT = sb("hidT", [128, 2, 1024], BF16)
        cmpo = sb("cmpo", [128, S // 16], BF16)
        ps = es.enter_context(nc.psum_tensor("ps", [128, 4, 512], F32))
        psb = es.enter_context(nc.psum_tensor("psb", [128, 2], F32))

        S_ = Sched(nc, es)
        c_pe, c_act, c_dve, c_pool = S_.chan("pe"), S_.chan("act"), S_.chan("dve"), S_.chan("pool")
        S_.main = {"act": c_act, "dve": c_dve, "pool": c_pool}
        c_w, c_st = S_.chan("ldw"), S_.chan("st")
        c_hb = [S_.chan("hb0"), S_.chan("hb1")]
        ring = Ring(4)
        hbr = Ring(2)

        S_.op("pool", lambda e: e.dma_start(out=w[:, :, :], in_=w_d.rearrange("(k p) n -> p k n", p=128)), chan=c_w, inc=16)
        S_.op("pool", lambda e: e.dma_start(out=w1[:, :, :], in_=w1_d.rearrange("(j d) h -> d j h", d=128)), chan=c_w, inc=16)
        S_.op("pool", lambda e: e.dma_start(out=w2[:, :, :], in_=w2_d.rearrange("(c p) d -> p c d", p=128)), chan=c_w, inc=16)
        t_w = S_.op("pool", lambda e: e.dma_start(out=posT[:, :], in_=posT_d[:, :]), chan=c_w, inc=16)
        S_.op("pool", lambda e: e.memset(cmpo[:, :], 0.0), chan=c_pool)
        t_ms = (c_pool, c_pool.count)

        t_raw = None
        for blk in range(NB):
            hs, hdeps = hbr.next()
            t_h = None
            for q in range(4):
                t_h = S_.op("sp", lambda e, hs=hs, blk=blk, q=q: e.dma_start(
                    out=hb[:, hs, q * 4:(q + 1) * 4, :],
                    in_=hT_d[q * 512:(q + 1) * 512, blk * 512:(blk + 1) * 512].rearrange("(k p) t -> p k t", p=128)),
                    deps=hdeps if q == 0 else [], chan=c_hb[hs], inc=16)
            for cb in range(3):
                s, bdeps = ring.next()
                for kc in range(KC):
                    tpe = S_.op("pe", lambda e, kc=kc, s=s, hs=hs, cb=cb: e.matmul(
                        ps[:, s, :], lhsT=w[:, kc, cb * 128:(cb + 1) * 128], rhs=hb[:, hs, kc, :], start=(kc == 0), stop=(kc == KC - 1)),
                        deps=([t_w, t_h] + bdeps) if kc == 0 else [], chan=c_pe if kc == KC - 1 else None)
                if cb % 2 == 0:
                    te = S_.op("act", lambda e, s=s, blk=blk, cb=cb: e.activation(out=rawT[:, cb, blk * 512:(blk + 1) * 512], in_=ps[:, s, :], func=AF.Copy),
                               deps=[tpe], chan=c_act)
                else:
                    te = S_.op("dve", lambda e, s=s, blk=blk, cb=cb: e.tensor_copy(out=rawT[:, cb, blk * 512:(blk + 1) * 512], in_=ps[:, s, :]),
                               deps=[tpe], chan=c_dve)
                ring.release(s, te)
            hbr.release(hs, tpe)
        t_rawA, t_rawD = (c_act, c_act.count), (c_dve, c_dve.count)
        for k in range(2):
            S_.op("sp", lambda e, k=k: e.dma_start(out=raw_d[k], in_=rawT[:, 1 + k, :]), deps=[t_rawA, t_rawD], chan=c_st, inc=16)
        for half in range(2):
            for j in range(32):
                tpb = S_.op("pe", lambda e, j=j, half=half: e.matmul(
                    psb[:, half:half + 1], lhsT=w1[:, j, half * 128:(half + 1) * 128], rhs=posT[:, j:j + 1], start=(j == 0), stop=(j == 31)),
                    deps=[t_w] if j == 0 else [], chan=c_pe if j == 31 else None)
        t_pb = S_.op("dve", lambda e: e.tensor_copy(out=pbias[:, :], in_=psb[:, :]), deps=[tpb], chan=c_dve)
        chunks = [(0, 512), (512, NCMP - 512)] if NCMP > 512 else [(0, NCMP)]
        t_hid = None
        for (n0, nn) in chunks:
            for half in range(2):
                s, bdeps = ring.next()
                for j in range(32):
                    tpe = S_.op("pe", lambda e, j=j, half=half, s=s, n0=n0, nn=nn: e.matmul(
                        ps[:, s, 0:nn], lhsT=w1[:, j, half * 128:(half + 1) * 128],
                        rhs=rawT[:, 0, 16 * n0 + j: 16 * n0 + j + 16 * (nn - 1) + 1: 16], start=(j == 0), stop=(j == 31)),
                        deps=([t_rawA, t_rawD] + bdeps) if j == 0 else [], chan=c_pe if j == 31 else None)
                t = S_.op("act", lambda e, s=s, nn=nn, half=half: e.activation(out=xx[:, 0:nn], in_=ps[:, s, 0:nn], func=AF.Identity,
                                                                              bias=pbias[:, half:half + 1], scale=1.0),
                          deps=[tpe, t_pb] + ([t_hid] if t_hid is not None else []), chan=c_act)
                ring.release(s, t)
                t = S_.op("dve", lambda e, nn=nn: e.tensor_tensor(out=x2[:, 0:nn], in0=xx[:, 0:nn], in1=xx[:, 0:nn], op=ALU.mult), deps=[t], chan=c_dve)
                t = S_.op("dve", lambda e, nn=nn: e.tensor_scalar(out=x2[:, 0:nn], in0=x2[:, 0:nn], scalar1=0.044715, scalar2=1.0,
                                                                  op0=ALU.mult, op1=ALU.add), deps=[t], chan=c_dve)
                t = S_.op("dve", lambda e, nn=nn: e.tensor_tensor(out=x2[:, 0:nn], in0=x2[:, 0:nn], in1=xx[:, 0:nn], op=ALU.mult), deps=[t], chan=c_dve)
                t = S_.op("act", lambda e, nn=nn: e.activation(out=th[:, 0:nn], in_=x2[:, 0:nn], func=AF.Tanh, scale=0.7978845608028654),
                          deps=[t], chan=c_act)
                t = S_.op("dve", lambda e, nn=nn: e.scalar_tensor_tensor(out=th[:, 0:nn], in0=th[:, 0:nn], scalar=1.0, in1=xx[:, 0:nn],
                                                                         op0=ALU.add, op1=ALU.mult), deps=[t], chan=c_dve)
                t_hid = S_.op("dve", lambda e, nn=nn, n0=n0, half=half: e.tensor_scalar(
                    out=hidT[:, half, n0:n0 + nn], in0=th[:, 0:nn], scalar1=0.5, scalar2=None, op0=ALU.mult), deps=[t], chan=c_dve)
        t_last = None
        for (n0, nn) in chunks:
            s, bdeps = ring.next()
            for half in range(2):
                tpe = S_.op("pe", lambda e, half=half, s=s, n0=n0, nn=nn: e.matmul(
                    ps[:, s, 0:nn], lhsT=w2[:, half, :], rhs=hidT[:, half, n0:n0 + nn], start=(half == 0), stop=(half == 1)),
                    deps=([t_hid] + bdeps) if half == 0 else [], chan=c_pe if half == 1 else None)
            t_last = S_.op("act", lambda e, s=s, n0=n0, nn=nn: e.activation(out=cmpo[:, n0:n0 + nn], in_=ps[:, s, 0:nn], func=AF.Copy),
                           deps=[tpe, t_ms], chan=c_act)
            ring.release(s, t_last)
        S_.op("sp", lambda e: e.dma_start(out=cmpT_d[:, :], in_=cmpo[:, :]), deps=[t_last], chan=c_st, inc=16)
        S_.op("sp", lambda e: e.nop(), deps=[(c_st, c_st.count)])
        S_.emit()
    return nc


D = 2048
KC = 16
DH = 128
NEG = -30000.0


def rel_bucket_np(d):
    n = np.maximum(d, 0)
    nf = np.maximum(n, 1).astype(np.float32)
    large = 16 + (np.log(nf / np.float32(16)) / np.float32(math.log(2048 / 16)) * np.float32(16)).astype(np.int32)
    large = np.minimum(large, 31)
    return np.where(n < 16, n, large)


def nsa_consts(rel4):
    p = np.arange(128)[:, None]
    u = np.arange(128)[None, :]
    d = p + 1889 - 16 * u
    Bc = np.where(d[:, None, :] >= 0, rel4[rel_bucket_np(d)].transpose(0, 2, 1), NEG).astype(np.float32)
    y2 = np.arange(2048)[None, :]
    d2 = p + 1920 - y2
    Bs2 = np.where(d2[:, None, :] >= 0, rel4[:, 0:2][rel_bucket_np(d2)].transpose(0, 2, 1), NEG).astype(np.float32)
    y = np.arange(640)[None, :]
    dw = p + 512 - y
    Bw = np.where(((dw >= 0) & (dw < 512))[:, None, :], rel4[:, 0:2][rel_bucket_np(dw)].transpose(0, 2, 1), NEG).astype(np.float32)
    x = np.arange(510)[None, :]
    c = x - 254
    hi = (p >= 64).astype(np.int64)
    Vu = (c <= hi).astype(np.float32).astype(NPBF)
    Fu = np.where((c == hi) | (c == hi - 1), 1e4, -5.0).astype(np.float32).astype(NPBF)
    t31 = np.ascontiguousarray(np.broadcast_to(rel4[31][None, :], (128, 4))).astype(np.float32)
    ident = np.eye(128, dtype=NPBF)
    return dict(Bc=np.ascontiguousarray(Bc), Bs2=np.ascontiguousarray(Bs2), Bw=np.ascontiguousarray(Bw),
                Vu=np.ascontiguousarray(Vu), Fu=np.ascontiguousarray(Fu), t31=t31, ident=ident)


def build_nsa(S, tiles=None):
    NT = S // 128
    NCMP = S // 16 - 1
    NCP = S // 16
    NSB = S // 64
    IW = NCP + 4
    tiles = list(range(NT)) if tiles is None else tiles
    nc = bass.Bass("TRN2", target_bir_lowering=False)
    dI = lambda name, shape, dt: nc.dram_tensor(name, shape, dt, kind="ExternalInput").ap()
    hT_d = dI("hT", [D, S], BF16)
    w_d = dI("w", [D, 524], F32)
    kcmpT_d = dI("kcmpT", [128, NCP], BF16)
    vcmp_d = dI("vcmp", [128, NCP // 128, 128], BF16)
    kselT_d = dI("kselT", [128, S], BF16)
    vsel_d = dI("vsel", [128, NT, 128], BF16)
    kwinT_d = dI("kwinT", [128, S], BF16)
    vwin_d = dI("vwin", [128, NT, 128], BF16)
    Bc_d = dI("Bc", [128, 4, 128], F32)
    Bs2_d = dI("Bs2", [128, 2, 2048], F32)
    Bw_d = dI("Bw", [128, 2, 640], F32)
    Vu_d = dI("Vu", [128, 510], BF16)
    Fu_d = dI("Fu", [128, 510], BF16)
    t31_d = dI("t31", [128, 4], F32)
    id_d = dI("ident", [128, 128], BF16)
    o_d = nc.dram_tensor("o", [S, 256], BF16, kind="ExternalOutput").ap()

    with ExitStack() as es:
        sb = lambda name, shape, dt: es.enter_context(nc.sbuf_tensor(name, shape, dt))
        kselT = sb("kselTsb", [128, S], BF16)
        vsel = sb("vselsb", [128, NT, 128], BF16)
        kwinT = sb("kwinTsb", [128, S], BF16)
        vwin = sb("vwinsb", [128, NT, 128], BF16)
        kcmpT = sb("kcmpTsb", [128, NCP], BF16)
        vcmp = sb("vcmpsb", [128, NCP // 128, 128], BF16)
        w = sb("wsb", [128, KC, 524], BF16)
        Bc = sb("Bcsb", [128, 4, 128], F32)
        Bs2 = sb("Bs2sb", [128, 2, 2048], F32)
        Bw = sb("Bwsb", [128, 2, 640], F32)
        Vu = sb("Vusb", [128, 510], BF16)
        Fu = sb("Fusb", [128, 510], BF16)
        t31 = sb("t31sb", [128, 4], F32)
        ident = sb("identsb", [128, 128], BF16)
        hTt = sb("hTt", [128, 1, KC, 128], BF16)
        qT = sb("qT", [128, 2, 4, 128], BF16)
        gate = sb("gate", [128, 2, 12], F32)
        E = sb("E", [128, 2, 512], BF16)
        Lb = sb("Lb", [128, 2, 512], F32)
        Pb = sb("Pb", [128, 3, 512], BF16)
        PT = sb("PT", [128, 3, 512], BF16)
        Ecmp = sb("Ecmp", [128, 1, NCP], F32)
        imp = sb("imp", [128, IW], F32)
        isel = sb("isel", [128, NSB], F32)
        sc = sb("sc", [128, NSB], F32)
        wk = isel
        selm = sb("selm", [128, 2, NSB], BF16)
        m8a = sb("m8a", [128, 8], F32)
        m8b = sb("m8b", [128, 8], F32)
        thr = sb("thr", [128, 1], F32)
        rsum = sb("rsum", [128, 2, 4], F32)
        rcmp = sb("rcmp", [128, 2, 4], F32)
        rs = sb("rs", [128, 2, 2, 40], F32)
        rtot = sb("rtot", [128, 4], F32)
        outt = sb("outt", [128, 2, 2, 128], F32)
        obuf = sb("obuf", [128, 2, 2, 128], BF16)
        pring = es.enter_context(nc.psum_tensor("pring", [128, 3, 512], F32))
        pT = es.enter_context(nc.psum_tensor("pT", [128, 2, 1024], BF16))
        pO = es.enter_context(nc.psum_tensor("pO", [128, 2, 512], F32))
        pq = es.enter_context(nc.psum_tensor("pq", [128, 512], F32))

        S_ = Sched(nc, es)
        c_pe, c_act, c_dve, c_pool = S_.chan("pe"), S_.chan("act"), S_.chan("dve"), S_.chan("pool")
        S_.main = {"act": c_act, "dve": c_dve, "pool": c_pool}
        c_ld, c_w = S_.chan("ld"), S_.chan("ldw")
        c_st = [S_.chan("st0"), S_.chan("st1")]
        c_h = [S_.chan("h0")]

        ld = lambda out, in_: S_.op("sp", lambda e: e.dma_start(out=out, in_=in_), chan=c_ld, inc=16)
        for q4 in range(4):
            ld(kselT[:, q4 * (S // 4):(q4 + 1) * (S // 4)], kselT_d[:, q4 * (S // 4):(q4 + 1) * (S // 4)])
            ld(kwinT[:, q4 * (S // 4):(q4 + 1) * (S // 4)], kwinT_d[:, q4 * (S // 4):(q4 + 1) * (S // 4)])
        ld(vsel[:, :, :], vsel_d[:, :, :])
        ld(vwin[:, :, :], vwin_d[:, :, :])
        ld(kcmpT[:, :], kcmpT_d[:, :])
        ld(vcmp[:, :, :], vcmp_d[:, :, :])
        ld(Bc[:, :, :], Bc_d[:, :, :]); ld(Bs2[:, :, :], Bs2_d[:, :, :]); ld(Bw[:, :, :], Bw_d[:, :, :])
        ld(Vu[:, :], Vu_d[:, :]); ld(Fu[:, :], Fu_d[:, :]); ld(t31[:, :], t31_d[:, :])
        t_ld = ld(ident[:, :], id_d[:, :])
        t_w = S_.op("pool", lambda e: e.dma_start(out=w[:, :, :], in_=w_d.rearrange("(k p) n -> p k n", p=128)), chan=c_w, inc=16)
        S_.op("pool", lambda e: e.memset(imp[:, :], 0.0), chan=c_pool)
        t_ms = (c_pool, c_pool.count)

        ring, pbr, ptr, ptsr, por = Ring(3), Ring(3), Ring(2), Ring(3), Ring(2)
        er, lbr, hr = Ring(2), Ring(2), Ring(1)
        queue = []

        def emitB(tk):
            cols = tk["cols"]
            nb = (cols + 127) // 128
            tsl, tdeps = ptr.next()
            tp = None
            for a in range(nb):
                ca = min(128, cols - a * 128)
                tp = S_.op("pe", lambda e, a=a, ca=ca, tsl=tsl, ps_=tk["pb"]: e.transpose(
                    out=pT[0:ca, tsl, a * 128:(a + 1) * 128], in_=Pb[:, ps_, a * 128:a * 128 + ca], identity=ident[:, :]),
                    deps=([tk["ready"], t_ld] + tdeps) if a == 0 else [], chan=c_pe if a == nb - 1 else None)
            pbr.release(tk["pb"], tp)
            pts, pdeps = ptsr.next()
            tk["pts"] = pts
            if cols % 128 == 0:
                tcp = S_.op("act", lambda e, tsl=tsl, pts=pts, cols=cols: e.activation(out=PT[:, pts, 0:cols], in_=pT[:, tsl, 0:cols], func=AF.Copy),
                            deps=[tp] + pdeps, chan=c_act)
            else:
                full = (cols // 128) * 128
                rem = cols - full
                if full:
                    S_.op("act", lambda e, tsl=tsl, pts=pts, full=full: e.activation(out=PT[:, pts, 0:full], in_=pT[:, tsl, 0:full], func=AF.Copy),
                          deps=[tp] + pdeps, chan=c_act)
                tcp = S_.op("act", lambda e, tsl=tsl, pts=pts, full=full, rem=rem: e.activation(
                    out=PT[0:rem, pts, full:full + 128], in_=pT[0:rem, tsl, full:full + 128], func=AF.Copy),
                    deps=[tp] + pdeps, chan=c_act)
            ptr.release(tsl, tcp)
            tk["ptready"] = tcp

        def emitC(tk):
            cols = tk["cols"]
            nb = (cols + 127) // 128
            g = tk["grp"]
            if g["os"] is None:
                g["os"], g["odeps"] = por.next()
            os_ = g["os"]
            tpv = None
            for a in range(nb):
                ca = min(128, cols - a * 128)
                first = tk["first"] and a == 0
                last = tk["last"] and a == nb - 1
                vap = tk["v"](a, ca)
                tpv = S_.op("pe", lambda e, a=a, ca=ca, os_=os_, pts=tk["pts"], vap=vap, first=first, last=last: e.matmul(
                    pO[:, os_, 0:128], lhsT=PT[0:ca, pts, a * 128:(a + 1) * 128], rhs=vap, start=first, stop=last),
                    deps=([tk["ptready"]] + (g["odeps"] if first else [])) if a == 0 else [], chan=c_pe if a == nb - 1 else None)
            ptsr.release(tk["pts"], tpv)
            if tk["last"]:
                g["done"](tpv)

        def push(tk):
            queue.append(tk)
            if len(queue) >= 2:
                emitB(queue[-2])
            if len(queue) >= 3:
                emitC(queue[-3])

        def flush():
            if len(queue) >= 1:
                emitB(queue[-1])
            if len(queue) >= 2:
                emitC(queue[-2])
            if len(queue) >= 1:
                emitC(queue[-1])
            queue.clear()

        out_state = {}
        selm_free = [[], []]
        ecmp_free = [[], []]
        outt_free = [[], []]
        obuf_free = [[], []]
        for ti, i in enumerate(tiles):
            t0 = 128 * i
            sl = ti % 2
            hs, hdeps = hr.next()
            t_h = S_.op("sp", lambda e, hs=hs, t0=t0: e.dma_start(
                out=hTt[:, hs, :, :], in_=hT_d[:, t0:t0 + 128].rearrange("(k p) t -> p k t", p=128)), deps=hdeps, chan=c_h[hs], inc=16)
            tq = None
            for h in range(4):
                for kc in range(KC):
                    tq = S_.op("pe", lambda e, h=h, kc=kc, hs=hs: e.matmul(
                        pq[:, h * 128:(h + 1) * 128], lhsT=w[:, kc, h * 128:(h + 1) * 128], rhs=hTt[:, hs, kc, :],
                        start=(kc == 0), stop=(kc == KC - 1)),
                        deps=([t_w, t_h] + out_state.get("pq_free", [])) if (h == 0 and kc == 0) else [],
                        chan=c_pe if (h == 3 and kc == KC - 1) else None)
            t_qT = S_.op("act", lambda e, sl=sl: e.activation(out=qT[:, sl, :, :], in_=pq[:, 0:512].rearrange("p (h t) -> p h t", h=4),
                                                              func=AF.Copy, scale=DH ** -0.5),
                         deps=[tq] + out_state.get("qT_free%d" % sl, []), chan=c_act)
            tg = None
            for kc in range(KC):
                tg = S_.op("pe", lambda e, kc=kc, hs=hs: e.matmul(pq[:, 0:12], lhsT=hTt[:, hs, kc, :], rhs=w[:, kc, 512:524],
                                                                   start=(kc == 0), stop=(kc == KC - 1)),
                           deps=[t_qT] if kc == 0 else [], chan=c_pe if kc == KC - 1 else None)
            hr.release(hs, tg)
            t_g = S_.op("act", lambda e, sl=sl: e.activation(out=gate[:, sl, :], in_=pq[:, 0:12], func=AF.Exp, scale=-1.0),
                        deps=[tg] + out_state.get("gate_free%d" % sl, []), chan=c_act)
            out_state["pq_free"] = [t_g]
            t_g = S_.op("dve", lambda e, sl=sl: e.tensor_scalar(out=gate[:, sl, :], in0=gate[:, sl, :], scalar1=1.0, scalar2=None, op0=ALU.add),
                        deps=[t_g], chan=c_dve)
            t_g = S_.op("dve", lambda e, sl=sl: e.reciprocal(out=gate[:, sl, :], in_=gate[:, sl, :]), deps=[t_g], chan=c_dve)

            qT_users = []
            groups_done = []

            def make_group(h, br, rs_cols_fn, cmp_rec=None):
                g = {"os": None, "odeps": None}

                def done(tpv, h=h, br=br, g=g, sl=sl):
                    os_ = g["os"]
                    if cmp_rec is None:
                        ncol = g["ncols"]
                        t1 = S_.op("dve", lambda e: e.reduce_sum(out=rtot[:, 0:1], in_=rs[:, sl, h, g["c0"]:g["c0"] + ncol],
                                                                 axis=mybir.AxisListType.X), deps=[g["rs_ready"]], chan=c_dve)
                        t1 = S_.op("dve", lambda e: e.tensor_scalar(out=rtot[:, 0:1], in0=rtot[:, 0:1], scalar1=1e-30, scalar2=None, op0=ALU.max),
                                   deps=[t1], chan=c_dve)
                        t1 = S_.op("dve", lambda e: e.reciprocal(out=rtot[:, 1:2], in_=rtot[:, 0:1]), deps=[t1], chan=c_dve)
                        recap = rtot[:, 1:2]
                    else:
                        t1 = cmp_rec[1]
                        recap = cmp_rec[0]
                    t2 = S_.op("dve", lambda e: e.tensor_tensor(out=rtot[:, 2:3], in0=recap, in1=gate[:, sl, h * 3 + br:h * 3 + br + 1], op=ALU.mult),
                               deps=[t1, t_g], chan=c_dve)
                    if br == 0:
                        t3 = S_.op("dve", lambda e: e.tensor_scalar(out=outt[:, sl, h, :], in0=pO[:, os_, 0:128], scalar1=rtot[:, 2:3], scalar2=None,
                                                                    op0=ALU.mult), deps=[t2, tpv] + outt_free[sl], chan=c_dve)
                    else:
                        t3 = S_.op("dve", lambda e: e.scalar_tensor_tensor(out=outt[:, sl, h, :], in0=pO[:, os_, 0:128], scalar=rtot[:, 2:3],
                                                                           in1=outt[:, sl, h, :], op0=ALU.mult, op1=ALU.add),
                                   deps=[t2, tpv], chan=c_dve)
                    por.release(os_, t3)
                    groups_done.append(t3)
                g["done"] = done
                return g

            hi = min(NCMP, 8 * i + 8)
            lo = max(0, 8 * i - 120)
            u0 = lo - (8 * i - 120)
            cchunks = [(0, min(512, hi))] + ([(512, hi - 512)] if hi > 512 else [])
            t_imp = None
            for h in range(4):
                ec = 0
                t_e = None
                for (n0, nn) in cchunks:
                    s, sdeps = ring.next()
                    tpe = S_.op("pe", lambda e, s=s, h=h, n0=n0, nn=nn, sl=sl: e.matmul(
                        pring[:, s, 0:nn], lhsT=qT[:, sl, h, :], rhs=kcmpT[:, n0:n0 + nn], start=True, stop=True),
                        deps=[t_qT, t_ld] + sdeps, chan=c_pe)
                    t_e = S_.op("act", lambda e, s=s, ec=ec, h=h, n0=n0, nn=nn: e.activation(
                        out=Ecmp[:, ec, n0:n0 + nn], in_=pring[:, s, 0:nn], func=AF.Exp, bias=t31[:, h:h + 1], scale=1.0),
                        deps=[tpe] + ecmp_free[ec], chan=c_act)
                    ring.release(s, t_e)
                s, sdeps = ring.next()
                nw = hi - lo
                tpe = S_.op("pe", lambda e, s=s, h=h, lo=lo, nw=nw, sl=sl: e.matmul(
                    pring[:, s, 0:nw], lhsT=qT[:, sl, h, :], rhs=kcmpT[:, lo:lo + nw], start=True, stop=True), deps=sdeps, chan=c_pe)
                qT_users.append(tpe)
                ls, ldeps = lbr.next()
                tl = S_.op("dve", lambda e, s=s, ls=ls, h=h, nw=nw, u0=u0: e.tensor_tensor(
                    out=Lb[:, ls, 0:nw], in0=pring[:, s, 0:nw], in1=Bc[:, h, u0:u0 + nw], op=ALU.add), deps=[tpe] + ldeps, chan=c_dve)
                ring.release(s, tl)
                t_e = S_.op("act", lambda e, ls=ls, ec=ec, lo=lo, nw=nw: e.activation(out=Ecmp[:, ec, lo:lo + nw], in_=Lb[:, ls, 0:nw], func=AF.Exp),
                            deps=[tl, t_e], chan=c_act)
                lbr.release(ls, t_e)
                t1 = S_.op("dve", lambda e, ec=ec, h=h, hi=hi, sl=sl: e.reduce_sum(out=rsum[:, sl, h:h + 1], in_=Ecmp[:, ec, 0:hi],
                                                                                  axis=mybir.AxisListType.X), deps=[t_e], chan=c_dve)
                t1 = S_.op("dve", lambda e, h=h, sl=sl: e.tensor_scalar(out=rsum[:, sl, h:h + 1], in0=rsum[:, sl, h:h + 1], scalar1=1e-30, scalar2=None,
                                                                        op0=ALU.max), deps=[t1], chan=c_dve)
                t_rc = S_.op("dve", lambda e, h=h, sl=sl: e.reciprocal(out=rcmp[:, sl, h:h + 1], in_=rsum[:, sl, h:h + 1]), deps=[t1], chan=c_dve)
                if h == 0:
                    t_imp = S_.op("dve", lambda e, ec=ec, hi=hi, sl=sl: e.tensor_scalar(
                        out=imp[:, 1:1 + hi], in0=Ecmp[:, ec, 0:hi], scalar1=rcmp[:, sl, 0:1], scalar2=None, op0=ALU.mult),
                        deps=[t_rc, t_ms] + out_state.get("imp_free", []), chan=c_dve)
                else:
                    t_imp = S_.op("dve", lambda e, ec=ec, hi=hi, h=h, sl=sl: e.scalar_tensor_tensor(
                        out=imp[:, 1:1 + hi], in0=Ecmp[:, ec, 0:hi], scalar=rcmp[:, sl, h:h + 1], in1=imp[:, 1:1 + hi],
                        op0=ALU.mult, op1=ALU.add), deps=[t_rc, t_imp], chan=c_dve)
                ecmp_free[ec] = [t_imp]
                if h < 2:
                    g = make_group(h, 0, None, cmp_rec=(rcmp[:, sl, h:h + 1], t_rc))
                    for ci, (n0, nn) in enumerate(cchunks):
                        pb, pdeps = pbr.next()
                        tcp = S_.op("pool", lambda e, pb=pb, ec=ec, n0=n0, nn=nn: e.tensor_copy(out=Pb[:, pb, 0:nn], in_=Ecmp[:, ec, n0:n0 + nn]),
                                    deps=[t_e] + pdeps, chan=c_pool)
                        ecmp_free[ec] = ecmp_free[ec] + [tcp]
                        push(dict(pb=pb, cols=nn, ready=tcp, grp=g, first=(ci == 0), last=(ci == len(cchunks) - 1),
                                  v=lambda a, ca, n0=n0: vcmp[0:ca, n0 // 128 + a, :]))
            ss = sl
            t = S_.op("dve", lambda e: e.tensor_reduce(out=isel[:, :], in_=imp[:, 0:4 * NSB].rearrange("p (j f) -> p j f", f=4),
                                                       axis=mybir.AxisListType.X, op=ALU.add), deps=[t_imp], chan=c_dve)
            t = S_.op("dve", lambda e: e.tensor_tensor(out=isel[:, :], in0=isel[:, :], in1=imp[:, 4:4 * NSB + 4:4], op=ALU.add), deps=[t], chan=c_dve)
            out_state["imp_free"] = [t]
            x0 = 254 - 2 * i
            t = S_.op("dve", lambda e, x0=x0: e.scalar_tensor_tensor(out=sc[:, :], in0=isel[:, :], scalar=1.0, in1=Vu[:, x0:x0 + NSB],
                                                                     op0=ALU.add, op1=ALU.mult), deps=[t, t_ld], chan=c_dve)
            t = S_.op("dve", lambda e, x0=x0: e.scalar_tensor_tensor(out=sc[:, :], in0=sc[:, :], scalar=-1.0, in1=Fu[:, x0:x0 + NSB],
                                                                     op0=ALU.add, op1=ALU.max), deps=[t], chan=c_dve)
            t = S_.op("dve", lambda e: e.memset(sc[:, 0:1], 1e4), deps=[t], chan=c_dve)
            t = S_.op("dve", lambda e: e.max(out=m8a[:, :], in_=sc[:, :]), deps=[t], chan=c_dve)
            t = S_.op("dve", lambda e: e.match_replace(out=wk[:, :], in_to_replace=m8a[:, :], in_values=sc[:, :], imm_value=-2.0), deps=[t], chan=c_dve)
            t = S_.op("dve", lambda e: e.max(out=m8b[:, :], in_=wk[:, :]), deps=[t], chan=c_dve)
            t = S_.op("dve", lambda e: e.tensor_scalar(out=thr[:, :], in0=m8b[:, 7:8], scalar1=-0.5, scalar2=None, op0=ALU.max), deps=[t], chan=c_dve)
            t_selm = S_.op("dve", lambda e, ss=ss: e.tensor_scalar(out=selm[:, ss, :], in0=sc[:, :], scalar1=thr[:, 0:1], scalar2=None, op0=ALU.is_ge),
                           deps=[t] + selm_free[ss], chan=c_dve)
            nkw = min(640, t0 + 128)
            s0 = t0 + 128 - nkw
            yoff = 640 - nkw
            wchunks = [(0, min(512, nkw))] + ([(512, nkw - 512)] if nkw > 512 else [])
            for h in range(2):
                g = make_group(h, 2, None)
                g["c0"], g["ncols"] = 32, len(wchunks)
                for ci, (off, cols) in enumerate(wchunks):
                    s, sdeps = ring.next()
                    tpe = S_.op("pe", lambda e, s=s, h=h, off=off, cols=cols, s0=s0, sl=sl: e.matmul(
                        pring[:, s, 0:cols], lhsT=qT[:, sl, h, :], rhs=kwinT[:, s0 + off:s0 + off + cols], start=True, stop=True),
                        deps=sdeps, chan=c_pe)
                    qT_users.append(tpe)
                    ls, ldeps = lbr.next()
                    tl = S_.op("dve", lambda e, s=s, ls=ls, h=h, off=off, cols=cols, yoff=yoff: e.tensor_tensor(
                        out=Lb[:, ls, 0:cols], in0=pring[:, s, 0:cols], in1=Bw[:, h, yoff + off:yoff + off + cols], op=ALU.add),
                        deps=[tpe] + ldeps, chan=c_dve)
                    ring.release(s, tl)
                    pb, pdeps = pbr.next()
                    t_p = S_.op("act", lambda e, ls=ls, pb=pb, cols=cols, h=h, ci=ci, sl=sl: e.activation(
                        out=Pb[:, pb, 0:cols], in_=Lb[:, ls, 0:cols], func=AF.Exp, accum_out=rs[:, sl, h, 32 + ci:33 + ci]),
                        deps=[tl] + pdeps, chan=c_act)
                    lbr.release(ls, t_p)
                    g["rs_ready"] = t_p
                    push(dict(pb=pb, cols=cols, ready=t_p, grp=g, first=(ci == 0), last=(ci == len(wchunks) - 1),
                              v=lambda a, ca, s0=s0, off=off: vwin[:, (s0 + off) // 128 + a, :]))
            nk = t0 + 128
            nch = (nk + 511) // 512
            t_m = None
            for h in range(2):
                g = make_group(h, 1, None)
                g["c0"], g["ncols"] = 0, nch
                for kb in range(nch):
                    cols = min(512, nk - 512 * kb)
                    far = (512 * kb <= t0 - 2048)
                    s, sdeps = ring.next()
                    tpe = S_.op("pe", lambda e, s=s, h=h, kb=kb, cols=cols, sl=sl: e.matmul(
                        pring[:, s, 0:cols], lhsT=qT[:, sl, h, :], rhs=kselT[:, 512 * kb:512 * kb + cols], start=True, stop=True),
                        deps=sdeps, chan=c_pe)
                    qT_users.append(tpe)
                    es_, edeps = er.next()
                    if far:
                        t_e = S_.op("act", lambda e, s=s, es_=es_, h=h, cols=cols: e.activation(
                            out=E[:, es_, 0:cols], in_=pring[:, s, 0:cols], func=AF.Exp, bias=t31[:, h:h + 1], scale=1.0),
                            deps=[tpe] + edeps, chan=c_act)
                        ring.release(s, t_e)
                    else:
                        y2 = 512 * kb - t0 + 1920
                        ls, ldeps = lbr.next()
                        tl = S_.op("dve", lambda e, s=s, ls=ls, h=h, cols=cols, y2=y2: e.tensor_tensor(
                            out=Lb[:, ls, 0:cols], in0=pring[:, s, 0:cols], in1=Bs2[:, h, y2:y2 + cols], op=ALU.add),
                            deps=[tpe] + ldeps, chan=c_dve)
                        ring.release(s, tl)
                        t_e = S_.op("act", lambda e, ls=ls, es_=es_, cols=cols: e.activation(out=E[:, es_, 0:cols], in_=Lb[:, ls, 0:cols], func=AF.Exp),
                                    deps=[tl] + edeps, chan=c_act)
                        lbr.release(ls, t_e)
                    pb, pdeps = pbr.next()
                    nj = cols // 64
                    t_m = S_.op("dve", lambda e, es_=es_, pb=pb, cols=cols, nj=nj, kb=kb, h=h, ss=ss, sl=sl: e.scalar_tensor_tensor(
                        out=Pb[:, pb, 0:cols].rearrange("p (j f) -> p j f", f=64),
                        in0=E[:, es_, 0:cols].rearrange("p (j f) -> p j f", f=64), scalar=1.0,
                        in1=selm[:, ss, 8 * kb:8 * kb + nj].unsqueeze(2).broadcast_to([128, nj, 64]),
                        op0=ALU.mult, op1=ALU.mult, accum_out=rs[:, sl, h, kb:kb + 1]),
                        deps=[t_e, t_selm] + pdeps, chan=c_dve)
                    er.release(es_, t_m)
                    g["rs_ready"] = t_m
                    push(dict(pb=pb, cols=cols, ready=t_m, grp=g, first=(kb == 0), last=(kb == nch - 1),
                              v=lambda a, ca, kb=kb: vsel[:, 4 * kb + a, :]))
            selm_free[ss] = [t_m]
            out_state["qT_free%d" % sl] = [qT_users[-1]]
            out_state["gate_free%d" % sl] = []
            flush()
            t_fin = groups_done[-1]
            t_ob = S_.op("act", lambda e, sl=sl: e.activation(out=obuf[:, sl, :, :], in_=outt[:, sl, :, :], func=AF.Copy),
                         deps=[(c_dve, c_dve.count)] + obuf_free[sl], chan=c_act)
            outt_free[sl] = [t_ob]
            out_state["gate_free%d" % sl] = [(c_dve, c_dve.count)]
            t_st = S_.op("sp", lambda e, sl=sl, t0=t0: e.dma_start(out=o_d[t0:t0 + 128, :].rearrange("p (h d) -> p h d", h=2), in_=obuf[:, sl, :, :]),
                         deps=[t_ob], chan=c_st[sl], inc=16)
            obuf_free[sl] = [t_st]
        S_.op("sp", lambda e: e.nop(), deps=[(c, c.count) for c in c_st if c.count > 0])
        S_.emit()
    return nc


NCORES = 8
FOX_CORES = 8


def build_cast(C):
    nc = bass.Bass("TRN2", target_bir_lowering=False)
    x_d = nc.dram_tensor("xT", [D, C], F32, kind="ExternalInput").ap()
    y_d = nc.dram_tensor("yT", [D, C], BF16, kind="ExternalOutput").ap()
    with ExitStack() as es:
        buf = es.enter_context(nc.sbuf_tensor("buf", [128, 2, C], BF16))
        S_ = Sched(nc, es)
        c_ld = [S_.chan("ld0"), S_.chan("ld1")]
        c_st = [S_.chan("st0"), S_.chan("st1")]
        st = [None, None]
        for kc in range(D // 128):
            s = kc % 2
            t = S_.op("pool", lambda e, kc=kc, s=s: e.dma_start(out=buf[:, s, :], in_=x_d[kc * 128:(kc + 1) * 128, :], max_dma_last_dim=4096),
                      deps=[st[s]], chan=c_ld[s], inc=16)
            st[s] = S_.op("sp", lambda e, kc=kc, s=s: e.dma_start(out=y_d[kc * 128:(kc + 1) * 128, :], in_=buf[:, s, :]),
                          deps=[t], chan=c_st[s], inc=16)
        S_.op("sp", lambda e: e.nop(), deps=st)
        S_.emit()
    return nc


def _launch(nc, in_maps):
    import concourse.bass_utils as bu
    res = bu.run_bass_kernel_spmd(nc, in_maps, core_ids=list(range(len(in_maps))))
    return res.results


def _pmaj(v):
    return np.ascontiguousarray(v.reshape(-1, 128, 128).transpose(1, 0, 2))


_PROGS = {}


def _prog(key, fn):
    if key not in _PROGS:
        _PROGS[key] = fn()
    return _PROGS[key]


def forward(inp, S):
    f32 = lambda a: np.ascontiguousarray(np.asarray(a, dtype=np.float32))
    x = f32(inp["x"]).reshape(S, D)
    fox_w_in, fox_b_f, fox_w_o = f32(inp["fox_w_in"]), f32(inp["fox_b_f"]), f32(inp["fox_w_o"])
    nsa_w_in, nsa_w_o, kv_w = f32(inp["nsa_w_in"]), f32(inp["nsa_w_o"]), f32(inp["kv_w"])
    rel_bias = f32(inp["rel_bias"])
    mlp_w1, mlp_w2 = f32(inp["mlp_w1"]), f32(inp["mlp_w2"])
    lng = [f32(inp[k]) for k in ("ln1_g", "ln1_b", "ln2_g", "ln2_b")]
    NPC = min(NCORES, S // TP)
    T = S // NPC
    NT = S // 128
    NCP = S // 16
    ident_bf = np.eye(128, dtype=NPBF)
    cst_fox = fox_consts()

    CC = S // NCORES
    xT = np.ascontiguousarray(x.T)
    nc_cast = _prog(("cast", CC), lambda: build_cast(CC))
    r = _launch(nc_cast, [{"xT": np.ascontiguousarray(xT[:, c * CC:(c + 1) * CC])} for c in range(NCORES)])
    hT = np.ascontiguousarray(np.concatenate([np.asarray(r[c]["yT"]) for c in range(NCORES)], axis=1))
    del xT
    h = x

    nc_post = _prog(("post", T), lambda: build_post(T, 4 * D, True))

    def post(A, h, wo, layer):
        aT = np.ascontiguousarray(A.T)
        lnp = np.ascontiguousarray(np.stack([lng[0][layer], lng[1][layer], lng[2][layer], lng[3][layer]]))
        maps = [{"aT": np.ascontiguousarray(aT[:, c * T:(c + 1) * T]), "h": np.ascontiguousarray(h[c * T:(c + 1) * T]),
                 "wo": wo, "w1": mlp_w1[layer], "w2": mlp_w2[layer], "lnp": lnp, "ident": ident_bf} for c in range(NPC)]
        r = _launch(nc_post, maps)
        h2 = np.concatenate([np.asarray(r[c]["hout"]) for c in range(NPC)], axis=0)
        hT2 = np.ascontiguousarray(np.concatenate([np.asarray(r[c]["hTout"]) for c in range(NPC)], axis=1))
        return h2, hT2

    nc_fox = _prog(("fox", S), lambda: build_fox(S, 2))
    for l in range(2):
        maps = []
        for c in range(NCORES):
            w = np.empty((2, D, 385), np.float32)
            for k in range(2):
                hd = 2 * c + k
                w[k, :, 0:128] = fox_w_in[l][:, hd * 128:(hd + 1) * 128]
                w[k, :, 128:256] = fox_w_in[l][:, 2048 + hd * 128:2048 + (hd + 1) * 128]
                w[k, :, 256:384] = fox_w_in[l][:, 4096 + hd * 128:4096 + (hd + 1) * 128]
                w[k, :, 384] = fox_w_in[l][:, 6144 + hd]
            maps.append({"hT": hT, "w": w, "bf": np.ascontiguousarray(fox_b_f[l][None, 2 * c:2 * c + 2]), "cst": cst_fox})
        r = []
        for p0 in range(0, NCORES, FOX_CORES):
            r += _launch(nc_fox, maps[p0:p0 + FOX_CORES])
        A = np.concatenate([np.asarray(r[c]["o"]) for c in range(NCORES)], axis=1)
        h, hT = post(A, h, fox_w_o[l], l)

    nc_skv = _prog(("skv", S), lambda: build_skv(S))
    maps = []
    for c in range(NCORES):
        sc, g = c // 4, c % 4
        cols = [(sc * 4 + g) * 128]
        for e in (2 * c, 2 * c + 1):
            cols.append(((2 + e // 4) * 4 + e % 4) * 128)
        w = np.ascontiguousarray(np.concatenate([kv_w[:, c0:c0 + 128] for c0 in cols], axis=1))
        maps.append({"hT": hT, "w": w, "w1": f32(inp["cmp_k_w1"] if sc == 0 else inp["cmp_v_w1"]),
                     "w2": f32(inp["cmp_k_w2"] if sc == 0 else inp["cmp_v_w2"]),
                     "posT": np.ascontiguousarray(f32(inp["cmp_pos_k"] if sc == 0 else inp["cmp_pos_v"]).T)})
    r = _launch(nc_skv, maps)
    kcmpT = [np.asarray(r[g]["cmpT"]) for g in range(4)]
    vcmp = [_pmaj(np.ascontiguousarray(np.asarray(r[4 + g]["cmpT"]).T)) if NCP % 128 == 0 else None for g in range(4)]
    raw = {}
    for c in range(NCORES):
        for k, e in enumerate((2 * c, 2 * c + 1)):
            raw[(2 + e // 4, e % 4)] = np.asarray(r[c]["raw"])[k]
    kselT = [np.ascontiguousarray(raw[(2, g)]) for g in range(4)]
    vsel = [_pmaj(np.ascontiguousarray(raw[(3, g)].T)) for g in range(4)]
    kwinT = [np.ascontiguousarray(raw[(4, g)]) for g in range(4)]
    vwin = [_pmaj(np.ascontiguousarray(raw[(5, g)].T)) for g in range(4)]

    nc_nsa = _prog(("nsa", S), lambda: build_nsa(S))
    for b in range(2):
        layer = 2 + b
        maps = []
        for c in range(NCORES):
            g, half = c // 2, c % 2
            ho = [2 * half, 2 * half + 1, 2 * (1 - half), 2 * (1 - half) + 1]
            heads = [4 * g + r_ for r_ in ho]
            w = np.ascontiguousarray(np.concatenate(
                [nsa_w_in[b][:, hd * 128:(hd + 1) * 128] for hd in heads] +
                [nsa_w_in[b][:, 2048 + 3 * hd:2048 + 3 * hd + 3] for hd in heads], axis=1))
            cs = nsa_consts(np.ascontiguousarray(rel_bias[:, heads]))
            maps.append(dict(hT=hT, w=w, kcmpT=kcmpT[g], vcmp=vcmp[g], kselT=kselT[g], vsel=vsel[g], kwinT=kwinT[g], vwin=vwin[g], **cs))
        r = _launch(nc_nsa, maps)
        A = np.concatenate([np.asarray(r[c]["o"]) for c in range(NCORES)], axis=1)
        h, hT = post(A, h, nsa_w_o[b], layer)
    return h.reshape(1, S, D).astype(np.float32)


def kernel(**inputs):
    return forward(inputs, 16384)
```

```python
import math
import numpy as np
import ml_dtypes
from contextlib import ExitStack
import concourse.bass as bass
import concourse.mybir as mybir
from concourse.bass_utils import run_bass_kernel_spmd

F32 = mybir.dt.float32
BF16 = mybir.dt.bfloat16
AF = mybir.ActivationFunctionType
ALU = mybir.AluOpType
NPBF = ml_dtypes.bfloat16


class Chan:
    def __init__(self, sem):
        self.sem = sem
        self.count = 0


class Sched:
    ENGS = ("pe", "act", "dve", "pool", "sp")

    def __init__(self, nc, es):
        self.nc, self.es = nc, es
        self.q = {e: [] for e in self.ENGS}
        self.nsem = 0
        self.main = {}

    def chan(self, name="c"):
        sem = self.es.enter_context(self.nc.semaphore(f"{name}{self.nsem}"))
        self.nsem += 1
        return Chan(sem)

    def op(self, eng, fn, deps=(), chan=None, inc=1):
        waits = [(d[0], d[1]) for d in deps if d is not None]
        mc = self.main.get(eng)
        if mc is not None and mc.count > 0:
            waits.append((mc, mc.count))
        t = None
        if chan is not None:
            chan.count += inc
            t = (chan, chan.count)
        self.q[eng].append((waits, fn, (chan, inc) if chan is not None else None))
        return t

    def emit(self):
        nc = self.nc
        q = self.q
        with nc.Block() as block:
            def replay(name):
                def f(e):
                    seen = {}
                    for waits, fn, inc in q[name]:
                        for ch, val in waits:
                            if seen.get(id(ch), 0) >= val:
                                continue
                            seen[id(ch)] = val
                            e.wait_ge(ch.sem, val)
                        ins = fn(e)
                        if inc is not None:
                            ins.then_inc(inc[0].sem, inc[1])
                return f
            block.tensor(replay("pe"))
            block.scalar(replay("act"))
            block.vector(replay("dve"))
            block.gpsimd(replay("pool"))
            block.sync(replay("sp"))


class Ring:
    def __init__(self, n):
        self.n = n
        self.i = 0
        self.free = [[] for _ in range(n)]

    def next(self):
        s = self.i % self.n
        self.i += 1
        deps = self.free[s]
        self.free[s] = []
        return s, deps

    def release(self, s, ticket):
        self.free[s].append(ticket)


ALPHA = 8.0 ** 0.25
LN_EPS = 1e-5
D = 2048
KC = 16
TP = 1024


def build_post(T, DFF, emit_hT=True):
    assert T % TP == 0 and DFF % 512 == 0
    NP, NTT, NG = T // TP, TP // 128, DFF // 512
    nc = bass.Bass("TRN2", target_bir_lowering=False)
    aT_d = nc.dram_tensor("aT", [D, T], BF16, kind="ExternalInput").ap()
    h_d = nc.dram_tensor("h", [T, D], F32, kind="ExternalInput").ap()
    wo_d = nc.dram_tensor("wo", [D, D], F32, kind="ExternalInput").ap()
    w1_d = nc.dram_tensor("w1", [D, DFF], F32, kind="ExternalInput").ap()
    w2_d = nc.dram_tensor("w2", [DFF, D], F32, kind="ExternalInput").ap()
    lnp_d = nc.dram_tensor("lnp", [4, D], F32, kind="ExternalInput").ap()
    id_d = nc.dram_tensor("ident", [128, 128], BF16, kind="ExternalInput").ap()
    hout_d = nc.dram_tensor("hout", [T, D], F32, kind="ExternalOutput").ap()
    if emit_hT:
        hTout_d = nc.dram_tensor("hTout", [D, T], BF16, kind="ExternalOutput").ap()

    with ExitStack() as es:
        sb = lambda name, shape, dt: es.enter_context(nc.sbuf_tensor(name, shape, dt))
        yacc = sb("yacc", [128, NTT, D], F32)
        h1T = sb("h1T", [128, KC, TP], BF16)
        wA = sb("wA", [128, 2, KC, 512], BF16)
        wBf = sb("wBf", [128, 2 * 4 * D], BF16)
        uT = sb("uT", [128, 2, 4, TP], BF16)
        lnp = sb("lnpsb", [128, 2, D], F32)
        tmp = sb("tmp", [128, 2, 512], F32)
        xb = sb("xb", [128, D], BF16)
        st = sb("st", [128, 4, 6], F32)
        mv = sb("mv", [128, 2], F32)
        rs = sb("rs", [128, 1], F32)
        nmr = sb("nmr", [128, 1], F32)
        ident = sb("identsb", [128, 128], BF16)
        ps = es.enter_context(nc.psum_tensor("ps", [128, 6, 512], F32))
        psT = es.enter_context(nc.psum_tensor("psT", [128, 2, 1024], BF16))
        wB = wBf[:, :].rearrange("p (s c n) -> p s c n", s=2, c=4)
        aTv = wBf[:, :].rearrange("p (k t) -> p k t", k=KC)

        S = Sched(nc, es)
        c_pe, c_act, c_dve, c_pool = S.chan("pe"), S.chan("act"), S.chan("dve"), S.chan("pool")
        S.main = {"act": c_act, "dve": c_dve, "pool": c_pool}
        c_h, c_aT, c_ln, c_st, c_id = S.chan("ldh"), S.chan("ldaT"), S.chan("ldln"), S.chan("st"), S.chan("ldid")
        c_wA = [S.chan("wA0"), S.chan("wA1")]
        c_wB = [S.chan("wB0"), S.chan("wB1")]
        bank = Ring(6)
        tbank = Ring(2)
        wAr, wBr, uTr, tmpr = Ring(2), Ring(2), Ring(2), Ring(2)

        t_id = S.op("sp", lambda e: e.dma_start(out=ident[:, :], in_=id_d[:, :]), chan=c_id, inc=16)

        def layer_norm(tt, deps):
            t = None
            for c in range(4):
                t = S.op("dve", lambda e, c=c: e.bn_stats(out=st[:, c, :], in_=yacc[:, tt, c * 512:(c + 1) * 512]),
                         deps=deps if c == 0 else [t], chan=c_dve)
            t = S.op("dve", lambda e: e.bn_aggr(out=mv[:, :], in_=st[:, :, :]), deps=[t], chan=c_dve)
            t = S.op("act", lambda e: e.activation(out=rs[:, :], in_=mv[:, 1:2], func=AF.Sqrt, bias=LN_EPS, scale=1.0),
                     deps=[t], chan=c_act)
            t = S.op("dve", lambda e: e.reciprocal(out=rs[:, :], in_=rs[:, :]), deps=[t], chan=c_dve)
            t = S.op("dve", lambda e: e.scalar_tensor_tensor(out=nmr[:, :], in0=mv[:, 0:1], scalar=-1.0, in1=rs[:, :],
                                                             op0=ALU.mult, op1=ALU.mult), deps=[t], chan=c_dve)
            t = S.op("act", lambda e: e.activation(out=yacc[:, tt, :], in_=yacc[:, tt, :], func=AF.Identity,
                                                   bias=nmr[:, 0:1], scale=rs[:, 0:1]), deps=[t], chan=c_act)
            t = S.op("dve", lambda e: e.tensor_tensor(out=yacc[:, tt, :], in0=yacc[:, tt, :], in1=lnp[:, 0, :], op=ALU.mult),
                     deps=[t], chan=c_dve)
            t = S.op("pool", lambda e: e.tensor_tensor(out=yacc[:, tt, :], in0=yacc[:, tt, :], in1=lnp[:, 1, :], op=ALU.add),
                     deps=[t], chan=c_pool)
            return t

        def to_T(tt, dep_x, extra_deps):
            t_xb = S.op("act", lambda e: e.activation(out=xb[:, :], in_=yacc[:, tt, :], func=AF.Copy),
                        deps=[dep_x] + list(xb_free), chan=c_act)
            last = None
            tps = []
            for q in range(4):
                s, fdeps = tbank.next()
                tp = None
                for j in range(4):
                    kc = q * 4 + j
                    tp = S.op("pe", lambda e, kc=kc, s=s, j=j: e.transpose(
                        out=psT[:, s, j * 128:(j + 1) * 128], in_=xb[:, kc * 128:(kc + 1) * 128], identity=ident[:, :]),
                        deps=([t_xb, t_id] + fdeps + list(extra_deps)) if j == 0 else [], chan=c_pe if j == 3 else None)
                tps.append(tp)
                last = S.op("dve", lambda e, q=q, s=s: e.tensor_copy(
                    out=h1T[:, q * 4:(q + 1) * 4, tt * 128:(tt + 1) * 128],
                    in_=psT[:, s, 0:512].rearrange("p (k t) -> p k t", k=4)), deps=[tp], chan=c_dve)
                tbank.release(s, last)
            xb_free[:] = [tps[-1]]
            return t_xb, last

        xb_free = []
        prev_pass_pe = None
        prev_pass_st = []
        prev_ln_done = None
        for p in range(NP):
            tok0 = p * TP
            t_h = None
            for tt in range(NTT):
                t_h = S.op("sp", lambda e, tt=tt, tok0=tok0: e.dma_start(out=yacc[:, tt, :], in_=h_d[tok0 + tt * 128: tok0 + (tt + 1) * 128, :]),
                           deps=prev_pass_st if tt == 0 else [], chan=c_h, inc=16)
            t_aT = None
            for q in range(4):
                t_aT = S.op("sp", lambda e, q=q, tok0=tok0: e.dma_start(
                    out=aTv[:, q * 4:(q + 1) * 4, :],
                    in_=aT_d[q * 512:(q + 1) * 512, tok0:tok0 + TP].rearrange("(k p) t -> p k t", p=128)),
                    deps=[prev_pass_pe] if q == 0 else [], chan=c_aT, inc=16)
            t_ln = None
            for i in range(2):
                t_ln = S.op("sp", lambda e, i=i: e.dma_start(out=lnp[:, i, :], in_=lnp_d[i:i + 1, :].broadcast_to([128, D])),
                            deps=[prev_ln_done] if i == 0 else [], chan=c_ln, inc=16)
            last_res = [None] * NTT
            for nb in range(4):
                s, fdeps = wAr.next()
                t_w = S.op("pool", lambda e, s=s, nb=nb: e.dma_start(
                    out=wA[:, s, :, :], in_=wo_d[:, nb * 512:(nb + 1) * 512].rearrange("(k p) n -> p k n", p=128)),
                    deps=fdeps, chan=c_wA[s], inc=16)
                for tt in range(NTT):
                    b, bdeps = bank.next()
                    tpe = None
                    for kc in range(KC):
                        tpe = S.op("pe", lambda e, kc=kc, tt=tt, s=s, b=b: e.matmul(
                            ps[:, b, :], lhsT=aTv[:, kc, tt * 128:(tt + 1) * 128], rhs=wA[:, s, kc, :],
                            start=(kc == 0), stop=(kc == KC - 1)),
                            deps=([t_w, t_aT] + bdeps) if kc == 0 else [], chan=c_pe if kc == KC - 1 else None)
                    te = S.op("dve", lambda e, tt=tt, nb=nb, b=b: e.scalar_tensor_tensor(
                        out=yacc[:, tt, nb * 512:(nb + 1) * 512], in0=yacc[:, tt, nb * 512:(nb + 1) * 512], scalar=ALPHA,
                        in1=ps[:, b, :], op0=ALU.mult, op1=ALU.add), deps=[tpe, t_h], chan=c_dve)
                    bank.release(b, te)
                    last_res[tt] = te
                wAr.release(s, tpe)
            last_wo_pe = tpe
            t_sc = None
            hT_ready = None
            for tt in range(NTT):
                t = layer_norm(tt, [last_res[tt], t_ln])
                t_xb, hT_ready = to_T(tt, t, prev_pass_st if tt == 0 else [])
                t_sc = S.op("act", lambda e, tt=tt: e.activation(out=yacc[:, tt, :], in_=yacc[:, tt, :], func=AF.Copy, scale=ALPHA),
                            deps=[t_xb], chan=c_act)
                ln1_last = t
            t_ln2 = None
            for i in range(2):
                t_ln2 = S.op("sp", lambda e, i=i: e.dma_start(out=lnp[:, i, :], in_=lnp_d[2 + i:3 + i, :].broadcast_to([128, D])),
                             deps=[ln1_last] if i == 0 else [], chan=c_ln, inc=16)
            last_acc = [[t_sc] * 4 for _ in range(NTT)]
            for g in range(NG):
                sa, fdeps = wAr.next()
                t_w1 = S.op("pool", lambda e, sa=sa, g=g: e.dma_start(
                    out=wA[:, sa, :, :], in_=w1_d[:, g * 512:(g + 1) * 512].rearrange("(k p) n -> p k n", p=128)),
                    deps=fdeps, chan=c_wA[sa], inc=16)
                sb_, fdeps = wBr.next()
                fdeps = fdeps + [last_wo_pe]
                t_w2 = None
                for c in range(4):
                    t_w2 = S.op("pool", lambda e, sb_=sb_, g=g, c=c: e.dma_start(
                        out=wB[:, sb_, c, :], in_=w2_d[g * 512 + c * 128: g * 512 + (c + 1) * 128, :], max_dma_last_dim=4096),
                        deps=fdeps if c == 0 else [], chan=c_wB[sb_], inc=16)
                su, udeps = uTr.next()
                t_u = []
                for c in range(4):
                    for blk in range(2):
                        b, bdeps = bank.next()
                        tpe = None
                        for kc in range(KC):
                            tpe = S.op("pe", lambda e, kc=kc, c=c, blk=blk, sa=sa, b=b: e.matmul(
                                ps[:, b, :], lhsT=wA[:, sa, kc, c * 128:(c + 1) * 128], rhs=h1T[:, kc, blk * 512:(blk + 1) * 512],
                                start=(kc == 0), stop=(kc == KC - 1)),
                                deps=([t_w1, hT_ready] + bdeps) if kc == 0 else [], chan=c_pe if kc == KC - 1 else None)
                        ts_, tdeps = tmpr.next()
                        ta = S.op("act", lambda e, ts_=ts_, b=b: e.activation(out=tmp[:, ts_, :], in_=ps[:, b, :], func=AF.Relu),
                                  deps=[tpe] + tdeps, chan=c_act)
                        bank.release(b, ta)
                        td = S.op("dve", lambda e, ts_=ts_, su=su, c=c, blk=blk: e.tensor_tensor(
                            out=uT[:, su, c, blk * 512:(blk + 1) * 512], in0=tmp[:, ts_, :], in1=tmp[:, ts_, :], op=ALU.mult),
                            deps=[ta] + (udeps if (c == 0 and blk == 0) else []), chan=c_dve)
                        tmpr.release(ts_, td)
                        t_u.append(td)
                wAr.release(sa, tpe)
                for tt in range(NTT):
                    for nb in range(4):
                        b, bdeps = bank.next()
                        tpe = None
                        for c in range(4):
                            tpe = S.op("pe", lambda e, c=c, tt=tt, nb=nb, su=su, sb_=sb_, b=b: e.matmul(
                                ps[:, b, :], lhsT=uT[:, su, c, tt * 128:(tt + 1) * 128], rhs=wB[:, sb_, c, nb * 512:(nb + 1) * 512],
                                start=(c == 0), stop=(c == 3)),
                                deps=([t_w2] + t_u + bdeps) if c == 0 else [], chan=c_pe if c == 3 else None)
                        te = S.op("dve", lambda e, tt=tt, nb=nb, b=b: e.tensor_tensor(
                            out=yacc[:, tt, nb * 512:(nb + 1) * 512], in0=ps[:, b, :], in1=yacc[:, tt, nb * 512:(nb + 1) * 512],
                            op=ALU.add), deps=[tpe, last_acc[tt][nb]], chan=c_dve)
                        bank.release(b, te)
                        last_acc[tt][nb] = te
                wBr.release(sb_, tpe)
                uTr.release(su, tpe)
                prev_pass_pe = tpe
            prev_pass_st = []
            hT2 = None
            for tt in range(NTT):
                t = layer_norm(tt, last_acc[tt] + [t_ln2])
                prev_ln_done = t
                t_st = S.op("sp", lambda e, tt=tt, tok0=tok0: e.dma_start(out=hout_d[tok0 + tt * 128: tok0 + (tt + 1) * 128, :], in_=yacc[:, tt, :]),
                            deps=[t], chan=c_st, inc=16)
                prev_pass_st = [t_st]
                if emit_hT:
                    t_xb, hT2 = to_T(tt, t, [prev_pass_pe] if tt == 0 else [])
            if emit_hT:
                for q in range(4):
                    t_st = S.op("sp", lambda e, q=q, tok0=tok0: e.dma_start(
                        out=hTout_d[q * 512:(q + 1) * 512, tok0:tok0 + TP].rearrange("(k p) t -> p k t", p=128),
                        in_=h1T[:, q * 4:(q + 1) * 4, :]), deps=[hT2], chan=c_st, inc=16)
                    prev_pass_st = [t_st]
        S.op("sp", lambda e: e.nop(), deps=prev_pass_st)
        S.emit()
    return nc


D = 2048
KC = 16
DH = 128
NEG = -30000.0


def fox_consts():
    ident = np.eye(128, dtype=np.float32)
    U = np.triu(np.ones((128, 128), np.float32))
    ones = np.ones((128, 128), np.float32)
    s = np.arange(128)[:, None]
    t = np.arange(128)[None, :]
    maskneg = np.where(s <= t, 0.0, NEG).astype(np.float32)
    return np.ascontiguousarray(np.stack([ident, U, ones, maskneg], axis=1))


def build_fox(S, NH=2):
    NT = S // 128
    NB = S // 512
    nc = bass.Bass("TRN2", target_bir_lowering=False)
    hT_d = nc.dram_tensor("hT", [D, S], BF16, kind="ExternalInput").ap()
    w_d = nc.dram_tensor("w", [NH, D, 385], F32, kind="ExternalInput").ap()
    bf_d = nc.dram_tensor("bf", [1, NH], F32, kind="ExternalInput").ap()
    cst_d = nc.dram_tensor("cst", [128, 4, 128], F32, kind="ExternalInput").ap()
    o_d = nc.dram_tensor("o", [S, NH * DH], BF16, kind="ExternalOutput").ap()

    with ExitStack() as es:
        sb = lambda name, shape, dt: es.enter_context(nc.sbuf_tensor(name, shape, dt))
        import os
        if os.environ.get("FOX_PAD"):
            pad_ = sb("padd", [128, int(os.environ["FOX_PAD"])], BF16)
        KT = sb("KT", [128, S], BF16)
        QT = sb("QT", [128, S], BF16)
        V1 = sb("V1", [128, NT, 129], BF16)
        w = sb("wsb", [128, KC, 385], BF16)
        hb = sb("hb", [128, 2, KC, 512], BF16)
        P = sb("P", [128, 3, 512], BF16)
        L = sb("L", [128, 2, 512], F32)
        ncrow = sb("ncrow", [128, 2, 512], F32)
        import os
        CRBF = bool(os.environ.get("FOX_BF16CR"))
        Dg = sb("Dg", [128, 2, 128], BF16 if CRBF else F32)
        onesbf = sb("onesbf", [128, 128], BF16)
        cst = sb("cstsb", [128, 4, 128], F32)
        lf = sb("lf", [128, NT], F32)
        e1 = sb("e1", [128, NT], F32)
        ll = sb("ll", [128, NT], F32)
        Tsb = sb("Tsb", [128, NT], F32)
        incl = sb("incl", [128, NT], F32)
        Ex = sb("Ex", [128, NT], F32)
        Cc = sb("Cc", [128, NT], F32)
        onesrow = sb("onesrow", [128, NT], F32)
        biasb = sb("biasb", [128, 2, NT], F32)
        fcol = sb("fcol", [128, 2, 4], F32)
        bfb = sb("bfb", [128, NH], F32)
        negbf = sb("negbf", [128, NH], F32)
        odsb = sb("odsb", [128, 2, 129], F32)
        osum = sb("osum", [128, 2, 129], F32)
        rec = sb("rec", [128, 2, 1], F32)
        obuf = sb("obuf", [128, 2, 4, 128], BF16)
        pss = es.enter_context(nc.psum_tensor("pss", [128, 3, 512], F32))
        psm = es.enter_context(nc.psum_tensor("psm", [128, 512], F32))
        po = es.enter_context(nc.psum_tensor("po", [128, 4, 512], F32))
        ident, U, ones, maskneg = cst[:, 0, :], cst[:, 1, :], cst[:, 2, :], cst[:, 3, :]

        S_ = Sched(nc, es)
        c_pe, c_act, c_dve, c_pool = S_.chan("pe"), S_.chan("act"), S_.chan("dve"), S_.chan("pool")
        S_.main = {"act": c_act, "dve": c_dve, "pool": c_pool}
        c_cst, c_w = S_.chan("ldc"), S_.chan("ldw")
        c_st = [S_.chan("st0"), S_.chan("st1")]
        c_hb = [S_.chan("hb0"), S_.chan("hb1")]
        engs = {"pe": c_pe, "act": c_act, "dve": c_dve, "pool": c_pool}

        def barrier():
            deps = [(c, c.count) for c in (c_pe, c_act, c_dve, c_pool, c_st[0], c_st[1]) if c.count > 0]
            for eng in ("pe", "act", "dve", "pool", "sp"):
                S_.op(eng, lambda e: e.nop(), deps=deps)

        t_c = S_.op("sp", lambda e: e.dma_start(out=cst[:, :, :], in_=cst_d[:, :, :]), chan=c_cst, inc=16)
        t_c = S_.op("sp", lambda e: e.dma_start(out=bfb[:, :], in_=bf_d[0:1, :].broadcast_to([128, NH])), chan=c_cst, inc=16)
        S_.op("dve", lambda e: e.tensor_scalar(out=negbf[:, :], in0=bfb[:, :], scalar1=-1.0, scalar2=None, op0=ALU.mult),
              deps=[t_c], chan=c_dve)
        S_.op("pool", lambda e: e.memset(V1[:, :, 128:129], 1.0), chan=c_pool)
        S_.op("pool", lambda e: e.memset(onesrow[:, :], 1.0), chan=c_pool)
        S_.op("pool", lambda e: e.memset(onesbf[:, :], 1.0), chan=c_pool)
        t_ob1 = (c_pool, c_pool.count)

        sring = Ring(3)
        hbr = Ring(2)
        for hd in range(NH):
            if hd > 0:
                barrier()
            t_w = S_.op("pool", lambda e, hd=hd: e.dma_start(out=w[:, :, :], in_=w_d[hd].rearrange("(k p) n -> p k n", p=128)),
                        chan=c_w, inc=16)
            for blk in range(NB):
                hs, hdeps = hbr.next()
                t_h = None
                for q in range(4):
                    t_h = S_.op("sp", lambda e, hs=hs, blk=blk, q=q: e.dma_start(
                        out=hb[:, hs, q * 4:(q + 1) * 4, :],
                        in_=hT_d[q * 512:(q + 1) * 512, blk * 512:(blk + 1) * 512].rearrange("(k p) t -> p k t", p=128)),
                        deps=hdeps if q == 0 else [], chan=c_hb[hs], inc=16)
                s, bdeps = sring.next()
                for kc in range(KC):
                    tpe = S_.op("pe", lambda e, kc=kc, s=s, hs=hs: e.matmul(
                        pss[:, s, :], lhsT=w[:, kc, 0:128], rhs=hb[:, hs, kc, :], start=(kc == 0), stop=(kc == KC - 1)),
                        deps=([t_w, t_h] + bdeps) if kc == 0 else [], chan=c_pe if kc == KC - 1 else None)
                te = S_.op("act", lambda e, s=s, blk=blk: e.activation(out=QT[:, blk * 512:(blk + 1) * 512], in_=pss[:, s, :],
                                                                      func=AF.Copy, scale=DH ** -0.5), deps=[tpe], chan=c_act)
                sring.release(s, te)
                s, bdeps = sring.next()
                for kc in range(KC):
                    tpe = S_.op("pe", lambda e, kc=kc, s=s, hs=hs: e.matmul(
                        pss[:, s, :], lhsT=w[:, kc, 128:256], rhs=hb[:, hs, kc, :], start=(kc == 0), stop=(kc == KC - 1)),
                        deps=bdeps if kc == 0 else [], chan=c_pe if kc == KC - 1 else None)
                te = S_.op("dve", lambda e, s=s, blk=blk: e.tensor_copy(out=KT[:, blk * 512:(blk + 1) * 512], in_=pss[:, s, :]),
                           deps=[tpe], chan=c_dve)
                sring.release(s, te)
                for sub in range(4):
                    tile = blk * 4 + sub
                    s, bdeps = sring.next()
                    for kc in range(KC):
                        tpe = S_.op("pe", lambda e, kc=kc, s=s, hs=hs, sub=sub: e.matmul(
                            pss[:, s, 0:129], lhsT=hb[:, hs, kc, sub * 128:(sub + 1) * 128], rhs=w[:, kc, 256:385],
                            start=(kc == 0), stop=(kc == KC - 1)),
                            deps=bdeps if kc == 0 else [], chan=c_pe if kc == KC - 1 else None)
                    ta = S_.op("act", lambda e, s=s, tile=tile: e.activation(out=V1[:, tile, 0:128], in_=pss[:, s, 0:128], func=AF.Copy),
                               deps=[tpe], chan=c_act)
                    td = S_.op("act", lambda e, s=s, tile=tile: e.activation(out=lf[:, tile:tile + 1], in_=pss[:, s, 128:129], func=AF.Copy),
                               deps=[tpe], chan=c_act)
                    sring.release(s, ta)
                    sring.release(s, td)
                hbr.release(hs, tpe)
            t_projA, t_projD = (c_act, c_act.count), (c_dve, c_dve.count)
            t = S_.op("act", lambda e, hd=hd: e.activation(out=e1[:, :], in_=lf[:, :], func=AF.Exp, bias=negbf[:, hd:hd + 1], scale=-1.0),
                      deps=[t_projD, t_projA], chan=c_act)
            t_l = S_.op("act", lambda e: e.activation(out=ll[:, :], in_=e1[:, :], func=AF.Ln, bias=1.0, scale=1.0), deps=[t], chan=c_act)
            s, bdeps = sring.next()
            t_W = S_.op("pe", lambda e: e.matmul(psm[:, 0:NT], lhsT=U, rhs=ll[:, :], start=True, stop=True), deps=[t_l, t_c], chan=c_pe)
            t_T = S_.op("pe", lambda e, s=s: e.matmul(pss[:, s, 0:NT], lhsT=ones, rhs=ll[:, :], start=True, stop=True), deps=bdeps, chan=c_pe)
            t = S_.op("dve", lambda e, s=s: e.tensor_copy(out=Tsb[:, :], in_=pss[:, s, 0:NT]), deps=[t_T], chan=c_dve)
            sring.release(s, t)
            t = S_.op("dve", lambda e: e.tensor_tensor_scan(out=incl[:, :], data0=onesrow[:, :], data1=Tsb[:, :], initial=0.0,
                                                            op0=ALU.mult, op1=ALU.add), deps=[t, (c_pool, c_pool.count)], chan=c_dve)
            t = S_.op("dve", lambda e: e.tensor_tensor(out=Ex[:, :], in0=incl[:, :], in1=Tsb[:, :], op=ALU.subtract), deps=[t], chan=c_dve)
            t_C = S_.op("dve", lambda e: e.tensor_tensor(out=Cc[:, :], in0=psm[:, 0:NT], in1=Ex[:, :], op=ALU.add), deps=[t, t_W], chan=c_dve)
            pring, lring, cring, bring, oring = Ring(3), Ring(2), Ring(2), Ring(2), Ring(2)
            po_free = []
            psm_free = [t_C]
            import os
            for b in range(int(os.environ.get('FOX_NB', NB))):
                nk = 4 * b + 4
                bs, bdeps_ = bring.next()
                t_bias = S_.op("dve", lambda e, bs=bs, b=b, nk=nk: e.tensor_scalar(
                    out=biasb[:, bs, 0:nk], in0=Cc[:, 0:nk], scalar1=Ex[:, 4 * b:4 * b + 1], scalar2=None, op0=ALU.subtract),
                    deps=[t_C] + bdeps_, chan=c_dve)
                t_f = S_.op("act", lambda e, bs=bs, b=b: e.activation(out=fcol[:, bs, :], in_=biasb[:, bs, 4 * b:4 * b + 4],
                                                                      func=AF.Exp, scale=-1.0), deps=[t_bias], chan=c_act)
                cs, cdeps = cring.next()
                t_cr = None
                for qq in range(4):
                    ds_ = qq % 2
                    t_dg = S_.op("dve", lambda e, ds_=ds_, bs=bs, b=b, qq=qq: e.tensor_scalar(
                        out=Dg[:, ds_, :], in0=ident, scalar1=biasb[:, bs, 4 * b + qq:4 * b + qq + 1], scalar2=-1.0,
                        op0=ALU.mult, op1=ALU.mult), deps=[t_bias, t_cr] if t_cr is not None else [t_bias, t_c], chan=c_dve)
                    t_cr = S_.op("pe", lambda e, ds_=ds_, qq=qq: e.matmul(psm[:, qq * 128:(qq + 1) * 128], lhsT=(onesbf[:, :] if CRBF else ones), rhs=Dg[:, ds_, :],
                                                                          start=True, stop=True),
                                 deps=[t_dg, t_ob1] + (psm_free if qq == 0 else []), chan=c_pe)
                t_ncr = S_.op("act", lambda e, cs=cs: e.activation(out=ncrow[:, cs, :], in_=psm[:, :], func=AF.Copy),
                              deps=[t_cr] + cdeps, chan=c_act)
                psm_free = [t_ncr]
                tiles = [("off", j) for j in range(4 * b)] + [("diag", kk) for kk in range(4)]
                pend = None
                first_pv = True
                last_pv = None
                diag_exp = []

                def emit_pv(info):
                    nonlocal first_pv, last_pv
                    kind, idx, pslot, t_exp = info
                    tpv = None
                    if kind == "off":
                        j = idx
                        for qq in range(4):
                            st_flag = (j == 0)
                            sp_flag = (j == 4 * b - 1)
                            tpv = S_.op("pe", lambda e, qq=qq, pslot=pslot, j=j, st_flag=st_flag, sp_flag=sp_flag: e.matmul(
                                po[:, qq, 0:129], lhsT=P[:, pslot, qq * 128:(qq + 1) * 128], rhs=V1[:, j, :],
                                start=st_flag, stop=sp_flag, skip_group_check=True),
                                deps=([t_exp] + (po_free if first_pv else [])) if qq == 0 else [], chan=c_pe if qq == 3 else None)
                            first_pv = False
                    else:
                        kk = idx
                        for qq in range(kk, 4):
                            st_flag = (b == 0 and kk == 0)
                            tpv = S_.op("pe", lambda e, qq=qq, pslot=pslot, kk=kk, st_flag=st_flag, b=b: e.matmul(
                                po[:, qq, 256:385], lhsT=P[:, pslot, (qq - kk) * 128:(qq - kk + 1) * 128], rhs=V1[:, 4 * b + kk, :],
                                start=st_flag, stop=(kk == qq), skip_group_check=True),
                                deps=([t_exp] + (po_free if first_pv else [])) if qq == kk else [], chan=c_pe if qq == 3 else None)
                            first_pv = False
                    pring.release(pslot, tpv)
                    last_pv = tpv

                for kind, idx in tiles:
                    s, sdeps = sring.next()
                    pslot, pdeps = pring.next()
                    if kind == "off":
                        j = idx
                        t_s = S_.op("pe", lambda e, s=s, j=j, b=b: e.matmul(
                            pss[:, s, :], lhsT=KT[:, j * 128:(j + 1) * 128], rhs=QT[:, b * 512:(b + 1) * 512], start=True, stop=True),
                            deps=sdeps + [t_projA, t_projD], chan=c_pe)
                        t_exp = S_.op("act", lambda e, s=s, pslot=pslot, bs=bs, j=j: e.activation(
                            out=P[:, pslot, :], in_=pss[:, s, :], func=AF.Exp, bias=biasb[:, bs, j:j + 1], scale=1.0),
                            deps=[t_s, t_bias] + pdeps, chan=c_act)
                        sring.release(s, t_exp)
                    else:
                        kk = idx
                        N = (4 - kk) * 128
                        t_s = S_.op("pe", lambda e, s=s, kk=kk, b=b, N=N: e.matmul(
                            pss[:, s, 0:N], lhsT=KT[:, (4 * b + kk) * 128:(4 * b + kk + 1) * 128],
                            rhs=QT[:, b * 512 + kk * 128:(b + 1) * 512], start=True, stop=True),
                            deps=sdeps + [t_projA, t_projD], chan=c_pe)
                        ls, ldeps = lring.next()
                        t_l1 = S_.op("dve", lambda e, s=s, ls=ls, cs=cs, kk=kk, N=N: e.tensor_tensor(
                            out=L[:, ls, 0:N], in0=pss[:, s, 0:N], in1=ncrow[:, cs, kk * 128:512], op=ALU.add),
                            deps=[t_s, t_ncr] + ldeps, chan=c_dve)
                        sring.release(s, t_l1)
                        import os
                        if os.environ.get("FOX_NOPOOL"):
                            t_l2 = S_.op("dve", lambda e, ls=ls: e.tensor_tensor(out=L[:, ls, 0:128], in0=L[:, ls, 0:128], in1=maskneg, op=ALU.add),
                                         deps=[t_l1, t_c], chan=c_dve)
                        else:
                            t_l2 = S_.op("pool", lambda e, ls=ls: e.tensor_tensor(out=L[:, ls, 0:128], in0=L[:, ls, 0:128], in1=maskneg, op=ALU.add),
                                         deps=[t_l1, t_c], chan=c_pool)
                        t_exp = S_.op("act", lambda e, ls=ls, pslot=pslot, bs=bs, kk=kk, b=b, N=N: e.activation(
                            out=P[:, pslot, 0:N], in_=L[:, ls, 0:N], func=AF.Exp, bias=biasb[:, bs, 4 * b + kk:4 * b + kk + 1], scale=1.0),
                            deps=[t_l2] + pdeps, chan=c_act)
                        lring.release(ls, t_exp)
                        diag_exp.append(t_exp)
                    if pend is not None:
                        emit_pv(pend)
                    pend = (kind, idx, pslot, t_exp)
                emit_pv(pend)
                cring.release(cs, diag_exp[-1])
                bring.release(bs, diag_exp[-1])
                os_, odeps = oring.next()
                t_o = None
                po_free = []
                for qq in range(4):
                    k2 = qq % 2
                    t1 = S_.op("dve", lambda e, qq=qq, k2=k2: e.tensor_copy(out=odsb[:, k2, :], in_=po[:, qq, 256:385]),
                               deps=[last_pv] + ([t_o] if t_o is not None else []), chan=c_dve)
                    if b > 0:
                        t2 = S_.op("dve", lambda e, qq=qq, k2=k2, bs=bs: e.scalar_tensor_tensor(
                            out=osum[:, k2, :], in0=po[:, qq, 0:129], scalar=fcol[:, bs, qq:qq + 1], in1=odsb[:, k2, :],
                            op0=ALU.mult, op1=ALU.add), deps=[t1, t_f], chan=c_dve)
                    else:
                        t2 = S_.op("dve", lambda e, k2=k2: e.tensor_copy(out=osum[:, k2, :], in_=odsb[:, k2, :]), deps=[t1], chan=c_dve)
                    t3 = S_.op("dve", lambda e, k2=k2: e.reciprocal(out=rec[:, k2, :], in_=osum[:, k2, 128:129]), deps=[t2], chan=c_dve)
                    t_o = S_.op("act", lambda e, qq=qq, k2=k2, os_=os_: e.activation(
                        out=obuf[:, os_, qq, :], in_=osum[:, k2, 0:128], func=AF.Identity, scale=rec[:, k2, 0:1]),
                        deps=[t3] + (odeps if qq == 0 else []), chan=c_act)
                    po_free += [t1, t2]
                t_st = S_.op("sp", lambda e, os_=os_, b=b, hd=hd: e.dma_start(
                    out=o_d[b * 512:(b + 1) * 512, hd * DH:(hd + 1) * DH].rearrange("(q p) d -> p q d", p=128),
                    in_=obuf[:, os_, :, :]), deps=[t_o], chan=c_st[os_], inc=16)
                oring.release(os_, t_st)
        S_.op("sp", lambda e: e.nop(), deps=[(c, c.count) for c in c_st if c.count > 0])
        S_.emit()
    return nc


D = 2048
KC = 16


def build_skv(S):
    NB = S // 512
    NCMP = S // 16 - 1
    nc = bass.Bass("TRN2", target_bir_lowering=False)
    hT_d = nc.dram_tensor("hT", [D, S], BF16, kind="ExternalInput").ap()
    w_d = nc.dram_tensor("w", [D, 384], F32, kind="ExternalInput").ap()
    w1_d = nc.dram_tensor("w1", [4096, 256], F32, kind="ExternalInput").ap()
    w2_d = nc.dram_tensor("w2", [256, 128], F32, kind="ExternalInput").ap()
    posT_d = nc.dram_tensor("posT", [128, 32], F32, kind="ExternalInput").ap()
    cmpT_d = nc.dram_tensor("cmpT", [128, S // 16], BF16, kind="ExternalOutput").ap()
    raw_d = nc.dram_tensor("raw", [2, 128, S], BF16, kind="ExternalOutput").ap()

    with ExitStack() as es:
        sb = lambda name, shape, dt: es.enter_context(nc.sbuf_tensor(name, shape, dt))
        rawT = sb("rawT", [128, 3, S], BF16)
        w = sb("wsb", [128, KC, 384], BF16)
        w1 = sb("w1sb", [128, 32, 256], BF16)
        w2 = sb("w2sb", [128, 2, 128], BF16)
        posT = sb("posTsb", [128, 32], BF16)
        hb = sb("hb", [128, 2, KC, 512], BF16)
        pbias = sb("pbias", [128, 2], F32)
        xx = sb("xx", [128, 512], F32)
        x2 = sb("x2", [128, 512], F32)
        th = sb("th", [128, 512], F32)
        hidT = sb("hidT", [128, 2, 1024], BF16)
        cmpo = sb("cmpo", [128, S // 16], BF16)
        ps = es.enter_context(nc.psum_tensor("ps", [128, 4, 512], F32))
        psb = es.enter_context(nc.psum_tensor("psb", [128, 2], F32))

        S_ = Sched(nc, es)
        c_pe, c_act, c_dve, c_pool = S_.chan("pe"), S_.chan("act"), S_.chan("dve"), S_.chan("pool")
        S_.main = {"act": c_act, "dve": c_dve, "pool": c_pool}
        c_w, c_st = S_.chan("ldw"), S_.chan("st")
        c_hb = [S_.chan("hb0"), S_.chan("hb1")]
        ring = Ring(4)
        hbr = Ring(2)

        S_.op("pool", lambda e: e.dma_start(out=w[:, :, :], in_=w_d.rearrange("(k p) n -> p k n", p=128)), chan=c_w, inc=16)
        S_.op("pool", lambda e: e.dma_start(out=w1[:, :, :], in_=w1_d.rearrange("(j d) h -> d j h", d=128)), chan=c_w, inc=16)
        S_.op("pool", lambda e: e.dma_start(out=w2[:, :, :], in_=w2_d.rearrange("(c p) d -> p c d", p=128)), chan=c_w, inc=16)
        t_w = S_.op("pool", lambda e: e.dma_start(out=posT[:, :], in_=posT_d[:, :]), chan=c_w, inc=16)
        S_.op("pool", lambda e: e.memset(cmpo[:, :], 0.0), chan=c_pool)
        t_ms = (c_pool, c_pool.count)

        t_raw = None
        for blk in range(NB):
            hs, hdeps = hbr.next()
            t_h = None
            for q in range(4):
                t_h = S_.op("sp", lambda e, hs=hs, blk=blk, q=q: e.dma_start(
                    out=hb[:, hs, q * 4:(q + 1) * 4, :],
                    in_=hT_d[q * 512:(q + 1) * 512, blk * 512:(blk + 1) * 512].rearrange("(k p) t -> p k t", p=128)),
                    deps=hdeps if q == 0 else [], chan=c_hb[hs], inc=16)
            for cb in range(3):
                s, bdeps = ring.next()
                for kc in range(KC):
                    tpe = S_.op("pe", lambda e, kc=kc, s=s, hs=hs, cb=cb: e.matmul(
                        ps[:, s, :], lhsT=w[:, kc, cb * 128:(cb + 1) * 128], rhs=hb[:, hs, kc, :], start=(kc == 0), stop=(kc == KC - 1)),
                        deps=([t_w, t_h] + bdeps) if kc == 0 else [], chan=c_pe if kc == KC - 1 else None)
                if cb % 2 == 0:
                    te = S_.op("act", lambda e, s=s, blk=blk, cb=cb: e.activation(out=rawT[:, cb, blk * 512:(blk + 1) * 512], in_=ps[:, s, :], func=AF.Copy),
                               deps=[tpe], chan=c_act)
                else:
                    te = S_.op("dve", lambda e, s=s, blk=blk, cb=cb: e.tensor_copy(out=rawT[:, cb, blk * 512:(blk + 1) * 512], in_=ps[:, s, :]),
                               deps=[tpe], chan=c_dve)
                ring.release(s, te)
            hbr.release(hs, tpe)
        t_rawA, t_rawD = (c_act, c_act.count), (c_dve, c_dve.count)
        for k in range(2):
            S_.op("sp", lambda e, k=k: e.dma_start(out=raw_d[k], in_=rawT[:, 1 + k, :]), deps=[t_rawA, t_rawD], chan=c_st, inc=16)
        for half in range(2):
            for j in range(32):
                tpb = S_.op("pe", lambda e, j=j, half=half: e.matmul(
                    psb[:, half:half + 1], lhsT=w1[:, j, half * 128:(half + 1) * 128], rhs=posT[:, j:j + 1], start=(j == 0), stop=(j == 31)),
                    deps=[t_w] if j == 0 else [], chan=c_pe if j == 31 else None)
        t_pb = S_.op("dve", lambda e: e.tensor_copy(out=pbias[:, :], in_=psb[:, :]), deps=[tpb], chan=c_dve)
        chunks = [(0, 512), (512, NCMP - 512)] if NCMP > 512 else [(0, NCMP)]
        t_hid = None
        for (n0, nn) in chunks:
            for half in range(2):
                s, bdeps = ring.next()
                for j in range(32):
                    tpe = S_.op("pe", lambda e, j=j, half=half, s=s, n0=n0, nn=nn: e.matmul(
                        ps[:, s, 0:nn], lhsT=w1[:, j, half * 128:(half + 1) * 128],
                        rhs=rawT[:, 0, 16 * n0 + j: 16 * n0 + j + 16 * (nn - 1) + 1: 16], start=(j == 0), stop=(j == 31)),
                        deps=([t_rawA, t_rawD] + bdeps) if j == 0 else [], chan=c_pe if j == 31 else None)
                t = S_.op("act", lambda e, s=s, nn=nn, half=half: e.activation(out=xx[:, 0:nn], in_=ps[:, s, 0:nn], func=AF.Identity,
                                                                              bias=pbias[:, half:half + 1], scale=1.0),
                          deps=[tpe, t_pb] + ([t_hid] if t_hid is not None else []), chan=c_act)
                ring.release(s, t)
                t = S_.op("dve", lambda e, nn=nn: e.tensor_tensor(out=x2[:, 0:nn], in0=xx[:, 0:nn], in1=xx[:, 0:nn], op=ALU.mult), deps=[t], chan=c_dve)
                t = S_.op("dve", lambda e, nn=nn: e.tensor_scalar(out=x2[:, 0:nn], in0=x2[:, 0:nn], scalar1=0.044715, scalar2=1.0,
                                                                  op0=ALU.mult, op1=ALU.add), deps=[t], chan=c_dve)
                t = S_.op("dve", lambda e, nn=nn: e.tensor_tensor(out=x2[:, 0:nn], in0=x2[:, 0:nn], in1=xx[:, 0:nn], op=ALU.mult), deps=[t], chan=c_dve)
                t = S_.op("act", lambda e, nn=nn: e.activation(out=th[:, 0:nn], in_=x2[:, 0:nn], func=AF.Tanh, scale=0.7978845608028654),
                          deps=[t], chan=c_act)
                t = S_.op("dve", lambda e, nn=nn: e.scalar_tensor_tensor(out=th[:, 0:nn], in0=th[:, 0:nn], scalar=1.0, in1=xx[:, 0:nn],
                                                                         op0=ALU.add, op1=ALU.mult), deps=[t], chan=c_dve)
                t_hid = S_.op("dve", lambda e, nn=nn, n0=n0, half=half: e.tensor_scalar(
                    out=hidT[:, half, n0:n0 + nn], in0=th[:, 0:nn], scalar1=0.5, scalar2=None, op0=ALU.mult), deps=[t], chan=c_dve)
        t_last = None
        for (n0, nn) in chunks:
            s, bdeps = ring.next()
            for half in range(2):
                tpe = S_.op("pe", lambda e, half=half, s=s, n0=n0, nn=nn: e.matmul(
                    ps[:, s, 0:nn], lhsT=w2[:, half, :], rhs=hidT[:, half, n0:n0 + nn], start=(half == 0), stop=(half == 1)),
                    deps=([t_hid] + bdeps) if half == 0 else [], chan=c_pe if half == 1 else None)
            t_last = S_.op("act", lambda e, s=s, n0=n0, nn=nn: e.activation(out=cmpo[:, n0:n0 + nn], in_=ps[:, s, 0:nn], func=AF.Copy),
                           deps=[tpe, t_ms], chan=c_act)
            ring.release(s, t_last)
        S_.op("sp", lambda e: e.dma_start(out=cmpT_d[:, :], in_=cmpo[:, :]), deps=[t_last], chan=c_st, inc=16)
        S_.op("sp", lambda e: e.nop(), deps=[(c_st, c_st.count)])
        S_.emit()
    return nc


D = 2048
KC = 16
DH = 128
NEG = -30000.0


def rel_bucket_np(d):
    n = np.maximum(d, 0)
    nf = np.maximum(n, 1).astype(np.float32)
    large = 16 + (np.log(nf / np.float32(16)) / np.float32(math.log(2048 / 16)) * np.float32(16)).astype(np.int32)
    large = np.minimum(large, 31)
    return np.where(n < 16, n, large)


def nsa_consts(rel4):
    p = np.arange(128)[:, None]
    u = np.arange(128)[None, :]
    d = p + 1889 - 16 * u
    Bc = np.where(d[:, None, :] >= 0, rel4[rel_bucket_np(d)].transpose(0, 2, 1), NEG).astype(np.float32)
    y2 = np.arange(2048)[None, :]
    d2 = p + 1920 - y2
    Bs2 = np.where(d2[:, None, :] >= 0, rel4[:, 0:2][rel_bucket_np(d2)].transpose(0, 2, 1), NEG).astype(np.float32)
    y = np.arange(640)[None, :]
    dw = p + 512 - y
    Bw = np.where(((dw >= 0) & (dw < 512))[:, None, :], rel4[:, 0:2][rel_bucket_np(dw)].transpose(0, 2, 1), NEG).astype(np.float32)
    x = np.arange(510)[None, :]
    c = x - 254
    hi = (p >= 64).astype(np.int64)
    Vu = (c <= hi).astype(np.float32).astype(NPBF)
    Fu = np.where((c == hi) | (c == hi - 1), 1e4, -5.0).astype(np.float32).astype(NPBF)
    t31 = np.ascontiguousarray(np.broadcast_to(rel4[31][None, :], (128, 4))).astype(np.float32)
    ident = np.eye(128, dtype=NPBF)
    return dict(Bc=np.ascontiguousarray(Bc), Bs2=np.ascontiguousarray(Bs2), Bw=np.ascontiguousarray(Bw),
                Vu=np.ascontiguousarray(Vu), Fu=np.ascontiguousarray(Fu), t31=t31, ident=ident)


def build_nsa(S, tiles=None):
    NT = S // 128
    NCMP = S // 16 - 1
    NCP = S // 16
    NSB = S // 64
    IW = NCP + 4
    tiles = list(range(NT)) if tiles is None else tiles
    nc = bass.Bass("TRN2", target_bir_lowering=False)
    dI = lambda name, shape, dt: nc.dram_tensor(name, shape, dt, kind="ExternalInput").ap()
    hT_d = dI("hT", [D, S], BF16)
    w_d = dI("w", [D, 524], F32)
    kcmpT_d = dI("kcmpT", [128, NCP], BF16)
    vcmp_d = dI("vcmp", [128, NCP // 128, 128], BF16)
    kselT_d = dI("kselT", [128, S], BF16)
    vsel_d = dI("vsel", [128, NT, 128], BF16)
    kwinT_d = dI("kwinT", [128, S], BF16)
    vwin_d = dI("vwin", [128, NT, 128], BF16)
    Bc_d = dI("Bc", [128, 4, 128], F32)
    Bs2_d = dI("Bs2", [128, 2, 2048], F32)
    Bw_d = dI("Bw", [128, 2, 640], F32)
    Vu_d = dI("Vu", [128, 510], BF16)
    Fu_d = dI("Fu", [128, 510], BF16)
    t31_d = dI("t31", [128, 4], F32)
    id_d = dI("ident", [128, 128], BF16)
    o_d = nc.dram_tensor("o", [S, 256], BF16, kind="ExternalOutput").ap()

    with ExitStack() as es:
        sb = lambda name, shape, dt: es.enter_context(nc.sbuf_tensor(name, shape, dt))
        kselT = sb("kselTsb", [128, S], BF16)
        vsel = sb("vselsb", [128, NT, 128], BF16)
        kwinT = sb("kwinTsb", [128, S], BF16)
        vwin = sb("vwinsb", [128, NT, 128], BF16)
        kcmpT = sb("kcmpTsb", [128, NCP], BF16)
        vcmp = sb("vcmpsb", [128, NCP // 128, 128], BF16)
        w = sb("wsb", [128, KC, 524], BF16)
        Bc = sb("Bcsb", [128, 4, 128], F32)
        Bs2 = sb("Bs2sb", [128, 2, 2048], F32)
        Bw = sb("Bwsb", [128, 2, 640], F32)
        Vu = sb("Vusb", [128, 510], BF16)
        Fu = sb("Fusb", [128, 510], BF16)
        t31 = sb("t31sb", [128, 4], F32)
        ident = sb("identsb", [128, 128], BF16)
        hTt = sb("hTt", [128, 1, KC, 128], BF16)
        qT = sb("qT", [128, 2, 4, 128], BF16)
        gate = sb("gate", [128, 2, 12], F32)
        E = sb("E", [128, 2, 512], BF16)
        Lb = sb("Lb", [128, 2, 512], F32)
        Pb = sb("Pb", [128, 4, 512], BF16)
        PT = sb("PT", [128, 3, 512], BF16)
        Ecmp = sb("Ecmp", [128, 1, NCP], F32)
        imp = sb("imp", [128, IW], F32)
        isel = sb("isel", [128, NSB], F32)
        sc = sb("sc", [128, NSB], F32)
        wk = isel
        selm = sb("selm", [128, 2, NSB], BF16)
        m8a = sb("m8a", [128, 8], F32)
        m8b = sb("m8b", [128, 8], F32)
        thr = sb("thr", [128, 1], F32)
        rsum = sb("rsum", [128, 2, 4], F32)
        rcmp = sb("rcmp", [128, 2, 4], F32)
        rs = sb("rs", [128, 2, 2, 40], F32)
        rtot = sb("rtot", [128, 4], F32)
        outt = sb("outt", [128, 2, 2, 128], F32)
        obuf = sb("obuf", [128, 2, 2, 128], BF16)
        pring = es.enter_context(nc.psum_tensor("pring", [128, 3, 512], F32))
        pT = es.enter_context(nc.psum_tensor("pT", [128, 2, 1024], BF16))
        pO = es.enter_context(nc.psum_tensor("pO", [128, 2, 512], F32))
        pq = es.enter_context(nc.psum_tensor("pq", [128, 512], F32))

        S_ = Sched(nc, es)
        c_pe, c_act, c_dve, c_pool = S_.chan("pe"), S_.chan("act"), S_.chan("dve"), S_.chan("pool")
        S_.main = {"act": c_act, "dve": c_dve, "pool": c_pool}
        c_ld, c_w = S_.chan("ld"), S_.chan("ldw")
        c_st = [S_.chan("st0"), S_.chan("st1")]
        c_h = [S_.chan("h0")]

        ld = lambda out, in_: S_.op("sp", lambda e: e.dma_start(out=out, in_=in_), chan=c_ld, inc=16)
        for q4 in range(4):
            ld(kselT[:, q4 * (S // 4):(q4 + 1) * (S // 4)], kselT_d[:, q4 * (S // 4):(q4 + 1) * (S // 4)])
            ld(kwinT[:, q4 * (S // 4):(q4 + 1) * (S // 4)], kwinT_d[:, q4 * (S // 4):(q4 + 1) * (S // 4)])
        ld(vsel[:, :, :], vsel_d[:, :, :])
        ld(vwin[:, :, :], vwin_d[:, :, :])
        ld(kcmpT[:, :], kcmpT_d[:, :])
        ld(vcmp[:, :, :], vcmp_d[:, :, :])
        ld(Bc[:, :, :], Bc_d[:, :, :]); ld(Bs2[:, :, :], Bs2_d[:, :, :]); ld(Bw[:, :, :], Bw_d[:, :, :])
        ld(Vu[:, :], Vu_d[:, :]); ld(Fu[:, :], Fu_d[:, :]); ld(t31[:, :], t31_d[:, :])
        t_ld = ld(ident[:, :], id_d[:, :])
        t_w = S_.op("pool", lambda e: e.dma_start(out=w[:, :, :], in_=w_d.rearrange("(k p) n -> p k n", p=128)), chan=c_w, inc=16)
        S_.op("pool", lambda e: e.memset(imp[:, :], 0.0), chan=c_pool)
        t_ms = (c_pool, c_pool.count)

        ring, pbr, ptr, ptsr, por = Ring(3), Ring(4), Ring(2), Ring(3), Ring(2)
        er, lbr, hr = Ring(2), Ring(2), Ring(1)
        queue = []

        def emitB(tk):
            cols = tk["cols"]
            nb = (cols + 127) // 128
            tsl, tdeps = ptr.next()
            tp = None
            for a in range(nb):
                ca = min(128, cols - a * 128)
                tp = S_.op("pe", lambda e, a=a, ca=ca, tsl=tsl, ps_=tk["pb"]: e.transpose(
                    out=pT[0:ca, tsl, a * 128:(a + 1) * 128], in_=Pb[:, ps_, a * 128:a * 128 + ca], identity=ident[:, :]),
                    deps=([tk["ready"], t_ld] + tdeps) if a == 0 else [], chan=c_pe if a == nb - 1 else None)
            pbr.release(tk["pb"], tp)
            pts, pdeps = ptsr.next()
            tk["pts"] = pts
            if cols % 128 == 0:
                tcp = S_.op("act", lambda e, tsl=tsl, pts=pts, cols=cols: e.activation(out=PT[:, pts, 0:cols], in_=pT[:, tsl, 0:cols], func=AF.Copy),
                            deps=[tp] + pdeps, chan=c_act)
            else:
                full = (cols // 128) * 128
                rem = cols - full
                if full:
                    S_.op("act", lambda e, tsl=tsl, pts=pts, full=full: e.activation(out=PT[:, pts, 0:full], in_=pT[:, tsl, 0:full], func=AF.Copy),
                          deps=[tp] + pdeps, chan=c_act)
                tcp = S_.op("act", lambda e, tsl=tsl, pts=pts, full=full, rem=rem: e.activation(
                    out=PT[0:rem, pts, full:full + 128], in_=pT[0:rem, tsl, full:full + 128], func=AF.Copy),
                    deps=[tp] + pdeps, chan=c_act)
            ptr.release(tsl, tcp)
            tk["ptready"] = tcp

        def emitC(tk):
            cols = tk["cols"]
            nb = (cols + 127) // 128
            g = tk["grp"]
            if g["os"] is None:
                g["os"], g["odeps"] = por.next()
            os_ = g["os"]
            tpv = None
            for a in range(nb):
                ca = min(128, cols - a * 128)
                first = tk["first"] and a == 0
                last = tk["last"] and a == nb - 1
                vap = tk["v"](a, ca)
                tpv = S_.op("pe", lambda e, a=a, ca=ca, os_=os_, pts=tk["pts"], vap=vap, first=first, last=last: e.matmul(
                    pO[:, os_, 0:128], lhsT=PT[0:ca, pts, a * 128:(a + 1) * 128], rhs=vap, start=first, stop=last),
                    deps=([tk["ptready"]] + (g["odeps"] if first else [])) if a == 0 else [], chan=c_pe if a == nb - 1 else None)
            ptsr.release(tk["pts"], tpv)
            if tk["last"]:
                g["done"](tpv)

        LAGB, LAGC = 2, 4
        state = {"nb": 0, "nc": 0}

        def push(tk):
            queue.append(tk)
            n = len(queue)
            if n - 1 - LAGB >= state["nb"]:
                emitB(queue[state["nb"]]); state["nb"] += 1
            if n - 1 - LAGC >= state["nc"]:
                emitC(queue[state["nc"]]); state["nc"] += 1

        def flush():
            while state["nc"] < len(queue):
                if state["nb"] < len(queue) and state["nb"] <= state["nc"] + (LAGC - LAGB):
                    emitB(queue[state["nb"]]); state["nb"] += 1
                    if state["nb"] - state["nc"] <= (LAGC - LAGB) and state["nb"] < len(queue):
                        continue
                emitC(queue[state["nc"]]); state["nc"] += 1
            queue.clear(); state["nb"] = 0; state["nc"] = 0

        out_state = {}
        selm_free = [[], []]
        ecmp_free = [[], []]
        outt_free = [[], []]
        obuf_free = [[], []]
        for ti, i in enumerate(tiles):
            t0 = 128 * i
            sl = ti % 2
            hs, hdeps = hr.next()
            t_h = S_.op("sp", lambda e, hs=hs, t0=t0: e.dma_start(
                out=hTt[:, hs, :, :], in_=hT_d[:, t0:t0 + 128].rearrange("(k p) t -> p k t", p=128)), deps=hdeps, chan=c_h[hs], inc=16)
            tq = None
            for h in range(4):
                for kc in range(KC):
                    tq = S_.op("pe", lambda e, h=h, kc=kc, hs=hs: e.matmul(
                        pq[:, h * 128:(h + 1) * 128], lhsT=w[:, kc, h * 128:(h + 1) * 128], rhs=hTt[:, hs, kc, :],
                        start=(kc == 0), stop=(kc == KC - 1)),
                        deps=([t_w, t_h] + out_state.get("pq_free", [])) if (h == 0 and kc == 0) else [],
                        chan=c_pe if (h == 3 and kc == KC - 1) else None)
            t_qT = S_.op("act", lambda e, sl=sl: e.activation(out=qT[:, sl, :, :], in_=pq[:, 0:512].rearrange("p (h t) -> p h t", h=4),
                                                              func=AF.Copy, scale=DH ** -0.5),
                         deps=[tq] + out_state.get("qT_free%d" % sl, []), chan=c_act)
            tg = None
            for kc in range(KC):
                tg = S_.op("pe", lambda e, kc=kc, hs=hs: e.matmul(pq[:, 0:12], lhsT=hTt[:, hs, kc, :], rhs=w[:, kc, 512:524],
                                                                   start=(kc == 0), stop=(kc == KC - 1)),
                           deps=[t_qT] if kc == 0 else [], chan=c_pe if kc == KC - 1 else None)
            hr.release(hs, tg)
            t_g = S_.op("act", lambda e, sl=sl: e.activation(out=gate[:, sl, :], in_=pq[:, 0:12], func=AF.Exp, scale=-1.0),
                        deps=[tg] + out_state.get("gate_free%d" % sl, []), chan=c_act)
            out_state["pq_free"] = [t_g]
            t_g = S_.op("dve", lambda e, sl=sl: e.tensor_scalar(out=gate[:, sl, :], in0=gate[:, sl, :], scalar1=1.0, scalar2=None, op0=ALU.add),
                        deps=[t_g], chan=c_dve)
            t_g = S_.op("dve", lambda e, sl=sl: e.reciprocal(out=gate[:, sl, :], in_=gate[:, sl, :]), deps=[t_g], chan=c_dve)

            qT_users = []
            groups_done = []

            def make_group(h, br, rs_cols_fn, cmp_rec=None, last_of_tile=False):
                g = {"os": None, "odeps": None}

                def done(tpv, h=h, br=br, g=g, sl=sl, t0=t0, t_g=t_g, last_of_tile=last_of_tile):
                    os_ = g["os"]
                    if cmp_rec is None:
                        ncol = g["ncols"]
                        t1 = S_.op("dve", lambda e: e.reduce_sum(out=rtot[:, 0:1], in_=rs[:, sl, h, g["c0"]:g["c0"] + ncol],
                                                                 axis=mybir.AxisListType.X), deps=[g["rs_ready"]], chan=c_dve)
                        t1 = S_.op("dve", lambda e: e.tensor_scalar(out=rtot[:, 0:1], in0=rtot[:, 0:1], scalar1=1e-30, scalar2=None, op0=ALU.max),
                                   deps=[t1], chan=c_dve)
                        t1 = S_.op("dve", lambda e: e.reciprocal(out=rtot[:, 1:2], in_=rtot[:, 0:1]), deps=[t1], chan=c_dve)
                        recap = rtot[:, 1:2]
                    else:
                        t1 = cmp_rec[1]
                        recap = cmp_rec[0]
                    t2 = S_.op("dve", lambda e: e.tensor_tensor(out=rtot[:, 2:3], in0=recap, in1=gate[:, sl, h * 3 + br:h * 3 + br + 1], op=ALU.mult),
                               deps=[t1, t_g], chan=c_dve)
                    if br == 0:
                        t3 = S_.op("dve", lambda e: e.tensor_scalar(out=outt[:, sl, h, :], in0=pO[:, os_, 0:128], scalar1=rtot[:, 2:3], scalar2=None,
                                                                    op0=ALU.mult), deps=[t2, tpv] + outt_free[sl], chan=c_dve)
                    else:
                        t3 = S_.op("dve", lambda e: e.scalar_tensor_tensor(out=outt[:, sl, h, :], in0=pO[:, os_, 0:128], scalar=rtot[:, 2:3],
                                                                           in1=outt[:, sl, h, :], op0=ALU.mult, op1=ALU.add),
                                   deps=[t2, tpv], chan=c_dve)
                    por.release(os_, t3)
                    groups_done.append(t3)
                    if last_of_tile:
                        t_ob = S_.op("act", lambda e: e.activation(out=obuf[:, sl, :, :], in_=outt[:, sl, :, :], func=AF.Copy),
                                     deps=[t3] + obuf_free[sl], chan=c_act)
                        outt_free[sl] = [t_ob]
                        out_state["gate_free%d" % sl] = [t3]
                        t_st = S_.op("sp", lambda e: e.dma_start(out=o_d[t0:t0 + 128, :].rearrange("p (h d) -> p h d", h=2), in_=obuf[:, sl, :, :]),
                                     deps=[t_ob], chan=c_st[sl], inc=16)
                        obuf_free[sl] = [t_st]
                g["done"] = done
                return g

            hi = min(NCMP, 8 * i + 8)
            lo = max(0, 8 * i - 120)
            u0 = lo - (8 * i - 120)
            cchunks = [(0, min(512, hi))] + ([(512, hi - 512)] if hi > 512 else [])
            t_imp = None
            for h in range(4):
                ec = 0
                t_e = None
                for (n0, nn) in cchunks:
                    s, sdeps = ring.next()
                    tpe = S_.op("pe", lambda e, s=s, h=h, n0=n0, nn=nn, sl=sl: e.matmul(
                        pring[:, s, 0:nn], lhsT=qT[:, sl, h, :], rhs=kcmpT[:, n0:n0 + nn], start=True, stop=True),
                        deps=[t_qT, t_ld] + sdeps, chan=c_pe)
                    t_e = S_.op("act", lambda e, s=s, ec=ec, h=h, n0=n0, nn=nn: e.activation(
                        out=Ecmp[:, ec, n0:n0 + nn], in_=pring[:, s, 0:nn], func=AF.Exp, bias=t31[:, h:h + 1], scale=1.0),
                        deps=[tpe] + ecmp_free[ec], chan=c_act)
                    ring.release(s, t_e)
                s, sdeps = ring.next()
                nw = hi - lo
                tpe = S_.op("pe", lambda e, s=s, h=h, lo=lo, nw=nw, sl=sl: e.matmul(
                    pring[:, s, 0:nw], lhsT=qT[:, sl, h, :], rhs=kcmpT[:, lo:lo + nw], start=True, stop=True), deps=sdeps, chan=c_pe)
                qT_users.append(tpe)
                ls, ldeps = lbr.next()
                tl = S_.op("dve", lambda e, s=s, ls=ls, h=h, nw=nw, u0=u0: e.tensor_tensor(
                    out=Lb[:, ls, 0:nw], in0=pring[:, s, 0:nw], in1=Bc[:, h, u0:u0 + nw], op=ALU.add), deps=[tpe] + ldeps, chan=c_dve)
                ring.release(s, tl)
                t_e = S_.op("act", lambda e, ls=ls, ec=ec, lo=lo, nw=nw: e.activation(out=Ecmp[:, ec, lo:lo + nw], in_=Lb[:, ls, 0:nw], func=AF.Exp),
                            deps=[tl, t_e], chan=c_act)
                lbr.release(ls, t_e)
                t1 = S_.op("dve", lambda e, ec=ec, h=h, hi=hi, sl=sl: e.reduce_sum(out=rsum[:, sl, h:h + 1], in_=Ecmp[:, ec, 0:hi],
                                                                                  axis=mybir.AxisListType.X), deps=[t_e], chan=c_dve)
                t1 = S_.op("dve", lambda e, h=h, sl=sl: e.tensor_scalar(out=rsum[:, sl, h:h + 1], in0=rsum[:, sl, h:h + 1], scalar1=1e-30, scalar2=None,
                                                                        op0=ALU.max), deps=[t1], chan=c_dve)
                t_rc = S_.op("dve", lambda e, h=h, sl=sl: e.reciprocal(out=rcmp[:, sl, h:h + 1], in_=rsum[:, sl, h:h + 1]), deps=[t1], chan=c_dve)
                if h == 0:
                    t_imp = S_.op("dve", lambda e, ec=ec, hi=hi, sl=sl: e.tensor_scalar(
                        out=imp[:, 1:1 + hi], in0=Ecmp[:, ec, 0:hi], scalar1=rcmp[:, sl, 0:1], scalar2=None, op0=ALU.mult),
                        deps=[t_rc, t_ms] + out_state.get("imp_free", []), chan=c_dve)
                else:
                    t_imp = S_.op("dve", lambda e, ec=ec, hi=hi, h=h, sl=sl: e.scalar_tensor_tensor(
                        out=imp[:, 1:1 + hi], in0=Ecmp[:, ec, 0:hi], scalar=rcmp[:, sl, h:h + 1], in1=imp[:, 1:1 + hi],
                        op0=ALU.mult, op1=ALU.add), deps=[t_rc, t_imp], chan=c_dve)
                ecmp_free[ec] = [t_imp]
                if h < 2:
                    g = make_group(h, 0, None, cmp_rec=(rcmp[:, sl, h:h + 1], t_rc))
                    for ci, (n0, nn) in enumerate(cchunks):
                        pb, pdeps = pbr.next()
                        tcp = S_.op("pool", lambda e, pb=pb, ec=ec, n0=n0, nn=nn: e.tensor_copy(out=Pb[:, pb, 0:nn], in_=Ecmp[:, ec, n0:n0 + nn]),
                                    deps=[t_e] + pdeps, chan=c_pool)
                        ecmp_free[ec] = ecmp_free[ec] + [tcp]
                        push(dict(pb=pb, cols=nn, ready=tcp, grp=g, first=(ci == 0), last=(ci == len(cchunks) - 1),
                                  v=lambda a, ca, n0=n0: vcmp[0:ca, n0 // 128 + a, :]))
            ss = sl
            t = S_.op("dve", lambda e: e.tensor_reduce(out=isel[:, :], in_=imp[:, 0:4 * NSB].rearrange("p (j f) -> p j f", f=4),
                                                       axis=mybir.AxisListType.X, op=ALU.add), deps=[t_imp], chan=c_dve)
            t = S_.op("dve", lambda e: e.tensor_tensor(out=isel[:, :], in0=isel[:, :], in1=imp[:, 4:4 * NSB + 4:4], op=ALU.add), deps=[t], chan=c_dve)
            out_state["imp_free"] = [t]
            x0 = 254 - 2 * i
            t = S_.op("dve", lambda e, x0=x0: e.scalar_tensor_tensor(out=sc[:, :], in0=isel[:, :], scalar=1.0, in1=Vu[:, x0:x0 + NSB],
                                                                     op0=ALU.add, op1=ALU.mult), deps=[t, t_ld], chan=c_dve)
            t = S_.op("dve", lambda e, x0=x0: e.scalar_tensor_tensor(out=sc[:, :], in0=sc[:, :], scalar=-1.0, in1=Fu[:, x0:x0 + NSB],
                                                                     op0=ALU.add, op1=ALU.max), deps=[t], chan=c_dve)
            t = S_.op("dve", lambda e: e.memset(sc[:, 0:1], 1e4), deps=[t], chan=c_dve)
            t = S_.op("dve", lambda e: e.max(out=m8a[:, :], in_=sc[:, :]), deps=[t], chan=c_dve)
            t = S_.op("dve", lambda e: e.match_replace(out=wk[:, :], in_to_replace=m8a[:, :], in_values=sc[:, :], imm_value=-2.0), deps=[t], chan=c_dve)
            t = S_.op("dve", lambda e: e.max(out=m8b[:, :], in_=wk[:, :]), deps=[t], chan=c_dve)
            t = S_.op("dve", lambda e: e.tensor_scalar(out=thr[:, :], in0=m8b[:, 7:8], scalar1=-0.5, scalar2=None, op0=ALU.max), deps=[t], chan=c_dve)
            t_selm = S_.op("dve", lambda e, ss=ss: e.tensor_scalar(out=selm[:, ss, :], in0=sc[:, :], scalar1=thr[:, 0:1], scalar2=None, op0=ALU.is_ge),
                           deps=[t] + selm_free[ss], chan=c_dve)
            nkw = min(640, t0 + 128)
            s0 = t0 + 128 - nkw
            yoff = 640 - nkw
            wchunks = [(0, min(512, nkw))] + ([(512, nkw - 512)] if nkw > 512 else [])
            for h in range(2):
                g = make_group(h, 2, None)
                g["c0"], g["ncols"] = 32, len(wchunks)
                for ci, (off, cols) in enumerate(wchunks):
                    s, sdeps = ring.next()
                    tpe = S_.op("pe", lambda e, s=s, h=h, off=off, cols=cols, s0=s0, sl=sl: e.matmul(
                        pring[:, s, 0:cols], lhsT=qT[:, sl, h, :], rhs=kwinT[:, s0 + off:s0 + off + cols], start=True, stop=True),
                        deps=sdeps, chan=c_pe)
                    qT_users.append(tpe)
                    ls, ldeps = lbr.next()
                    tl = S_.op("dve", lambda e, s=s, ls=ls, h=h, off=off, cols=cols, yoff=yoff: e.tensor_tensor(
                        out=Lb[:, ls, 0:cols], in0=pring[:, s, 0:cols], in1=Bw[:, h, yoff + off:yoff + off + cols], op=ALU.add),
                        deps=[tpe] + ldeps, chan=c_dve)
                    ring.release(s, tl)
                    pb, pdeps = pbr.next()
                    t_p = S_.op("act", lambda e, ls=ls, pb=pb, cols=cols, h=h, ci=ci, sl=sl: e.activation(
                        out=Pb[:, pb, 0:cols], in_=Lb[:, ls, 0:cols], func=AF.Exp, accum_out=rs[:, sl, h, 32 + ci:33 + ci]),
                        deps=[tl] + pdeps, chan=c_act)
                    lbr.release(ls, t_p)
                    g["rs_ready"] = t_p
                    push(dict(pb=pb, cols=cols, ready=t_p, grp=g, first=(ci == 0), last=(ci == len(wchunks) - 1),
                              v=lambda a, ca, s0=s0, off=off: vwin[:, (s0 + off) // 128 + a, :]))
            nk = t0 + 128
            nch = (nk + 511) // 512
            t_m = None
            for h in range(2):
                g = make_group(h, 1, None, last_of_tile=(h == 1))
                g["c0"], g["ncols"] = 0, nch
                for kb in range(nch):
                    cols = min(512, nk - 512 * kb)
                    far = (512 * kb <= t0 - 2048)
                    s, sdeps = ring.next()
                    tpe = S_.op("pe", lambda e, s=s, h=h, kb=kb, cols=cols, sl=sl: e.matmul(
                        pring[:, s, 0:cols], lhsT=qT[:, sl, h, :], rhs=kselT[:, 512 * kb:512 * kb + cols], start=True, stop=True),
                        deps=sdeps, chan=c_pe)
                    qT_users.append(tpe)
                    es_, edeps = er.next()
                    if far:
                        t_e = S_.op("act", lambda e, s=s, es_=es_, h=h, cols=cols: e.activation(
                            out=E[:, es_, 0:cols], in_=pring[:, s, 0:cols], func=AF.Exp, bias=t31[:, h:h + 1], scale=1.0),
                            deps=[tpe] + edeps, chan=c_act)
                        ring.release(s, t_e)
                    else:
                        y2 = 512 * kb - t0 + 1920
                        ls, ldeps = lbr.next()
                        tl = S_.op("dve", lambda e, s=s, ls=ls, h=h, cols=cols, y2=y2: e.tensor_tensor(
                            out=Lb[:, ls, 0:cols], in0=pring[:, s, 0:cols], in1=Bs2[:, h, y2:y2 + cols], op=ALU.add),
                            deps=[tpe] + ldeps, chan=c_dve)
                        ring.release(s, tl)
                        t_e = S_.op("act", lambda e, ls=ls, es_=es_, cols=cols: e.activation(out=E[:, es_, 0:cols], in_=Lb[:, ls, 0:cols], func=AF.Exp),
                                    deps=[tl] + edeps, chan=c_act)
                        lbr.release(ls, t_e)
                    pb, pdeps = pbr.next()
                    nj = cols // 64
                    t_m = S_.op("dve", lambda e, es_=es_, pb=pb, cols=cols, nj=nj, kb=kb, h=h, ss=ss, sl=sl: e.scalar_tensor_tensor(
                        out=Pb[:, pb, 0:cols].rearrange("p (j f) -> p j f", f=64),
                        in0=E[:, es_, 0:cols].rearrange("p (j f) -> p j f", f=64), scalar=1.0,
                        in1=selm[:, ss, 8 * kb:8 * kb + nj].unsqueeze(2).broadcast_to([128, nj, 64]),
                        op0=ALU.mult, op1=ALU.mult, accum_out=rs[:, sl, h, kb:kb + 1]),
                        deps=[t_e, t_selm] + pdeps, chan=c_dve)
                    er.release(es_, t_m)
                    g["rs_ready"] = t_m
                    push(dict(pb=pb, cols=cols, ready=t_m, grp=g, first=(kb == 0), last=(kb == nch - 1),
                              v=lambda a, ca, kb=kb: vsel[:, 4 * kb + a, :]))
            selm_free[ss] = [t_m]
            out_state["qT_free%d" % sl] = [qT_users[-1]]
        flush()
        S_.op("sp", lambda e: e.nop(), deps=[(c, c.count) for c in c_st if c.count > 0])
        S_.emit()
    return nc


NCORES = 8
FOX_CORES = 8


def build_cast(C):
    nc = bass.Bass("TRN2", target_bir_lowering=False)
    x_d = nc.dram_tensor("xT", [D, C], F32, kind="ExternalInput").ap()
    y_d = nc.dram_tensor("yT", [D, C], BF16, kind="ExternalOutput").ap()
    with ExitStack() as es:
        buf = es.enter_context(nc.sbuf_tensor("buf", [128, 2, C], BF16))
        S_ = Sched(nc, es)
        c_ld = [S_.chan("ld0"), S_.chan("ld1")]
        c_st = [S_.chan("st0"), S_.chan("st1")]
        st = [None, None]
        for kc in range(D // 128):
            s = kc % 2
            t = S_.op("pool", lambda e, kc=kc, s=s: e.dma_start(out=buf[:, s, :], in_=x_d[kc * 128:(kc + 1) * 128, :], max_dma_last_dim=4096),
                      deps=[st[s]], chan=c_ld[s], inc=16)
            st[s] = S_.op("sp", lambda e, kc=kc, s=s: e.dma_start(out=y_d[kc * 128:(kc + 1) * 128, :], in_=buf[:, s, :]),
                          deps=[t], chan=c_st[s], inc=16)
        S_.op("sp", lambda e: e.nop(), deps=st)
        S_.emit()
    return nc


def _launch(nc, in_maps):
    import concourse.bass_utils as bu
    res = bu.run_bass_kernel_spmd(nc, in_maps, core_ids=list(range(len(in_maps))))
    return res.results


def _pmaj(v):
    return np.ascontiguousarray(v.reshape(-1, 128, 128).transpose(1, 0, 2))


_PROGS = {}


def _prog(key, fn):
    if key not in _PROGS:
        _PROGS[key] = fn()
    return _PROGS[key]


def forward(inp, S):
    f32 = lambda a: np.ascontiguousarray(np.asarray(a, dtype=np.float32))
    x = f32(inp["x"]).reshape(S, D)
    fox_w_in, fox_b_f, fox_w_o = f32(inp["fox_w_in"]), f32(inp["fox_b_f"]), f32(inp["fox_w_o"])
    nsa_w_in, nsa_w_o, kv_w = f32(inp["nsa_w_in"]), f32(inp["nsa_w_o"]), f32(inp["kv_w"])
    rel_bias = f32(inp["rel_bias"])
    mlp_w1, mlp_w2 = f32(inp["mlp_w1"]), f32(inp["mlp_w2"])
    lng = [f32(inp[k]) for k in ("ln1_g", "ln1_b", "ln2_g", "ln2_b")]
    NPC = min(NCORES, S // TP)
    T = S // NPC
    NT = S // 128
    NCP = S // 16
    ident_bf = np.eye(128, dtype=NPBF)
    cst_fox = fox_consts()

    CC = S // NCORES
    xT = np.ascontiguousarray(x.T)
    nc_cast = _prog(("cast", CC), lambda: build_cast(CC))
    r = _launch(nc_cast, [{"xT": np.ascontiguousarray(xT[:, c * CC:(c + 1) * CC])} for c in range(NCORES)])
    hT = np.ascontiguousarray(np.concatenate([np.asarray(r[c]["yT"]) for c in range(NCORES)], axis=1))
    del xT
    h = x

    nc_post = _prog(("post", T), lambda: build_post(T, 4 * D, True))

    def post(A, h, wo, layer):
        aT = np.ascontiguousarray(A.T)
        lnp = np.ascontiguousarray(np.stack([lng[0][layer], lng[1][layer], lng[2][layer], lng[3][layer]]))
        maps = [{"aT": np.ascontiguousarray(aT[:, c * T:(c + 1) * T]), "h": np.ascontiguousarray(h[c * T:(c + 1) * T]),
                 "wo": wo, "w1": mlp_w1[layer], "w2": mlp_w2[layer], "lnp": lnp, "ident": ident_bf} for c in range(NPC)]
        r = _launch(nc_post, maps)
        h2 = np.concatenate([np.asarray(r[c]["hout"]) for c in range(NPC)], axis=0)
        hT2 = np.ascontiguousarray(np.concatenate([np.asarray(r[c]["hTout"]) for c in range(NPC)], axis=1))
        return h2, hT2

    nc_fox = _prog(("fox", S), lambda: build_fox(S, 2))
    for l in range(2):
        maps = []
        for c in range(NCORES):
            w = np.empty((2, D, 385), np.float32)
            for k in range(2):
                hd = 2 * c + k
                w[k, :, 0:128] = fox_w_in[l][:, hd * 128:(hd + 1) * 128]
                w[k, :, 128:256] = fox_w_in[l][:, 2048 + hd * 128:2048 + (hd + 1) * 128]
                w[k, :, 256:384] = fox_w_in[l][:, 4096 + hd * 128:4096 + (hd + 1) * 128]
                w[k, :, 384] = fox_w_in[l][:, 6144 + hd]
            maps.append({"hT": hT, "w": w, "bf": np.ascontiguousarray(fox_b_f[l][None, 2 * c:2 * c + 2]), "cst": cst_fox})
        r = []
        for p0 in range(0, NCORES, FOX_CORES):
            r += _launch(nc_fox, maps[p0:p0 + FOX_CORES])
        A = np.concatenate([np.asarray(r[c]["o"]) for c in range(NCORES)], axis=1)
        h, hT = post(A, h, fox_w_o[l], l)

    nc_skv = _prog(("skv", S), lambda: build_skv(S))
    maps = []
    for c in range(NCORES):
        sc, g = c // 4, c % 4
        cols = [(sc * 4 + g) * 128]
        for e in (2 * c, 2 * c + 1):
            cols.append(((2 + e // 4) * 4 + e % 4) * 128)
        w = np.ascontiguousarray(np.concatenate([kv_w[:, c0:c0 + 128] for c0 in cols], axis=1))
        maps.append({"hT": hT, "w": w, "w1": f32(inp["cmp_k_w1"] if sc == 0 else inp["cmp_v_w1"]),
                     "w2": f32(inp["cmp_k_w2"] if sc == 0 else inp["cmp_v_w2"]),
                     "posT": np.ascontiguousarray(f32(inp["cmp_pos_k"] if sc == 0 else inp["cmp_pos_v"]).T)})
    r = _launch(nc_skv, maps)
    kcmpT = [np.asarray(r[g]["cmpT"]) for g in range(4)]
    vcmp = [_pmaj(np.ascontiguousarray(np.asarray(r[4 + g]["cmpT"]).T)) if NCP % 128 == 0 else None for g in range(4)]
    raw = {}
    for c in range(NCORES):
        for k, e in enumerate((2 * c, 2 * c + 1)):
            raw[(2 + e // 4, e % 4)] = np.asarray(r[c]["raw"])[k]
    kselT = [np.ascontiguousarray(raw[(2, g)]) for g in range(4)]
    vsel = [_pmaj(np.ascontiguousarray(raw[(3, g)].T)) for g in range(4)]
    kwinT = [np.ascontiguousarray(raw[(4, g)]) for g in range(4)]
    vwin = [_pmaj(np.ascontiguousarray(raw[(5, g)].T)) for g in range(4)]

    nc_nsa = _prog(("nsa", S), lambda: build_nsa(S))
    for b in range(2):
        layer = 2 + b
        maps = []
        for c in range(NCORES):
            g, half = c // 2, c % 2
            ho = [2 * half, 2 * half + 1, 2 * (1 - half), 2 * (1 - half) + 1]
            heads = [4 * g + r_ for r_ in ho]
            w = np.ascontiguousarray(np.concatenate(
                [nsa_w_in[b][:, hd * 128:(hd + 1) * 128] for hd in heads] +
                [nsa_w_in[b][:, 2048 + 3 * hd:2048 + 3 * hd + 3] for hd in heads], axis=1))
            cs = nsa_consts(np.ascontiguousarray(rel_bias[:, heads]))
            maps.append(dict(hT=hT, w=w, kcmpT=kcmpT[g], vcmp=vcmp[g], kselT=kselT[g], vsel=vsel[g], kwinT=kwinT[g], vwin=vwin[g], **cs))
        r = _launch(nc_nsa, maps)
        A = np.concatenate([np.asarray(r[c]["o"]) for c in range(NCORES)], axis=1)
        h, hT = post(A, h, nsa_w_o[b], layer)
    return h.reshape(1, S, D).astype(np.float32)


def kernel(**inputs):
    return forward(inputs, 16384)
```

```python
import math
import numpy as np
import ml_dtypes
from contextlib import ExitStack
import concourse.bass as bass
import concourse.mybir as mybir
from concourse.bass_utils import run_bass_kernel_spmd

F32 = mybir.dt.float32
BF16 = mybir.dt.bfloat16
AF = mybir.ActivationFunctionType
ALU = mybir.AluOpType
NPBF = ml_dtypes.bfloat16


class Chan:
    def __init__(self, sem):
        self.sem = sem
        self.count = 0


class Sched:
    ENGS = ("pe", "act", "dve", "pool", "sp")

    def __init__(self, nc, es):
        self.nc, self.es = nc, es
        self.q = {e: [] for e in self.ENGS}
        self.nsem = 0
        self.main = {}

    def chan(self, name="c"):
        sem = self.es.enter_context(self.nc.semaphore(f"{name}{self.nsem}"))
        self.nsem += 1
        return Chan(sem)

    def op(self, eng, fn, deps=(), chan=None, inc=1):
        waits = [(d[0], d[1]) for d in deps if d is not None]
        mc = self.main.get(eng)
        if mc is not None and mc.count > 0:
            waits.append((mc, mc.count))
        t = None
        if chan is not None:
            chan.count += inc
            t = (chan, chan.count)
        self.q[eng].append((waits, fn, (chan, inc) if chan is not None else None))
        return t

    def emit(self):
        nc = self.nc
        q = self.q
        with nc.Block() as block:
            def replay(name):
                def f(e):
                    seen = {}
                    for waits, fn, inc in q[name]:
                        for ch, val in waits:
                            if seen.get(id(ch), 0) >= val:
                                continue
                            seen[id(ch)] = val
                            e.wait_ge(ch.sem, val)
                        ins = fn(e)
                        if inc is not None:
                            ins.then_inc(inc[0].sem, inc[1])
                return f
            block.tensor(replay("pe"))
            block.scalar(replay("act"))
            block.vector(replay("dve"))
            block.gpsimd(replay("pool"))
            block.sync(replay("sp"))


class Ring:
    def __init__(self, n):
        self.n = n
        self.i = 0
        self.free = [[] for _ in range(n)]

    def next(self):
        s = self.i % self.n
        self.i += 1
        deps = self.free[s]
        self.free[s] = []
        return s, deps

    def release(self, s, ticket):
        self.free[s].append(ticket)


ALPHA = 8.0 ** 0.25
LN_EPS = 1e-5
D = 2048
KC = 16
TP = 1024


def build_post(T, DFF, emit_hT=True):
    assert T % TP == 0 and DFF % 512 == 0
    NP, NTT, NG = T // TP, TP // 128, DFF // 512
    nc = bass.Bass("TRN2", target_bir_lowering=False)
    aT_d = nc.dram_tensor("aT", [D, T], BF16, kind="ExternalInput").ap()
    h_d = nc.dram_tensor("h", [T, D], F32, kind="ExternalInput").ap()
    wo_d = nc.dram_tensor("wo", [D, D], F32, kind="ExternalInput").ap()
    w1_d = nc.dram_tensor("w1", [D, DFF], F32, kind="ExternalInput").ap()
    w2_d = nc.dram_tensor("w2", [DFF, D], F32, kind="ExternalInput").ap()
    lnp_d = nc.dram_tensor("lnp", [4, D], F32, kind="ExternalInput").ap()
    id_d = nc.dram_tensor("ident", [128, 128], BF16, kind="ExternalInput").ap()
    hout_d = nc.dram_tensor("hout", [T, D], F32, kind="ExternalOutput").ap()
    if emit_hT:
        hTout_d = nc.dram_tensor("hTout", [D, T], BF16, kind="ExternalOutput").ap()

    with ExitStack() as es:
        sb = lambda name, shape, dt: es.enter_context(nc.sbuf_tensor(name, shape, dt))
        yacc = sb("yacc", [128, NTT, D], F32)
        h1T = sb("h1T", [128, KC, TP], BF16)
        wA = sb("wA", [128, 2, KC, 512], BF16)
        wBf = sb("wBf", [128, 2 * 4 * D], BF16)
        uT = sb("uT", [128, 2, 4, TP], BF16)
        lnp = sb("lnpsb", [128, 2, D], F32)
        tmp = sb("tmp", [128, 2, 512], F32)
        xb = sb("xb", [128, D], BF16)
        st = sb("st", [128, 4, 6], F32)
        mv = sb("mv", [128, 2], F32)
        rs = sb("rs", [128, 1], F32)
        nmr = sb("nmr", [128, 1], F32)
        ident = sb("identsb", [128, 128], BF16)
        ps = es.enter_context(nc.psum_tensor("ps", [128, 6, 512], F32))
        psT = es.enter_context(nc.psum_tensor("psT", [128, 2, 1024], BF16))
        wB = wBf[:, :].rearrange("p (s c n) -> p s c n", s=2, c=4)
        aTv = wBf[:, :].rearrange("p (k t) -> p k t", k=KC)

        S = Sched(nc, es)
        c_pe, c_act, c_dve, c_pool = S.chan("pe"), S.chan("act"), S.chan("dve"), S.chan("pool")
        S.main = {"act": c_act, "dve": c_dve, "pool": c_pool}
        c_h, c_aT, c_ln, c_st, c_id = S.chan("ldh"), S.chan("ldaT"), S.chan("ldln"), S.chan("st"), S.chan("ldid")
        c_wA = [S.chan("wA0"), S.chan("wA1")]
        c_wB = [S.chan("wB0"), S.chan("wB1")]
        bank = Ring(6)
        tbank = Ring(2)
        wAr, wBr, uTr, tmpr = Ring(2), Ring(2), Ring(2), Ring(2)

        t_id = S.op("sp", lambda e: e.dma_start(out=ident[:, :], in_=id_d[:, :]), chan=c_id, inc=16)

        def layer_norm(tt, deps):
            t = None
            for c in range(4):
                t = S.op("dve", lambda e, c=c: e.bn_stats(out=st[:, c, :], in_=yacc[:, tt, c * 512:(c + 1) * 512]),
                         deps=deps if c == 0 else [t], chan=c_dve)
            t = S.op("dve", lambda e: e.bn_aggr(out=mv[:, :], in_=st[:, :, :]), deps=[t], chan=c_dve)
            t = S.op("act", lambda e: e.activation(out=rs[:, :], in_=mv[:, 1:2], func=AF.Sqrt, bias=LN_EPS, scale=1.0),
                     deps=[t], chan=c_act)
            t = S.op("dve", lambda e: e.reciprocal(out=rs[:, :], in_=rs[:, :]), deps=[t], chan=c_dve)
            t = S.op("dve", lambda e: e.scalar_tensor_tensor(out=nmr[:, :], in0=mv[:, 0:1], scalar=-1.0, in1=rs[:, :],
                                                             op0=ALU.mult, op1=ALU.mult), deps=[t], chan=c_dve)
            t = S.op("act", lambda e: e.activation(out=yacc[:, tt, :], in_=yacc[:, tt, :], func=AF.Identity,
                                                   bias=nmr[:, 0:1], scale=rs[:, 0:1]), deps=[t], chan=c_act)
            t = S.op("dve", lambda e: e.tensor_tensor(out=yacc[:, tt, :], in0=yacc[:, tt, :], in1=lnp[:, 0, :], op=ALU.mult),
                     deps=[t], chan=c_dve)
            t = S.op("pool", lambda e: e.tensor_tensor(out=yacc[:, tt, :], in0=yacc[:, tt, :], in1=lnp[:, 1, :], op=ALU.add),
                     deps=[t], chan=c_pool)
            return t

        def to_T(tt, dep_x, extra_deps):
            t_xb = S.op("act", lambda e: e.activation(out=xb[:, :], in_=yacc[:, tt, :], func=AF.Copy),
                        deps=[dep_x] + list(xb_free), chan=c_act)
            last = None
            tps = []
            for q in range(4):
                s, fdeps = tbank.next()
                tp = None
                for j in range(4):
                    kc = q * 4 + j
                    tp = S.op("pe", lambda e, kc=kc, s=s, j=j: e.transpose(
                        out=psT[:, s, j * 128:(j + 1) * 128], in_=xb[:, kc * 128:(kc + 1) * 128], identity=ident[:, :]),
                        deps=([t_xb, t_id] + fdeps + list(extra_deps)) if j == 0 else [], chan=c_pe if j == 3 else None)
                tps.append(tp)
                last = S.op("dve", lambda e, q=q, s=s: e.tensor_copy(
                    out=h1T[:, q * 4:(q + 1) * 4, tt * 128:(tt + 1) * 128],
                    in_=psT[:, s, 0:512].rearrange("p (k t) -> p k t", k=4)), deps=[tp], chan=c_dve)
                tbank.release(s, last)
            xb_free[:] = [tps[-1]]
            return t_xb, last

        xb_free = []
        prev_pass_pe = None
        prev_pass_st = []
        prev_ln_done = None
        for p in range(NP):
            tok0 = p * TP
            t_h = None
            for tt in range(NTT):
                t_h = S.op("sp", lambda e, tt=tt, tok0=tok0: e.dma_start(out=yacc[:, tt, :], in_=h_d[tok0 + tt * 128: tok0 + (tt + 1) * 128, :]),
                           deps=prev_pass_st if tt == 0 else [], chan=c_h, inc=16)
            t_aT = None
            for q in range(4):
                t_aT = S.op("sp", lambda e, q=q, tok0=tok0: e.dma_start(
                    out=aTv[:, q * 4:(q + 1) * 4, :],
                    in_=aT_d[q * 512:(q + 1) * 512, tok0:tok0 + TP].rearrange("(k p) t -> p k t", p=128)),
                    deps=[prev_pass_pe] if q == 0 else [], chan=c_aT, inc=16)
            t_ln = None
            for i in range(2):
                t_ln = S.op("sp", lambda e, i=i: e.dma_start(out=lnp[:, i, :], in_=lnp_d[i:i + 1, :].broadcast_to([128, D])),
                            deps=[prev_ln_done] if i == 0 else [], chan=c_ln, inc=16)
            last_res = [None] * NTT
            for nb in range(4):
                s, fdeps = wAr.next()
                t_w = S.op("pool", lambda e, s=s, nb=nb: e.dma_start(
                    out=wA[:, s, :, :], in_=wo_d[:, nb * 512:(nb + 1) * 512].rearrange("(k p) n -> p k n", p=128)),
                    deps=fdeps, chan=c_wA[s], inc=16)
                for tt in range(NTT):
                    b, bdeps = bank.next()
                    tpe = None
                    for kc in range(KC):
                        tpe = S.op("pe", lambda e, kc=kc, tt=tt, s=s, b=b: e.matmul(
                            ps[:, b, :], lhsT=aTv[:, kc, tt * 128:(tt + 1) * 128], rhs=wA[:, s, kc, :],
                            start=(kc == 0), stop=(kc == KC - 1)),
                            deps=([t_w, t_aT] + bdeps) if kc == 0 else [], chan=c_pe if kc == KC - 1 else None)
                    te = S.op("dve", lambda e, tt=tt, nb=nb, b=b: e.scalar_tensor_tensor(
                        out=yacc[:, tt, nb * 512:(nb + 1) * 512], in0=yacc[:, tt, nb * 512:(nb + 1) * 512], scalar=ALPHA,
                        in1=ps[:, b, :], op0=ALU.mult, op1=ALU.add), deps=[tpe, t_h], chan=c_dve)
                    bank.release(b, te)
                    last_res[tt] = te
                wAr.release(s, tpe)
            last_wo_pe = tpe
            t_sc = None
            hT_ready = None
            for tt in range(NTT):
                t = layer_norm(tt, [last_res[tt], t_ln])
                t_xb, hT_ready = to_T(tt, t, prev_pass_st if tt == 0 else [])
                t_sc = S.op("act", lambda e, tt=tt: e.activation(out=yacc[:, tt, :], in_=yacc[:, tt, :], func=AF.Copy, scale=ALPHA),
                            deps=[t_xb], chan=c_act)
                ln1_last = t
            t_ln2 = None
            for i in range(2):
                t_ln2 = S.op("sp", lambda e, i=i: e.dma_start(out=lnp[:, i, :], in_=lnp_d[2 + i:3 + i, :].broadcast_to([128, D])),
                             deps=[ln1_last] if i == 0 else [], chan=c_ln, inc=16)
            last_acc = [[t_sc] * 4 for _ in range(NTT)]
            for g in range(NG):
                sa, fdeps = wAr.next()
                t_w1 = S.op("pool", lambda e, sa=sa, g=g: e.dma_start(
                    out=wA[:, sa, :, :], in_=w1_d[:, g * 512:(g + 1) * 512].rearrange("(k p) n -> p k n", p=128)),
                    deps=fdeps, chan=c_wA[sa], inc=16)
                sb_, fdeps = wBr.next()
                fdeps = fdeps + [last_wo_pe]
                t_w2 = None
                for c in range(4):
                    t_w2 = S.op("pool", lambda e, sb_=sb_, g=g, c=c: e.dma_start(
                        out=wB[:, sb_, c, :], in_=w2_d[g * 512 + c * 128: g * 512 + (c + 1) * 128, :], max_dma_last_dim=4096),
                        deps=fdeps if c == 0 else [], chan=c_wB[sb_], inc=16)
                su, udeps = uTr.next()
                t_u = []
                for c in range(4):
                    for blk in range(2):
                        b, bdeps = bank.next()
                        tpe = None
                        for kc in range(KC):
                            tpe = S.op("pe", lambda e, kc=kc, c=c, blk=blk, sa=sa, b=b: e.matmul(
                                ps[:, b, :], lhsT=wA[:, sa, kc, c * 128:(c + 1) * 128], rhs=h1T[:, kc, blk * 512:(blk + 1) * 512],
                                start=(kc == 0), stop=(kc == KC - 1)),
                                deps=([t_w1, hT_ready] + bdeps) if kc == 0 else [], chan=c_pe if kc == KC - 1 else None)
                        ts_, tdeps = tmpr.next()
                        ta = S.op("act", lambda e, ts_=ts_, b=b: e.activation(out=tmp[:, ts_, :], in_=ps[:, b, :], func=AF.Relu),
                                  deps=[tpe] + tdeps, chan=c_act)
                        bank.release(b, ta)
                        td = S.op("dve", lambda e, ts_=ts_, su=su, c=c, blk=blk: e.tensor_tensor(
                            out=uT[:, su, c, blk * 512:(blk + 1) * 512], in0=tmp[:, ts_, :], in1=tmp[:, ts_, :], op=ALU.mult),
                            deps=[ta] + (udeps if (c == 0 and blk == 0) else []), chan=c_dve)
                        tmpr.release(ts_, td)
                        t_u.append(td)
                wAr.release(sa, tpe)
                for tt in range(NTT):
                    for nb in range(4):
                        b, bdeps = bank.next()
                        tpe = None
                        for c in range(4):
                            tpe = S.op("pe", lambda e, c=c, tt=tt, nb=nb, su=su, sb_=sb_, b=b: e.matmul(
                                ps[:, b, :], lhsT=uT[:, su, c, tt * 128:(tt + 1) * 128], rhs=wB[:, sb_, c, nb * 512:(nb + 1) * 512],
                                start=(c == 0), stop=(c == 3)),
                                deps=([t_w2] + t_u + bdeps) if c == 0 else [], chan=c_pe if c == 3 else None)
                        te = S.op("dve", lambda e, tt=tt, nb=nb, b=b: e.tensor_tensor(
                            out=yacc[:, tt, nb * 512:(nb + 1) * 512], in0=ps[:, b, :], in1=yacc[:, tt, nb * 512:(nb + 1) * 512],
                            op=ALU.add), deps=[tpe, last_acc[tt][nb]], chan=c_dve)
                        bank.release(b, te)
                        last_acc[tt][nb] = te
                wBr.release(sb_, tpe)
                uTr.release(su, tpe)
                prev_pass_pe = tpe
            prev_pass_st = []
            hT2 = None
            for tt in range(NTT):
                t = layer_norm(tt, last_acc[tt] + [t_ln2])
                prev_ln_done = t
                t_st = S.op("sp", lambda e, tt=tt, tok0=tok0: e.dma_start(out=hout_d[tok0 + tt * 128: tok0 + (tt + 1) * 128, :], in_=yacc[:, tt, :]),
                            deps=[t], chan=c_st, inc=16)
                prev_pass_st = [t_st]
                if emit_hT:
                    t_xb, hT2 = to_T(tt, t, [prev_pass_pe] if tt == 0 else [])
            if emit_hT:
                for q in range(4):
                    t_st = S.op("sp", lambda e, q=q, tok0=tok0: e.dma_start(
                        out=hTout_d[q * 512:(q + 1) * 512, tok0:tok0 + TP].rearrange("(k p) t -> p k t", p=128),
                        in_=h1T[:, q * 4:(q + 1) * 4, :]), deps=[hT2], chan=c_st, inc=16)
                    prev_pass_st = [t_st]
        S.op("sp", lambda e: e.nop(), deps=prev_pass_st)
        S.emit()
    return nc


D = 2048
KC = 16
DH = 128
NEG = -30000.0


def fox_consts():
    ident = np.eye(128, dtype=np.float32)
    U = np.triu(np.ones((128, 128), np.float32))
    ones = np.ones((128, 128), np.float32)
    s = np.arange(128)[:, None]
    t = np.arange(128)[None, :]
    maskneg = np.where(s <= t, 0.0, NEG).astype(np.float32)
    return np.ascontiguousarray(np.stack([ident, U, ones, maskneg], axis=1))


def build_fox(S, NH=2):
    NT = S // 128
    NB = S // 512
    nc = bass.Bass("TRN2", target_bir_lowering=False)
    hT_d = nc.dram_tensor("hT", [D, S], BF16, kind="ExternalInput").ap()
    w_d = nc.dram_tensor("w", [NH, D, 385], F32, kind="ExternalInput").ap()
    bf_d = nc.dram_tensor("bf", [1, NH], F32, kind="ExternalInput").ap()
    cst_d = nc.dram_tensor("cst", [128, 4, 128], F32, kind="ExternalInput").ap()
    o_d = nc.dram_tensor("o", [S, NH * DH], BF16, kind="ExternalOutput").ap()

    with ExitStack() as es:
        sb = lambda name, shape, dt: es.enter_context(nc.sbuf_tensor(name, shape, dt))
        import os
        if os.environ.get("FOX_PAD"):
            pad_ = sb("padd", [128, int(os.environ["FOX_PAD"])], BF16)
        KT = sb("KT", [128, S], BF16)
        QT = sb("QT", [128, S], BF16)
        V1 = sb("V1", [128, NT, 129], BF16)
        w = sb("wsb", [128, KC, 385], BF16)
        hb = sb("hb", [128, 2, KC, 512], BF16)
        P = sb("P", [128, 4, 512], BF16)
        L = sb("L", [128, 2, 512], F32)
        ncrow = sb("ncrow", [128, 2, 512], F32)
        import os
        CRBF = bool(os.environ.get("FOX_BF16CR"))
        Dg = sb("Dg", [128, 2, 128], BF16 if CRBF else F32)
        onesbf = sb("onesbf", [128, 128], BF16)
        cst = sb("cstsb", [128, 4, 128], F32)
        lf = sb("lf", [128, NT], F32)
        e1 = sb("e1", [128, NT], F32)
        ll = sb("ll", [128, NT], F32)
        Tsb = sb("Tsb", [128, NT], F32)
        incl = sb("incl", [128, NT], F32)
        Ex = sb("Ex", [128, NT], F32)
        Cc = sb("Cc", [128, NT], F32)
        onesrow = sb("onesrow", [128, NT], F32)
        biasb = sb("biasb", [128, 2, NT], F32)
        fcol = sb("fcol", [128, 2, 4], F32)
        bfb = sb("bfb", [128, NH], F32)
        negbf = sb("negbf", [128, NH], F32)
        odsb = sb("odsb", [128, 2, 129], F32)
        osum = sb("osum", [128, 2, 129], F32)
        rec = sb("rec", [128, 2, 1], F32)
        obuf = sb("obuf", [128, 2, 4, 128], BF16)
        pss = es.enter_context(nc.psum_tensor("pss", [128, 3, 512], F32))
        psm = es.enter_context(nc.psum_tensor("psm", [128, 512], F32))
        po = es.enter_context(nc.psum_tensor("po", [128, 4, 512], F32))
        ident, U, ones, maskneg = cst[:, 0, :], cst[:, 1, :], cst[:, 2, :], cst[:, 3, :]

        S_ = Sched(nc, es)
        c_pe, c_act, c_dve, c_pool = S_.chan("pe"), S_.chan("act"), S_.chan("dve"), S_.chan("pool")
        S_.main = {"act": c_act, "dve": c_dve, "pool": c_pool}
        c_cst, c_w = S_.chan("ldc"), S_.chan("ldw")
        c_st = [S_.chan("st0"), S_.chan("st1")]
        c_hb = [S_.chan("hb0"), S_.chan("hb1")]
        engs = {"pe": c_pe, "act": c_act, "dve": c_dve, "pool": c_pool}

        def barrier():
            deps = [(c, c.count) for c in (c_pe, c_act, c_dve, c_pool, c_st[0], c_st[1]) if c.count > 0]
            for eng in ("pe", "act", "dve", "pool", "sp"):
                S_.op(eng, lambda e: e.nop(), deps=deps)

        t_c = S_.op("sp", lambda e: e.dma_start(out=cst[:, :, :], in_=cst_d[:, :, :]), chan=c_cst, inc=16)
        t_c = S_.op("sp", lambda e: e.dma_start(out=bfb[:, :], in_=bf_d[0:1, :].broadcast_to([128, NH])), chan=c_cst, inc=16)
        S_.op("dve", lambda e: e.tensor_scalar(out=negbf[:, :], in0=bfb[:, :], scalar1=-1.0, scalar2=None, op0=ALU.mult),
              deps=[t_c], chan=c_dve)
        S_.op("pool", lambda e: e.memset(V1[:, :, 128:129], 1.0), chan=c_pool)
        S_.op("pool", lambda e: e.memset(onesrow[:, :], 1.0), chan=c_pool)
        S_.op("pool", lambda e: e.memset(onesbf[:, :], 1.0), chan=c_pool)
        t_ob1 = (c_pool, c_pool.count)

        sring = Ring(3)
        hbr = Ring(2)
        for hd in range(NH):
            if hd > 0:
                barrier()
            t_w = S_.op("pool", lambda e, hd=hd: e.dma_start(out=w[:, :, :], in_=w_d[hd].rearrange("(k p) n -> p k n", p=128)),
                        chan=c_w, inc=16)
            for blk in range(NB):
                hs, hdeps = hbr.next()
                t_h = None
                for q in range(4):
                    t_h = S_.op("sp", lambda e, hs=hs, blk=blk, q=q: e.dma_start(
                        out=hb[:, hs, q * 4:(q + 1) * 4, :],
                        in_=hT_d[q * 512:(q + 1) * 512, blk * 512:(blk + 1) * 512].rearrange("(k p) t -> p k t", p=128)),
                        deps=hdeps if q == 0 else [], chan=c_hb[hs], inc=16)
                s, bdeps = sring.next()
                for kc in range(KC):
                    tpe = S_.op("pe", lambda e, kc=kc, s=s, hs=hs: e.matmul(
                        pss[:, s, :], lhsT=w[:, kc, 0:128], rhs=hb[:, hs, kc, :], start=(kc == 0), stop=(kc == KC - 1)),
                        deps=([t_w, t_h] + bdeps) if kc == 0 else [], chan=c_pe if kc == KC - 1 else None)
                te = S_.op("act", lambda e, s=s, blk=blk: e.activation(out=QT[:, blk * 512:(blk + 1) * 512], in_=pss[:, s, :],
                                                                      func=AF.Copy, scale=DH ** -0.5), deps=[tpe], chan=c_act)
                sring.release(s, te)
                s, bdeps = sring.next()
                for kc in range(KC):
                    tpe = S_.op("pe", lambda e, kc=kc, s=s, hs=hs: e.matmul(
                        pss[:, s, :], lhsT=w[:, kc, 128:256], rhs=hb[:, hs, kc, :], start=(kc == 0), stop=(kc == KC - 1)),
                        deps=bdeps if kc == 0 else [], chan=c_pe if kc == KC - 1 else None)
                te = S_.op("dve", lambda e, s=s, blk=blk: e.tensor_copy(out=KT[:, blk * 512:(blk + 1) * 512], in_=pss[:, s, :]),
                           deps=[tpe], chan=c_dve)
                sring.release(s, te)
                for sub in range(4):
                    tile = blk * 4 + sub
                    s, bdeps = sring.next()
                    for kc in range(KC):
                        tpe = S_.op("pe", lambda e, kc=kc, s=s, hs=hs, sub=sub: e.matmul(
                            pss[:, s, 0:129], lhsT=hb[:, hs, kc, sub * 128:(sub + 1) * 128], rhs=w[:, kc, 256:385],
                            start=(kc == 0), stop=(kc == KC - 1)),
                            deps=bdeps if kc == 0 else [], chan=c_pe if kc == KC - 1 else None)
                    ta = S_.op("act", lambda e, s=s, tile=tile: e.activation(out=V1[:, tile, 0:128], in_=pss[:, s, 0:128], func=AF.Copy),
                               deps=[tpe], chan=c_act)
                    td = S_.op("act", lambda e, s=s, tile=tile: e.activation(out=lf[:, tile:tile + 1], in_=pss[:, s, 128:129], func=AF.Copy),
                               deps=[tpe], chan=c_act)
                    sring.release(s, ta)
                    sring.release(s, td)
                hbr.release(hs, tpe)
            t_projA, t_projD = (c_act, c_act.count), (c_dve, c_dve.count)
            t = S_.op("act", lambda e, hd=hd: e.activation(out=e1[:, :], in_=lf[:, :], func=AF.Exp, bias=negbf[:, hd:hd + 1], scale=-1.0),
                      deps=[t_projD, t_projA], chan=c_act)
            t_l = S_.op("act", lambda e: e.activation(out=ll[:, :], in_=e1[:, :], func=AF.Ln, bias=1.0, scale=1.0), deps=[t], chan=c_act)
            s, bdeps = sring.next()
            t_W = S_.op("pe", lambda e: e.matmul(psm[:, 0:NT], lhsT=U, rhs=ll[:, :], start=True, stop=True), deps=[t_l, t_c], chan=c_pe)
            t_T = S_.op("pe", lambda e, s=s: e.matmul(pss[:, s, 0:NT], lhsT=ones, rhs=ll[:, :], start=True, stop=True), deps=bdeps, chan=c_pe)
            t = S_.op("dve", lambda e, s=s: e.tensor_copy(out=Tsb[:, :], in_=pss[:, s, 0:NT]), deps=[t_T], chan=c_dve)
            sring.release(s, t)
            t = S_.op("dve", lambda e: e.tensor_tensor_scan(out=incl[:, :], data0=onesrow[:, :], data1=Tsb[:, :], initial=0.0,
                                                            op0=ALU.mult, op1=ALU.add), deps=[t, (c_pool, c_pool.count)], chan=c_dve)
            t = S_.op("dve", lambda e: e.tensor_tensor(out=Ex[:, :], in0=incl[:, :], in1=Tsb[:, :], op=ALU.subtract), deps=[t], chan=c_dve)
            t_C = S_.op("dve", lambda e: e.tensor_tensor(out=Cc[:, :], in0=psm[:, 0:NT], in1=Ex[:, :], op=ALU.add), deps=[t, t_W], chan=c_dve)
            pring, lring, cring, bring, oring = Ring(4), Ring(2), Ring(2), Ring(2), Ring(2)
            po_free = []
            psm_free = [t_C]
            import os
            for b in range(int(os.environ.get('FOX_NB', NB))):
                nk = 4 * b + 4
                bs, bdeps_ = bring.next()
                t_bias = S_.op("dve", lambda e, bs=bs, b=b, nk=nk: e.tensor_scalar(
                    out=biasb[:, bs, 0:nk], in0=Cc[:, 0:nk], scalar1=Ex[:, 4 * b:4 * b + 1], scalar2=None, op0=ALU.subtract),
                    deps=[t_C] + bdeps_, chan=c_dve)
                t_f = S_.op("act", lambda e, bs=bs, b=b: e.activation(out=fcol[:, bs, :], in_=biasb[:, bs, 4 * b:4 * b + 4],
                                                                      func=AF.Exp, scale=-1.0), deps=[t_bias], chan=c_act)
                cs, cdeps = cring.next()
                t_cr = None
                for qq in range(4):
                    ds_ = qq % 2
                    t_dg = S_.op("dve", lambda e, ds_=ds_, bs=bs, b=b, qq=qq: e.tensor_scalar(
                        out=Dg[:, ds_, :], in0=ident, scalar1=biasb[:, bs, 4 * b + qq:4 * b + qq + 1], scalar2=-1.0,
                        op0=ALU.mult, op1=ALU.mult), deps=[t_bias, t_cr] if t_cr is not None else [t_bias, t_c], chan=c_dve)
                    t_cr = S_.op("pe", lambda e, ds_=ds_, qq=qq: e.matmul(psm[:, qq * 128:(qq + 1) * 128], lhsT=(onesbf[:, :] if CRBF else ones), rhs=Dg[:, ds_, :],
                                                                          start=True, stop=True),
                                 deps=[t_dg, t_ob1] + (psm_free if qq == 0 else []), chan=c_pe)
                t_ncr = S_.op("act", lambda e, cs=cs: e.activation(out=ncrow[:, cs, :], in_=psm[:, :], func=AF.Copy),
                              deps=[t_cr] + cdeps, chan=c_act)
                psm_free = [t_ncr]
                tiles = [("off", j) for j in range(4 * b)] + [("diag", kk) for kk in range(4)]
                pendq = []
                first_pv = True
                last_pv = None
                diag_exp = []

                def emit_pv(info):
                    nonlocal first_pv, last_pv
                    kind, idx, pslot, t_exp = info
                    tpv = None
                    if kind == "off":
                        j = idx
                        for qq in range(4):
                            st_flag = (j == 0)
                            sp_flag = (j == 4 * b - 1)
                            tpv = S_.op("pe", lambda e, qq=qq, pslot=pslot, j=j, st_flag=st_flag, sp_flag=sp_flag: e.matmul(
                                po[:, qq, 0:129], lhsT=P[:, pslot, qq * 128:(qq + 1) * 128], rhs=V1[:, j, :],
                                start=st_flag, stop=sp_flag, skip_group_check=True),
                                deps=([t_exp] + (po_free if first_pv else [])) if qq == 0 else [], chan=c_pe if qq == 3 else None)
                            first_pv = False
                    else:
                        kk = idx
                        for qq in range(kk, 4):
                            st_flag = (b == 0 and kk == 0)
                            tpv = S_.op("pe", lambda e, qq=qq, pslot=pslot, kk=kk, st_flag=st_flag, b=b: e.matmul(
                                po[:, qq, 256:385], lhsT=P[:, pslot, (qq - kk) * 128:(qq - kk + 1) * 128], rhs=V1[:, 4 * b + kk, :],
                                start=st_flag, stop=(kk == qq), skip_group_check=True),
                                deps=([t_exp] + (po_free if first_pv else [])) if qq == kk else [], chan=c_pe if qq == 3 else None)
                            first_pv = False
                    pring.release(pslot, tpv)
                    last_pv = tpv

                for kind, idx in tiles:
                    s, sdeps = sring.next()
                    pslot, pdeps = pring.next()
                    if kind == "off":
                        j = idx
                        t_s = S_.op("pe", lambda e, s=s, j=j, b=b: e.matmul(
                            pss[:, s, :], lhsT=KT[:, j * 128:(j + 1) * 128], rhs=QT[:, b * 512:(b + 1) * 512], start=True, stop=True),
                            deps=sdeps + [t_projA, t_projD], chan=c_pe)
                        t_exp = S_.op("act", lambda e, s=s, pslot=pslot, bs=bs, j=j: e.activation(
                            out=P[:, pslot, :], in_=pss[:, s, :], func=AF.Exp, bias=biasb[:, bs, j:j + 1], scale=1.0),
                            deps=[t_s, t_bias] + pdeps, chan=c_act)
                        sring.release(s, t_exp)
                    else:
                        kk = idx
                        N = (4 - kk) * 128
                        t_s = S_.op("pe", lambda e, s=s, kk=kk, b=b, N=N: e.matmul(
                            pss[:, s, 0:N], lhsT=KT[:, (4 * b + kk) * 128:(4 * b + kk + 1) * 128],
                            rhs=QT[:, b * 512 + kk * 128:(b + 1) * 512], start=True, stop=True),
                            deps=sdeps + [t_projA, t_projD], chan=c_pe)
                        ls, ldeps = lring.next()
                        t_l1 = S_.op("dve", lambda e, s=s, ls=ls, cs=cs, kk=kk, N=N: e.tensor_tensor(
                            out=L[:, ls, 0:N], in0=pss[:, s, 0:N], in1=ncrow[:, cs, kk * 128:512], op=ALU.add),
                            deps=[t_s, t_ncr] + ldeps, chan=c_dve)
                        sring.release(s, t_l1)
                        import os
                        if os.environ.get("FOX_NOPOOL"):
                            t_l2 = S_.op("dve", lambda e, ls=ls: e.tensor_tensor(out=L[:, ls, 0:128], in0=L[:, ls, 0:128], in1=maskneg, op=ALU.add),
                                         deps=[t_l1, t_c], chan=c_dve)
                        else:
                            t_l2 = S_.op("pool", lambda e, ls=ls: e.tensor_tensor(out=L[:, ls, 0:128], in0=L[:, ls, 0:128], in1=maskneg, op=ALU.add),
                                         deps=[t_l1, t_c], chan=c_pool)
                        t_exp = S_.op("act", lambda e, ls=ls, pslot=pslot, bs=bs, kk=kk, b=b, N=N: e.activation(
                            out=P[:, pslot, 0:N], in_=L[:, ls, 0:N], func=AF.Exp, bias=biasb[:, bs, 4 * b + kk:4 * b + kk + 1], scale=1.0),
                            deps=[t_l2] + pdeps, chan=c_act)
                        lring.release(ls, t_exp)
                        diag_exp.append(t_exp)
                    pendq.append((kind, idx, pslot, t_exp))
                    if len(pendq) > 2:
                        emit_pv(pendq.pop(0))
                while pendq:
                    emit_pv(pendq.pop(0))
                cring.release(cs, diag_exp[-1])
                bring.release(bs, diag_exp[-1])
                os_, odeps = oring.next()
                t_o = None
                po_free = []
                for qq in range(4):
                    k2 = qq % 2
                    t1 = S_.op("dve", lambda e, qq=qq, k2=k2: e.tensor_copy(out=odsb[:, k2, :], in_=po[:, qq, 256:385]),
                               deps=[last_pv] + ([t_o] if t_o is not None else []), chan=c_dve)
                    if b > 0:
                        t2 = S_.op("dve", lambda e, qq=qq, k2=k2, bs=bs: e.scalar_tensor_tensor(
                            out=osum[:, k2, :], in0=po[:, qq, 0:129], scalar=fcol[:, bs, qq:qq + 1], in1=odsb[:, k2, :],
                            op0=ALU.mult, op1=ALU.add), deps=[t1, t_f], chan=c_dve)
                    else:
                        t2 = S_.op("dve", lambda e, k2=k2: e.tensor_copy(out=osum[:, k2, :], in_=odsb[:, k2, :]), deps=[t1], chan=c_dve)
                    t3 = S_.op("dve", lambda e, k2=k2: e.reciprocal(out=rec[:, k2, :], in_=osum[:, k2, 128:129]), deps=[t2], chan=c_dve)
                    t_o = S_.op("act", lambda e, qq=qq, k2=k2, os_=os_: e.activation(
                        out=obuf[:, os_, qq, :], in_=osum[:, k2, 0:128], func=AF.Identity, scale=rec[:, k2, 0:1]),
                        deps=[t3] + (odeps if qq == 0 else []), chan=c_act)
                    po_free += [t1, t2]
                t_st = S_.op("sp", lambda e, os_=os_, b=b, hd=hd: e.dma_start(
                    out=o_d[b * 512:(b + 1) * 512, hd * DH:(hd + 1) * DH].rearrange("(q p) d -> p q d", p=128),
                    in_=obuf[:, os_, :, :]), deps=[t_o], chan=c_st[os_], inc=16)
                oring.release(os_, t_st)
        S_.op("sp", lambda e: e.nop(), deps=[(c, c.count) for c in c_st if c.count > 0])
        S_.emit()
    return nc


D = 2048
KC = 16


def build_skv(S):
    NB = S // 512
    NCMP = S // 16 - 1
    nc = bass.Bass("TRN2", target_bir_lowering=False)
    hT_d = nc.dram_tensor("hT", [D, S], BF16, kind="ExternalInput").ap()
    w_d = nc.dram_tensor("w", [D, 384], F32, kind="ExternalInput").ap()
    w1_d = nc.dram_tensor("w1", [4096, 256], F32, kind="ExternalInput").ap()
    w2_d = nc.dram_tensor("w2", [256, 128], F32, kind="ExternalInput").ap()
    posT_d = nc.dram_tensor("posT", [128, 32], F32, kind="ExternalInput").ap()
    cmpT_d = nc.dram_tensor("cmpT", [128, S // 16], BF16, kind="ExternalOutput").ap()
    raw_d = nc.dram_tensor("raw", [2, 128, S], BF16, kind="ExternalOutput").ap()

    with ExitStack() as es:
        sb = lambda name, shape, dt: es.enter_context(nc.sbuf_tensor(name, shape, dt))
        rawT = sb("rawT", [128, 3, S], BF16)
        w = sb("wsb", [128, KC, 384], BF16)
        w1 = sb("w1sb", [128, 32, 256], BF16)
        w2 = sb("w2sb", [128, 2, 128], BF16)
        posT = sb("posTsb", [128, 32], BF16)
        hb = sb("hb", [128, 2, KC, 512], BF16)
        pbias = sb("pbias", [128, 2], F32)
        xx = sb("xx", [128, 512], F32)
        x2 = sb("x2", [128, 512], F32)
        th = sb("th", [128, 512], F32)
        hidT = sb("hidT", [128, 2, 1024], BF16)
        cmpo = sb("cmpo", [128, S // 16], BF16)
        ps = es.enter_context(nc.psum_tensor("ps", [128, 4, 512], F32))
        psb = es.enter_context(nc.psum_tensor("psb", [128, 2], F32))

        S_ = Sched(nc, es)
        c_pe, c_act, c_dve, c_pool = S_.chan("pe"), S_.chan("act"), S_.chan("dve"), S_.chan("pool")
        S_.main = {"act": c_act, "dve": c_dve, "pool": c_pool}
        c_w, c_st = S_.chan("ldw"), S_.chan("st")
        c_hb = [S_.chan("hb0"), S_.chan("hb1")]
        ring = Ring(4)
        hbr = Ring(2)

        S_.op("pool", lambda e: e.dma_start(out=w[:, :, :], in_=w_d.rearrange("(k p) n -> p k n", p=128)), chan=c_w, inc=16)
        S_.op("pool", lambda e: e.dma_start(out=w1[:, :, :], in_=w1_d.rearrange("(j d) h -> d j h", d=128)), chan=c_w, inc=16)
        S_.op("pool", lambda e: e.dma_start(out=w2[:, :, :], in_=w2_d.rearrange("(c p) d -> p c d", p=128)), chan=c_w, inc=16)
        t_w = S_.op("pool", lambda e: e.dma_start(out=posT[:, :], in_=posT_d[:, :]), chan=c_w, inc=16)
        S_.op("pool", lambda e: e.memset(cmpo[:, :], 0.0), chan=c_pool)
        t_ms = (c_pool, c_pool.count)

        t_raw = None
        for blk in range(NB):
            hs, hdeps = hbr.next()
            t_h = None
            for q in range(4):
                t_h = S_.op("sp", lambda e, hs=hs, blk=blk, q=q: e.dma_start(
                    out=hb[:, hs, q * 4:(q + 1) * 4, :],
                    in_=hT_d[q * 512:(q + 1) * 512, blk * 512:(blk + 1) * 512].rearrange("(k p) t -> p k t", p=128)),
                    deps=hdeps if q == 0 else [], chan=c_hb[hs], inc=16)
            for cb in range(3):
                s, bdeps = ring.next()
                for kc in range(KC):
                    tpe = S_.op("pe", lambda e, kc=kc, s=s, hs=hs, cb=cb: e.matmul(
                        ps[:, s, :], lhsT=w[:, kc, cb * 128:(cb + 1) * 128], rhs=hb[:, hs, kc, :], start=(kc == 0), stop=(kc == KC - 1)),
                        deps=([t_w, t_h] + bdeps) if kc == 0 else [], chan=c_pe if kc == KC - 1 else None)
                if cb % 2 == 0:
                    te = S_.op("act", lambda e, s=s, blk=blk, cb=cb: e.activation(out=rawT[:, cb, blk * 512:(blk + 1) * 512], in_=ps[:, s, :], func=AF.Copy),
                               deps=[tpe], chan=c_act)
                else:
                    te = S_.op("dve", lambda e, s=s, blk=blk, cb=cb: e.tensor_copy(out=rawT[:, cb, blk * 512:(blk + 1) * 512], in_=ps[:, s, :]),
                               deps=[tpe], chan=c_dve)
                ring.release(s, te)
            hbr.release(hs, tpe)
        t_rawA, t_rawD = (c_act, c_act.count), (c_dve, c_dve.count)
        for k in range(2):
            S_.op("sp", lambda e, k=k: e.dma_start(out=raw_d[k], in_=rawT[:, 1 + k, :]), deps=[t_rawA, t_rawD], chan=c_st, inc=16)
        for half in range(2):
            for j in range(32):
                tpb = S_.op("pe", lambda e, j=j, half=half: e.matmul(
                    psb[:, half:half + 1], lhsT=w1[:, j, half * 128:(half + 1) * 128], rhs=posT[:, j:j + 1], start=(j == 0), stop=(j == 31)),
                    deps=[t_w] if j == 0 else [], chan=c_pe if j == 31 else None)
        t_pb = S_.op("dve", lambda e: e.tensor_copy(out=pbias[:, :], in_=psb[:, :]), deps=[tpb], chan=c_dve)
        chunks = [(0, 512), (512, NCMP - 512)] if NCMP > 512 else [(0, NCMP)]
        t_hid = None
        for (n0, nn) in chunks:
            for half in range(2):
                s, bdeps = ring.next()
                for j in range(32):
                    tpe = S_.op("pe", lambda e, j=j, half=half, s=s, n0=n0, nn=nn: e.matmul(
                        ps[:, s, 0:nn], lhsT=w1[:, j, half * 128:(half + 1) * 128],
                        rhs=rawT[:, 0, 16 * n0 + j: 16 * n0 + j + 16 * (nn - 1) + 1: 16], start=(j == 0), stop=(j == 31)),
                        deps=([t_rawA, t_rawD] + bdeps) if j == 0 else [], chan=c_pe if j == 31 else None)
                t = S_.op("act", lambda e, s=s, nn=nn, half=half: e.activation(out=xx[:, 0:nn], in_=ps[:, s, 0:nn], func=AF.Identity,
                                                                              bias=pbias[:, half:half + 1], scale=1.0),
                          deps=[tpe, t_pb] + ([t_hid] if t_hid is not None else []), chan=c_act)
                ring.release(s, t)
                t = S_.op("dve", lambda e, nn=nn: e.tensor_tensor(out=x2[:, 0:nn], in0=xx[:, 0:nn], in1=xx[:, 0:nn], op=ALU.mult), deps=[t], chan=c_dve)
                t = S_.op("dve", lambda e, nn=nn: e.tensor_scalar(out=x2[:, 0:nn], in0=x2[:, 0:nn], scalar1=0.044715, scalar2=1.0,
                                                                  op0=ALU.mult, op1=ALU.add), deps=[t], chan=c_dve)
                t = S_.op("dve", lambda e, nn=nn: e.tensor_tensor(out=x2[:, 0:nn], in0=x2[:, 0:nn], in1=xx[:, 0:nn], op=ALU.mult), deps=[t], chan=c_dve)
                t = S_.op("act", lambda e, nn=nn: e.activation(out=th[:, 0:nn], in_=x2[:, 0:nn], func=AF.Tanh, scale=0.7978845608028654),
                          deps=[t], chan=c_act)
                t = S_.op("dve", lambda e, nn=nn: e.scalar_tensor_tensor(out=th[:, 0:nn], in0=th[:, 0:nn], scalar=1.0, in1=xx[:, 0:nn],
                                                                         op0=ALU.add, op1=ALU.mult), deps=[t], chan=c_dve)
                t_hid = S_.op("dve", lambda e, nn=nn, n0=n0, half=half: e.tensor_scalar(
                    out=hidT[:, half, n0:n0 + nn], in0=th[:, 0:nn], scalar1=0.5, scalar2=None, op0=ALU.mult), deps=[t], chan=c_dve)
        t_last = None
        for (n0, nn) in chunks:
            s, bdeps = ring.next()
            for half in range(2):
                tpe = S_.op("pe", lambda e, half=half, s=s, n0=n0, nn=nn: e.matmul(
                    ps[:, s, 0:nn], lhsT=w2[:, half, :], rhs=hidT[:, half, n0:n0 + nn], start=(half == 0), stop=(half == 1)),
                    deps=([t_hid] + bdeps) if half == 0 else [], chan=c_pe if half == 1 else None)
            t_last = S_.op("act", lambda e, s=s, n0=n0, nn=nn: e.activation(out=cmpo[:, n0:n0 + nn], in_=ps[:, s, 0:nn], func=AF.Copy),
                           deps=[tpe, t_ms], chan=c_act)
            ring.release(s, t_last)
        S_.op("sp", lambda e: e.dma_start(out=cmpT_d[:, :], in_=cmpo[:, :]), deps=[t_last], chan=c_st, inc=16)
        S_.op("sp", lambda e: e.nop(), deps=[(c_st, c_st.count)])
        S_.emit()
    return nc


D = 2048
KC = 16
DH = 128
NEG = -30000.0


def rel_bucket_np(d):
    n = np.maximum(d, 0)
    nf = np.maximum(n, 1).astype(np.float32)
    large = 16 + (np.log(nf / np.float32(16)) / np.float32(math.log(2048 / 16)) * np.float32(16)).astype(np.int32)
    large = np.minimum(large, 31)
    return np.where(n < 16, n, large)


def nsa_consts(rel4):
    p = np.arange(128)[:, None]
    u = np.arange(128)[None, :]
    d = p + 1889 - 16 * u
    Bc = np.where(d[:, None, :] >= 0, rel4[rel_bucket_np(d)].transpose(0, 2, 1), NEG).astype(np.float32)
    y2 = np.arange(2048)[None, :]
    d2 = p + 1920 - y2
    Bs2 = np.where(d2[:, None, :] >= 0, rel4[:, 0:2][rel_bucket_np(d2)].transpose(0, 2, 1), NEG).astype(np.float32)
    y = np.arange(640)[None, :]
    dw = p + 512 - y
    Bw = np.where(((dw >= 0) & (dw < 512))[:, None, :], rel4[:, 0:2][rel_bucket_np(dw)].transpose(0, 2, 1), NEG).astype(np.float32)
    x = np.arange(510)[None, :]
    c = x - 254
    hi = (p >= 64).astype(np.int64)
    Vu = (c <= hi).astype(np.float32).astype(NPBF)
    Fu = np.where((c == hi) | (c == hi - 1), 1e4, -5.0).astype(np.float32).astype(NPBF)
    t31 = np.ascontiguousarray(np.broadcast_to(rel4[31][None, :], (128, 4))).astype(np.float32)
    ident = np.eye(128, dtype=NPBF)
    return dict(Bc=np.ascontiguousarray(Bc), Bs2=np.ascontiguousarray(Bs2), Bw=np.ascontiguousarray(Bw),
                Vu=np.ascontiguousarray(Vu), Fu=np.ascontiguousarray(Fu), t31=t31, ident=ident)


def build_nsa(S, tiles=None):
    NT = S // 128
    NCMP = S // 16 - 1
    NCP = S // 16
    NSB = S // 64
    IW = NCP + 4
    tiles = list(range(NT)) if tiles is None else tiles
    nc = bass.Bass("TRN2", target_bir_lowering=False)
    dI = lambda name, shape, dt: nc.dram_tensor(name, shape, dt, kind="ExternalInput").ap()
    hT_d = dI("hT", [D, S], BF16)
    w_d = dI("w", [D, 524], F32)
    kcmpT_d = dI("kcmpT", [128, NCP], BF16)
    vcmp_d = dI("vcmp", [128, NCP // 128, 128], BF16)
    kselT_d = dI("kselT", [128, S], BF16)
    vsel_d = dI("vsel", [128, NT, 128], BF16)
    kwinT_d = dI("kwinT", [128, S], BF16)
    vwin_d = dI("vwin", [128, NT, 128], BF16)
    Bc_d = dI("Bc", [128, 4, 128], F32)
    Bs2_d = dI("Bs2", [128, 2, 2048], F32)
    Bw_d = dI("Bw", [128, 2, 640], F32)
    Vu_d = dI("Vu", [128, 510], BF16)
    Fu_d = dI("Fu", [128, 510], BF16)
    t31_d = dI("t31", [128, 4], F32)
    id_d = dI("ident", [128, 128], BF16)
    o_d = nc.dram_tensor("o", [S, 256], BF16, kind="ExternalOutput").ap()

    with ExitStack() as es:
        sb = lambda name, shape, dt: es.enter_context(nc.sbuf_tensor(name, shape, dt))
        kselT = sb("kselTsb", [128, S], BF16)
        vsel = sb("vselsb", [128, NT, 128], BF16)
        kwinT = sb("kwinTsb", [128, S], BF16)
        vwin = sb("vwinsb", [128, NT, 128], BF16)
        kcmpT = sb("kcmpTsb", [128, NCP], BF16)
        vcmp = sb("vcmpsb", [128, NCP // 128, 128], BF16)
        w = sb("wsb", [128, KC, 524], BF16)
        Bc = sb("Bcsb", [128, 4, 128], F32)
        Bs2 = sb("Bs2sb", [128, 2, 2048], F32)
        Bw = sb("Bwsb", [128, 2, 640], F32)
        Vu = sb("Vusb", [128, 510], BF16)
        Fu = sb("Fusb", [128, 510], BF16)
        t31 = sb("t31sb", [128, 4], F32)
        ident = sb("identsb", [128, 128], BF16)
        hTt = sb("hTt", [128, 1, KC, 128], BF16)
        qT = sb("qT", [128, 2, 4, 128], BF16)
        gate = sb("gate", [128, 2, 12], F32)
        E = sb("E", [128, 2, 512], BF16)
        Lb = sb("Lb", [128, 2, 512], F32)
        Pb = sb("Pb", [128, 4, 512], BF16)
        PT = sb("PT", [128, 3, 512], BF16)
        Ecmp = sb("Ecmp", [128, 1, NCP], F32)
        imp = sb("imp", [128, IW], F32)
        isel = sb("isel", [128, NSB], F32)
        sc = sb("sc", [128, NSB], F32)
        wk = isel
        selm = sb("selm", [128, 2, NSB], BF16)
        m8a = sb("m8a", [128, 8], F32)
        m8b = sb("m8b", [128, 8], F32)
        thr = sb("thr", [128, 1], F32)
        rsum = sb("rsum", [128, 2, 4], F32)
        rcmp = sb("rcmp", [128, 2, 4], F32)
        rs = sb("rs", [128, 2, 2, 40], F32)
        rtot = sb("rtot", [128, 4], F32)
        outt = sb("outt", [128, 2, 2, 128], F32)
        obuf = sb("obuf", [128, 2, 2, 128], BF16)
        pring = es.enter_context(nc.psum_tensor("pring", [128, 3, 512], F32))
        pT = es.enter_context(nc.psum_tensor("pT", [128, 2, 1024], BF16))
        pO = es.enter_context(nc.psum_tensor("pO", [128, 2, 512], F32))
        pq = es.enter_context(nc.psum_tensor("pq", [128, 512], F32))

        S_ = Sched(nc, es)
        c_pe, c_act, c_dve, c_pool = S_.chan("pe"), S_.chan("act"), S_.chan("dve"), S_.chan("pool")
        S_.main = {"act": c_act, "dve": c_dve, "pool": c_pool}
        c_ld, c_w = S_.chan("ld"), S_.chan("ldw")
        c_st = [S_.chan("st0"), S_.chan("st1")]
        c_h = [S_.chan("h0")]

        ld = lambda out, in_: S_.op("sp", lambda e: e.dma_start(out=out, in_=in_), chan=c_ld, inc=16)
        for q4 in range(4):
            ld(kselT[:, q4 * (S // 4):(q4 + 1) * (S // 4)], kselT_d[:, q4 * (S // 4):(q4 + 1) * (S // 4)])
            ld(kwinT[:, q4 * (S // 4):(q4 + 1) * (S // 4)], kwinT_d[:, q4 * (S // 4):(q4 + 1) * (S // 4)])
        ld(vsel[:, :, :], vsel_d[:, :, :])
        ld(vwin[:, :, :], vwin_d[:, :, :])
        ld(kcmpT[:, :], kcmpT_d[:, :])
        ld(vcmp[:, :, :], vcmp_d[:, :, :])
        ld(Bc[:, :, :], Bc_d[:, :, :]); ld(Bs2[:, :, :], Bs2_d[:, :, :]); ld(Bw[:, :, :], Bw_d[:, :, :])
        ld(Vu[:, :], Vu_d[:, :]); ld(Fu[:, :], Fu_d[:, :]); ld(t31[:, :], t31_d[:, :])
        t_ld = ld(ident[:, :], id_d[:, :])
        t_w = S_.op("pool", lambda e: e.dma_start(out=w[:, :, :], in_=w_d.rearrange("(k p) n -> p k n", p=128)), chan=c_w, inc=16)
        S_.op("pool", lambda e: e.memset(imp[:, :], 0.0), chan=c_pool)
        t_ms = (c_pool, c_pool.count)

        ring, pbr, ptr, ptsr, por = Ring(3), Ring(4), Ring(2), Ring(3), Ring(2)
        er, lbr, hr = Ring(2), Ring(2), Ring(1)
        queue = []

        def emitB(tk):
            cols = tk["cols"]
            nb = (cols + 127) // 128
            tsl, tdeps = ptr.next()
            tp = None
            for a in range(nb):
                ca = min(128, cols - a * 128)
                tp = S_.op("pe", lambda e, a=a, ca=ca, tsl=tsl, ps_=tk["pb"]: e.transpose(
                    out=pT[0:ca, tsl, a * 128:(a + 1) * 128], in_=Pb[:, ps_, a * 128:a * 128 + ca], identity=ident[:, :]),
                    deps=([tk["ready"], t_ld] + tdeps) if a == 0 else [], chan=c_pe if a == nb - 1 else None)
            pbr.release(tk["pb"], tp)
            pts, pdeps = ptsr.next()
            tk["pts"] = pts
            if cols % 128 == 0:
                tcp = S_.op("act", lambda e, tsl=tsl, pts=pts, cols=cols: e.activation(out=PT[:, pts, 0:cols], in_=pT[:, tsl, 0:cols], func=AF.Copy),
                            deps=[tp] + pdeps, chan=c_act)
            else:
                full = (cols // 128) * 128
                rem = cols - full
                if full:
                    S_.op("act", lambda e, tsl=tsl, pts=pts, full=full: e.activation(out=PT[:, pts, 0:full], in_=pT[:, tsl, 0:full], func=AF.Copy),
                          deps=[tp] + pdeps, chan=c_act)
                tcp = S_.op("act", lambda e, tsl=tsl, pts=pts, full=full, rem=rem: e.activation(
                    out=PT[0:rem, pts, full:full + 128], in_=pT[0:rem, tsl, full:full + 128], func=AF.Copy),
                    deps=[tp] + pdeps, chan=c_act)
            ptr.release(tsl, tcp)
            tk["ptready"] = tcp

        def emitC(tk):
            cols = tk["cols"]
            nb = (cols + 127) // 128
            g = tk["grp"]
            if g["os"] is None:
                g["os"], g["odeps"] = por.next()
            os_ = g["os"]
            tpv = None
            for a in range(nb):
                ca = min(128, cols - a * 128)
                first = tk["first"] and a == 0
                last = tk["last"] and a == nb - 1
                vap = tk["v"](a, ca)
                tpv = S_.op("pe", lambda e, a=a, ca=ca, os_=os_, pts=tk["pts"], vap=vap, first=first, last=last: e.matmul(
                    pO[:, os_, 0:128], lhsT=PT[0:ca, pts, a * 128:(a + 1) * 128], rhs=vap, start=first, stop=last),
                    deps=([tk["ptready"]] + (g["odeps"] if first else [])) if a == 0 else [], chan=c_pe if a == nb - 1 else None)
            ptsr.release(tk["pts"], tpv)
            if tk["last"]:
                g["done"](tpv)

        LAGB, LAGC = 2, 4
        state = {"nb": 0, "nc": 0}

        def push(tk):
            queue.append(tk)
            n = len(queue)
            if n - 1 - LAGB >= state["nb"]:
                emitB(queue[state["nb"]]); state["nb"] += 1
            if n - 1 - LAGC >= state["nc"]:
                emitC(queue[state["nc"]]); state["nc"] += 1

        def flush():
            while state["nc"] < len(queue):
                if state["nb"] < len(queue) and state["nb"] <= state["nc"] + (LAGC - LAGB):
                    emitB(queue[state["nb"]]); state["nb"] += 1
                    if state["nb"] - state["nc"] <= (LAGC - LAGB) and state["nb"] < len(queue):
                        continue
                emitC(queue[state["nc"]]); state["nc"] += 1
            queue.clear(); state["nb"] = 0; state["nc"] = 0

        out_state = {}
        selm_free = [[], []]
        ecmp_free = [[], []]
        outt_free = [[], []]
        obuf_free = [[], []]
        for ti, i in enumerate(tiles):
            t0 = 128 * i
            sl = ti % 2
            hs, hdeps = hr.next()
            t_h = S_.op("sp", lambda e, hs=hs, t0=t0: e.dma_start(
                out=hTt[:, hs, :, :], in_=hT_d[:, t0:t0 + 128].rearrange("(k p) t -> p k t", p=128)), deps=hdeps, chan=c_h[hs], inc=16)
            tq = None
            for h in range(4):
                for kc in range(KC):
                    tq = S_.op("pe", lambda e, h=h, kc=kc, hs=hs: e.matmul(
                        pq[:, h * 128:(h + 1) * 128], lhsT=w[:, kc, h * 128:(h + 1) * 128], rhs=hTt[:, hs, kc, :],
                        start=(kc == 0), stop=(kc == KC - 1)),
                        deps=([t_w, t_h] + out_state.get("pq_free", [])) if (h == 0 and kc == 0) else [],
                        chan=c_pe if (h == 3 and kc == KC - 1) else None)
            t_qT = S_.op("act", lambda e, sl=sl: e.activation(out=qT[:, sl, :, :], in_=pq[:, 0:512].rearrange("p (h t) -> p h t", h=4),
                                                              func=AF.Copy, scale=DH ** -0.5),
                         deps=[tq] + out_state.get("qT_free%d" % sl, []), chan=c_act)
            tg = None
            for kc in range(KC):
                tg = S_.op("pe", lambda e, kc=kc, hs=hs: e.matmul(pq[:, 0:12], lhsT=hTt[:, hs, kc, :], rhs=w[:, kc, 512:524],
                                                                   start=(kc == 0), stop=(kc == KC - 1)),
                           deps=[t_qT] if kc == 0 else [], chan=c_pe if kc == KC - 1 else None)
            hr.release(hs, tg)
            t_g = S_.op("act", lambda e, sl=sl: e.activation(out=gate[:, sl, :], in_=pq[:, 0:12], func=AF.Exp, scale=-1.0),
                        deps=[tg] + out_state.get("gate_free%d" % sl, []), chan=c_act)
            out_state["pq_free"] = [t_g]
            t_g = S_.op("dve", lambda e, sl=sl: e.tensor_scalar(out=gate[:, sl, :], in0=gate[:, sl, :], scalar1=1.0, scalar2=None, op0=ALU.add),
                        deps=[t_g], chan=c_dve)
            t_g = S_.op("dve", lambda e, sl=sl: e.reciprocal(out=gate[:, sl, :], in_=gate[:, sl, :]), deps=[t_g], chan=c_dve)

            qT_users = []
            groups_done = []

            def make_group(h, br, rs_cols_fn, cmp_rec=None, last_of_tile=False):
                g = {"os": None, "odeps": None}

                def done(tpv, h=h, br=br, g=g, sl=sl, t0=t0, t_g=t_g, last_of_tile=last_of_tile):
                    os_ = g["os"]
                    if cmp_rec is None:
                        ncol = g["ncols"]
                        t1 = S_.op("dve", lambda e: e.reduce_sum(out=rtot[:, 0:1], in_=rs[:, sl, h, g["c0"]:g["c0"] + ncol],
                                                                 axis=mybir.AxisListType.X), deps=[g["rs_ready"]], chan=c_dve)
                        t1 = S_.op("dve", lambda e: e.tensor_scalar(out=rtot[:, 0:1], in0=rtot[:, 0:1], scalar1=1e-30, scalar2=None, op0=ALU.max),
                                   deps=[t1], chan=c_dve)
                        t1 = S_.op("dve", lambda e: e.reciprocal(out=rtot[:, 1:2], in_=rtot[:, 0:1]), deps=[t1], chan=c_dve)
                        recap = rtot[:, 1:2]
                    else:
                        t1 = cmp_rec[1]
                        recap = cmp_rec[0]
                    t2 = S_.op("dve", lambda e: e.tensor_tensor(out=rtot[:, 2:3], in0=recap, in1=gate[:, sl, h * 3 + br:h * 3 + br + 1], op=ALU.mult),
                               deps=[t1, t_g], chan=c_dve)
                    if br == 0:
                        t3 = S_.op("dve", lambda e: e.tensor_scalar(out=outt[:, sl, h, :], in0=pO[:, os_, 0:128], scalar1=rtot[:, 2:3], scalar2=None,
                                                                    op0=ALU.mult), deps=[t2, tpv] + outt_free[sl], chan=c_dve)
                    else:
                        t3 = S_.op("dve", lambda e: e.scalar_tensor_tensor(out=outt[:, sl, h, :], in0=pO[:, os_, 0:128], scalar=rtot[:, 2:3],
                                                                           in1=outt[:, sl, h, :], op0=ALU.mult, op1=ALU.add),
                                   deps=[t2, tpv], chan=c_dve)
                    por.release(os_, t3)
                    groups_done.append(t3)
                    if last_of_tile:
                        t_ob = S_.op("act", lambda e: e.activation(out=obuf[:, sl, :, :], in_=outt[:, sl, :, :], func=AF.Copy),
                                     deps=[t3] + obuf_free[sl], chan=c_act)
                        outt_free[sl] = [t_ob]
                        out_state["gate_free%d" % sl] = [t3]
                        t_st = S_.op("sp", lambda e: e.dma_start(out=o_d[t0:t0 + 128, :].rearrange("p (h d) -> p h d", h=2), in_=obuf[:, sl, :, :]),
                                     deps=[t_ob], chan=c_st[sl], inc=16)
                        obuf_free[sl] = [t_st]
                g["done"] = done
                return g

            hi = min(NCMP, 8 * i + 8)
            lo = max(0, 8 * i - 120)
            u0 = lo - (8 * i - 120)
            cchunks = [(0, min(512, hi))] + ([(512, hi - 512)] if hi > 512 else [])
            t_imp = None
            for h in range(4):
                ec = 0
                t_e = None
                for (n0, nn) in cchunks:
                    s, sdeps = ring.next()
                    tpe = S_.op("pe", lambda e, s=s, h=h, n0=n0, nn=nn, sl=sl: e.matmul(
                        pring[:, s, 0:nn], lhsT=qT[:, sl, h, :], rhs=kcmpT[:, n0:n0 + nn], start=True, stop=True),
                        deps=[t_qT, t_ld] + sdeps, chan=c_pe)
                    t_e = S_.op("act", lambda e, s=s, ec=ec, h=h, n0=n0, nn=nn: e.activation(
                        out=Ecmp[:, ec, n0:n0 + nn], in_=pring[:, s, 0:nn], func=AF.Exp, bias=t31[:, h:h + 1], scale=1.0),
                        deps=[tpe] + ecmp_free[ec], chan=c_act)
                    ring.release(s, t_e)
                s, sdeps = ring.next()
                nw = hi - lo
                tpe = S_.op("pe", lambda e, s=s, h=h, lo=lo, nw=nw, sl=sl: e.matmul(
                    pring[:, s, 0:nw], lhsT=qT[:, sl, h, :], rhs=kcmpT[:, lo:lo + nw], start=True, stop=True), deps=sdeps, chan=c_pe)
                qT_users.append(tpe)
                ls, ldeps = lbr.next()
                tl = S_.op("dve", lambda e, s=s, ls=ls, h=h, nw=nw, u0=u0: e.tensor_tensor(
                    out=Lb[:, ls, 0:nw], in0=pring[:, s, 0:nw], in1=Bc[:, h, u0:u0 + nw], op=ALU.add), deps=[tpe] + ldeps, chan=c_dve)
                ring.release(s, tl)
                t_e = S_.op("act", lambda e, ls=ls, ec=ec, lo=lo, nw=nw: e.activation(out=Ecmp[:, ec, lo:lo + nw], in_=Lb[:, ls, 0:nw], func=AF.Exp),
                            deps=[tl, t_e], chan=c_act)
                lbr.release(ls, t_e)
                t1 = S_.op("dve", lambda e, ec=ec, h=h, hi=hi, sl=sl: e.reduce_sum(out=rsum[:, sl, h:h + 1], in_=Ecmp[:, ec, 0:hi],
                                                                                  axis=mybir.AxisListType.X), deps=[t_e], chan=c_dve)
                t1 = S_.op("dve", lambda e, h=h, sl=sl: e.tensor_scalar(out=rsum[:, sl, h:h + 1], in0=rsum[:, sl, h:h + 1], scalar1=1e-30, scalar2=None,
                                                                        op0=ALU.max), deps=[t1], chan=c_dve)
                t_rc = S_.op("dve", lambda e, h=h, sl=sl: e.reciprocal(out=rcmp[:, sl, h:h + 1], in_=rsum[:, sl, h:h + 1]), deps=[t1], chan=c_dve)
                if h == 0:
                    t_imp = S_.op("dve", lambda e, ec=ec, hi=hi, sl=sl: e.tensor_scalar(
                        out=imp[:, 1:1 + hi], in0=Ecmp[:, ec, 0:hi], scalar1=rcmp[:, sl, 0:1], scalar2=None, op0=ALU.mult),
                        deps=[t_rc, t_ms] + out_state.get("imp_free", []), chan=c_dve)
                else:
                    t_imp = S_.op("dve", lambda e, ec=ec, hi=hi, h=h, sl=sl: e.scalar_tensor_tensor(
                        out=imp[:, 1:1 + hi], in0=Ecmp[:, ec, 0:hi], scalar=rcmp[:, sl, h:h + 1], in1=imp[:, 1:1 + hi],
                        op0=ALU.mult, op1=ALU.add), deps=[t_rc, t_imp], chan=c_dve)
                ecmp_free[ec] = [t_imp]
                if h < 2:
                    g = make_group(h, 0, None, cmp_rec=(rcmp[:, sl, h:h + 1], t_rc))
                    for ci, (n0, nn) in enumerate(cchunks):
                        pb, pdeps = pbr.next()
                        tcp = S_.op("pool", lambda e, pb=pb, ec=ec, n0=n0, nn=nn: e.tensor_copy(out=Pb[:, pb, 0:nn], in_=Ecmp[:, ec, n0:n0 + nn]),
                                    deps=[t_e] + pdeps, chan=c_pool)
                        ecmp_free[ec] = ecmp_free[ec] + [tcp]
                        push(dict(pb=pb, cols=nn, ready=tcp, grp=g, first=(ci == 0), last=(ci == len(cchunks) - 1),
                                  v=lambda a, ca, n0=n0: vcmp[0:ca, n0 // 128 + a, :]))
            ss = sl
            t = S_.op("dve", lambda e: e.tensor_reduce(out=isel[:, :], in_=imp[:, 0:4 * NSB].rearrange("p (j f) -> p j f", f=4),
                                                       axis=mybir.AxisListType.X, op=ALU.add), deps=[t_imp], chan=c_dve)
            t = S_.op("dve", lambda e: e.tensor_tensor(out=isel[:, :], in0=isel[:, :], in1=imp[:, 4:4 * NSB + 4:4], op=ALU.add), deps=[t], chan=c_dve)
            out_state["imp_free"] = [t]
            x0 = 254 - 2 * i
            t = S_.op("dve", lambda e, x0=x0: e.scalar_tensor_tensor(out=sc[:, :], in0=isel[:, :], scalar=1.0, in1=Vu[:, x0:x0 + NSB],
                                                                     op0=ALU.add, op1=ALU.mult), deps=[t, t_ld], chan=c_dve)
            t = S_.op("dve", lambda e, x0=x0: e.scalar_tensor_tensor(out=sc[:, :], in0=sc[:, :], scalar=-1.0, in1=Fu[:, x0:x0 + NSB],
                                                                     op0=ALU.add, op1=ALU.max), deps=[t], chan=c_dve)
            t = S_.op("dve", lambda e: e.memset(sc[:, 0:1], 1e4), deps=[t], chan=c_dve)
            t = S_.op("dve", lambda e: e.max(out=m8a[:, :], in_=sc[:, :]), deps=[t], chan=c_dve)
            t = S_.op("dve", lambda e: e.match_replace(out=wk[:, :], in_to_replace=m8a[:, :], in_values=sc[:, :], imm_value=-2.0), deps=[t], chan=c_dve)
            t = S_.op("dve", lambda e: e.max(out=m8b[:, :], in_=wk[:, :]), deps=[t], chan=c_dve)
            t = S_.op("dve", lambda e: e.tensor_scalar(out=thr[:, :], in0=m8b[:, 7:8], scalar1=-0.5, scalar2=None, op0=ALU.max), deps=[t], chan=c_dve)
            t_selm = S_.op("dve", lambda e, ss=ss: e.tensor_scalar(out=selm[:, ss, :], in0=sc[:, :], scalar1=thr[:, 0:1], scalar2=None, op0=ALU.is_ge),
                           deps=[t] + selm_free[ss], chan=c_dve)
            nkw = min(640, t0 + 128)
            s0 = t0 + 128 - nkw
            yoff = 640 - nkw
            wchunks = [(0, min(512, nkw))] + ([(512, nkw - 512)] if nkw > 512 else [])
            for h in range(2):
                g = make_group(h, 2, None)
                g["c0"], g["ncols"] = 32, len(wchunks)
                for ci, (off, cols) in enumerate(wchunks):
                    s, sdeps = ring.next()
                    tpe = S_.op("pe", lambda e, s=s, h=h, off=off, cols=cols, s0=s0, sl=sl: e.matmul(
                        pring[:, s, 0:cols], lhsT=qT[:, sl, h, :], rhs=kwinT[:, s0 + off:s0 + off + cols], start=True, stop=True),
                        deps=sdeps, chan=c_pe)
                    qT_users.append(tpe)
                    ls, ldeps = lbr.next()
                    tl = S_.op("dve", lambda e, s=s, ls=ls, h=h, off=off, cols=cols, yoff=yoff: e.tensor_tensor(
                        out=Lb[:, ls, 0:cols], in0=pring[:, s, 0:cols], in1=Bw[:, h, yoff + off:yoff + off + cols], op=ALU.add),
                        deps=[tpe] + ldeps, chan=c_dve)
                    ring.release(s, tl)
                    pb, pdeps = pbr.next()
                    t_p = S_.op("act", lambda e, ls=ls, pb=pb, cols=cols, h=h, ci=ci, sl=sl: e.activation(
                        out=Pb[:, pb, 0:cols], in_=Lb[:, ls, 0:cols], func=AF.Exp, accum_out=rs[:, sl, h, 32 + ci:33 + ci]),
                        deps=[tl] + pdeps, chan=c_act)
                    lbr.release(ls, t_p)
                    g["rs_ready"] = t_p
                    push(dict(pb=pb, cols=cols, ready=t_p, grp=g, first=(ci == 0), last=(ci == len(wchunks) - 1),
                              v=lambda a, ca, s0=s0, off=off: vwin[:, (s0 + off) // 128 + a, :]))
            nk = t0 + 128
            nch = (nk + 511) // 512
            t_m = None
            for h in range(2):
                g = make_group(h, 1, None, last_of_tile=(h == 1))
                g["c0"], g["ncols"] = 0, nch
                for kb in range(nch):
                    cols = min(512, nk - 512 * kb)
                    far = (512 * kb <= t0 - 2048)
                    s, sdeps = ring.next()
                    tpe = S_.op("pe", lambda e, s=s, h=h, kb=kb, cols=cols, sl=sl: e.matmul(
                        pring[:, s, 0:cols], lhsT=qT[:, sl, h, :], rhs=kselT[:, 512 * kb:512 * kb + cols], start=True, stop=True),
                        deps=sdeps, chan=c_pe)
                    qT_users.append(tpe)
                    es_, edeps = er.next()
                    if far:
                        t_e = S_.op("act", lambda e, s=s, es_=es_, h=h, cols=cols: e.activation(
                            out=E[:, es_, 0:cols], in_=pring[:, s, 0:cols], func=AF.Exp, bias=t31[:, h:h + 1], scale=1.0),
                            deps=[tpe] + edeps, chan=c_act)
                        ring.release(s, t_e)
                    else:
                        y2 = 512 * kb - t0 + 1920
                        ls, ldeps = lbr.next()
                        tl = S_.op("dve", lambda e, s=s, ls=ls, h=h, cols=cols, y2=y2: e.tensor_tensor(
                            out=Lb[:, ls, 0:cols], in0=pring[:, s, 0:cols], in1=Bs2[:, h, y2:y2 + cols], op=ALU.add),
                            deps=[tpe] + ldeps, chan=c_dve)
                        ring.release(s, tl)
                        t_e = S_.op("act", lambda e, ls=ls, es_=es_, cols=cols: e.activation(out=E[:, es_, 0:cols], in_=Lb[:, ls, 0:cols], func=AF.Exp),
                                    deps=[tl] + edeps, chan=c_act)
                        lbr.release(ls, t_e)
                    pb, pdeps = pbr.next()
                    nj = cols // 64
                    t_m = S_.op("dve", lambda e, es_=es_, pb=pb, cols=cols, nj=nj, kb=kb, h=h, ss=ss, sl=sl: e.scalar_tensor_tensor(
                        out=Pb[:, pb, 0:cols].rearrange("p (j f) -> p j f", f=64),
                        in0=E[:, es_, 0:cols].rearrange("p (j f) -> p j f", f=64), scalar=1.0,
                        in1=selm[:, ss, 8 * kb:8 * kb + nj].unsqueeze(2).broadcast_to([128, nj, 64]),
                        op0=ALU.mult, op1=ALU.mult, accum_out=rs[:, sl, h, kb:kb + 1]),
                        deps=[t_e, t_selm] + pdeps, chan=c_dve)
                    er.release(es_, t_m)
                    g["rs_ready"] = t_m
                    push(dict(pb=pb, cols=cols, ready=t_m, grp=g, first=(kb == 0), last=(kb == nch - 1),
                              v=lambda a, ca, kb=kb: vsel[:, 4 * kb + a, :]))
            selm_free[ss] = [t_m]
            out_state["qT_free%d" % sl] = [qT_users[-1]]
        flush()
        S_.op("sp", lambda e: e.nop(), deps=[(c, c.count) for c in c_st if c.count > 0])
        S_.emit()
    return nc


NCORES = 8
FOX_CORES = 8


def build_cast(C):
    nc = bass.Bass("TRN2", target_bir_lowering=False)
    x_d = nc.dram_tensor("xT", [D, C], F32, kind="ExternalInput").ap()
    y_d = nc.dram_tensor("yT", [D, C], BF16, kind="ExternalOutput").ap()
    with ExitStack() as es:
        buf = es.enter_context(nc.sbuf_tensor("buf", [128, 2, C], BF16))
        S_ = Sched(nc, es)
        c_ld = [S_.chan("ld0"), S_.chan("ld1")]
        c_st = [S_.chan("st0"), S_.chan("st1")]
        st = [None, None]
        for kc in range(D // 128):
            s = kc % 2
            t = S_.op("pool", lambda e, kc=kc, s=s: e.dma_start(out=buf[:, s, :], in_=x_d[kc * 128:(kc + 1) * 128, :], max_dma_last_dim=4096),
                      deps=[st[s]], chan=c_ld[s], inc=16)
            st[s] = S_.op("sp", lambda e, kc=kc, s=s: e.dma_start(out=y_d[kc * 128:(kc + 1) * 128, :], in_=buf[:, s, :]),
                          deps=[t], chan=c_st[s], inc=16)
        S_.op("sp", lambda e: e.nop(), deps=st)
        S_.emit()
    return nc


def _launch(nc, in_maps):
    import concourse.bass_utils as bu
    res = bu.run_bass_kernel_spmd(nc, in_maps, core_ids=list(range(len(in_maps))))
    return res.results


def _pmaj(v):
    return np.ascontiguousarray(v.reshape(-1, 128, 128).transpose(1, 0, 2))


_PROGS = {}


def _prog(key, fn):
    if key not in _PROGS:
        _PROGS[key] = fn()
    return _PROGS[key]


def forward(inp, S):
    f32 = lambda a: np.ascontiguousarray(np.asarray(a, dtype=np.float32))
    x = f32(inp["x"]).reshape(S, D)
    fox_w_in, fox_b_f, fox_w_o = f32(inp["fox_w_in"]), f32(inp["fox_b_f"]), f32(inp["fox_w_o"])
    nsa_w_in, nsa_w_o, kv_w = f32(inp["nsa_w_in"]), f32(inp["nsa_w_o"]), f32(inp["kv_w"])
    rel_bias = f32(inp["rel_bias"])
    mlp_w1, mlp_w2 = f32(inp["mlp_w1"]), f32(inp["mlp_w2"])
    lng = [f32(inp[k]) for k in ("ln1_g", "ln1_b", "ln2_g", "ln2_b")]
    NPC = min(NCORES, S // TP)
    T = S // NPC
    NT = S // 128
    NCP = S // 16
    ident_bf = np.eye(128, dtype=NPBF)
    cst_fox = fox_consts()

    CC = S // NCORES
    xT = np.ascontiguousarray(x.T)
    nc_cast = _prog(("cast", CC), lambda: build_cast(CC))
    r = _launch(nc_cast, [{"xT": np.ascontiguousarray(xT[:, c * CC:(c + 1) * CC])} for c in range(NCORES)])
    hT = np.ascontiguousarray(np.concatenate([np.asarray(r[c]["yT"]) for c in range(NCORES)], axis=1))
    del xT
    h = x

    nc_post = _prog(("post", T), lambda: build_post(T, 4 * D, True))

    def post(A, h, wo, layer):
        aT = np.ascontiguousarray(A.T)
        lnp = np.ascontiguousarray(np.stack([lng[0][layer], lng[1][layer], lng[2][layer], lng[3][layer]]))
        maps = [{"aT": np.ascontiguousarray(aT[:, c * T:(c + 1) * T]), "h": np.ascontiguousarray(h[c * T:(c + 1) * T]),
                 "wo": wo, "w1": mlp_w1[layer], "w2": mlp_w2[layer], "lnp": lnp, "ident": ident_bf} for c in range(NPC)]
        r = _launch(nc_post, maps)
        h2 = np.concatenate([np.asarray(r[c]["hout"]) for c in range(NPC)], axis=0)
        hT2 = np.ascontiguousarray(np.concatenate([np.asarray(r[c]["hTout"]) for c in range(NPC)], axis=1))
        return h2, hT2

    nc_fox = _prog(("fox", S), lambda: build_fox(S, 2))
    for l in range(2):
        maps = []
        for c in range(NCORES):
            w = np.empty((2, D, 385), np.float32)
            for k in range(2):
                hd = 2 * c + k
                w[k, :, 0:128] = fox_w_in[l][:, hd * 128:(hd + 1) * 128]
                w[k, :, 128:256] = fox_w_in[l][:, 2048 + hd * 128:2048 + (hd + 1) * 128]
                w[k, :, 256:384] = fox_w_in[l][:, 4096 + hd * 128:4096 + (hd + 1) * 128]
                w[k, :, 384] = fox_w_in[l][:, 6144 + hd]
            maps.append({"hT": hT, "w": w, "bf": np.ascontiguousarray(fox_b_f[l][None, 2 * c:2 * c + 2]), "cst": cst_fox})
        r = []
        for p0 in range(0, NCORES, FOX_CORES):
            r += _launch(nc_fox, maps[p0:p0 + FOX_CORES])
        A = np.concatenate([np.asarray(r[c]["o"]) for c in range(NCORES)], axis=1)
        h, hT = post(A, h, fox_w_o[l], l)

    nc_skv = _prog(("skv", S), lambda: build_skv(S))
    maps = []
    for c in range(NCORES):
        sc, g = c // 4, c % 4
        cols = [(sc * 4 + g) * 128]
        for e in (2 * c, 2 * c + 1):
            cols.append(((2 + e // 4) * 4 + e % 4) * 128)
        w = np.ascontiguousarray(np.concatenate([kv_w[:, c0:c0 + 128] for c0 in cols], axis=1))
        maps.append({"hT": hT, "w": w, "w1": f32(inp["cmp_k_w1"] if sc == 0 else inp["cmp_v_w1"]),
                     "w2": f32(inp["cmp_k_w2"] if sc == 0 else inp["cmp_v_w2"]),
                     "posT": np.ascontiguousarray(f32(inp["cmp_pos_k"] if sc == 0 else inp["cmp_pos_v"]).T)})
    r = _launch(nc_skv, maps)
    kcmpT = [np.asarray(r[g]["cmpT"]) for g in range(4)]
    vcmp = [_pmaj(np.ascontiguousarray(np.asarray(r[4 + g]["cmpT"]).T)) if NCP % 128 == 0 else None for g in range(4)]
    raw = {}
    for c in range(NCORES):
        for k, e in enumerate((2 * c, 2 * c + 1)):
            raw[(2 + e // 4, e % 4)] = np.asarray(r[c]["raw"])[k]
    kselT = [np.ascontiguousarray(raw[(2, g)]) for g in range(4)]
    vsel = [_pmaj(np.ascontiguousarray(raw[(3, g)].T)) for g in range(4)]
    kwinT = [np.ascontiguousarray(raw[(4, g)]) for g in range(4)]
    vwin = [_pmaj(np.ascontiguousarray(raw[(5, g)].T)) for g in range(4)]

    nc_nsa = _prog(("nsa", S), lambda: build_nsa(S))
    for b in range(2):
        layer = 2 + b
        maps = []
        for c in range(NCORES):
            g, half = c // 2, c % 2
            ho = [2 * half, 2 * half + 1, 2 * (1 - half), 2 * (1 - half) + 1]
            heads = [4 * g + r_ for r_ in ho]
            w = np.ascontiguousarray(np.concatenate(
                [nsa_w_in[b][:, hd * 128:(hd + 1) * 128] for hd in heads] +
                [nsa_w_in[b][:, 2048 + 3 * hd:2048 + 3 * hd + 3] for hd in heads], axis=1))
            cs = nsa_consts(np.ascontiguousarray(rel_bias[:, heads]))
            maps.append(dict(hT=hT, w=w, kcmpT=kcmpT[g], vcmp=vcmp[g], kselT=kselT[g], vsel=vsel[g], kwinT=kwinT[g], vwin=vwin[g], **cs))
        r = _launch(nc_nsa, maps)
        A = np.concatenate([np.asarray(r[c]["o"]) for c in range(NCORES)], axis=1)
        h, hT = post(A, h, nsa_w_o[b], layer)
    return h.reshape(1, S, D).astype(np.float32)


def kernel(**inputs):
    return forward(inputs, 16384)
```
